# Optimizing a Trainium2 kernel written in Bass

```python
import jax
import jax.numpy as jnp
from jax import lax
import numpy as np

D_MODEL = 1024
BATCH = 2
SEQ = 8192
DEPTH = 2

GRID_W = 64
CTX_LEN = 256
N_MOD = 9
D_FF = ((8 * D_MODEL // 3 + 127) // 128) * 128
EPS = 1e-6

N_GROUPS = 4
GROUP_W = D_MODEL // N_GROUPS
MIX_W = N_GROUPS * GROUP_W

CONV_K = 31
LN_EPS = 1e-5
LRU_BLOCKS = 4
LRU_CONV_K = 4
LRU_C = 8.0
RWKV_HEAD = 64
RWKV_HEADS = GROUP_W // RWKV_HEAD
DECAY_LORA = 64
AAA_LORA = 64
GATE_LORA = 128
GN_EPS = 64e-5
QK_NOPE = 64
QK_ROPE = 32
V_HEAD = 64
MLA_HEADS = GROUP_W // V_HEAD
Q_LORA = 256
KV_LORA = 128
ROPE_BASE = 10000.0
Q_BLOCK = 128
SM_SCALE = (QK_NOPE + QK_ROPE) ** -0.5

A_COLS = 2 * GROUP_W
B_COLS = 2 * GROUP_W
C_COLS = 3 * GROUP_W + DECAY_LORA + AAA_LORA + GATE_LORA
D_COLS = Q_LORA + KV_LORA + QK_ROPE
IN_COLS = A_COLS + B_COLS + C_COLS + D_COLS
IN_SPLITS = (A_COLS, A_COLS + B_COLS, A_COLS + B_COLS + C_COLS)
RWKV_SPLITS = (GROUP_W, 2 * GROUP_W, 3 * GROUP_W, 3 * GROUP_W + DECAY_LORA,
               3 * GROUP_W + DECAY_LORA + AAA_LORA)

kernel_name = 'hybrid_parallel_group_dit_block'


def rms_norm(x, gain=None):
    xf = x.astype(jnp.float32)
    y = xf * lax.rsqrt(jnp.mean(xf * xf, axis=-1, keepdims=True) + EPS)
    if gain is not None:
        y = y * gain.astype(jnp.float32)
    return y.astype(x.dtype)


def layer_norm(x, gain, bias):
    xf = x.astype(jnp.float32)
    mu = jnp.mean(xf, axis=-1, keepdims=True)
    var = jnp.mean(jnp.square(xf - mu), axis=-1, keepdims=True)
    y = (xf - mu) * lax.rsqrt(var + LN_EPS)
    return (y * gain.astype(jnp.float32) + bias.astype(jnp.float32)).astype(x.dtype)


def group_norm_heads(y, gain, bias):
    yf = y.astype(jnp.float32)
    mu = jnp.mean(yf, axis=-1, keepdims=True)
    var = jnp.mean(jnp.square(yf - mu), axis=-1, keepdims=True)
    yn = (yf - mu) * lax.rsqrt(var + GN_EPS)
    g = gain.astype(jnp.float32).reshape(RWKV_HEADS, RWKV_HEAD)
    b = bias.astype(jnp.float32).reshape(RWKV_HEADS, RWKV_HEAD)
    return (yn * g + b).astype(y.dtype)


def l2_normalize(t):
    tf = t.astype(jnp.float32)
    return (tf * lax.rsqrt(jnp.maximum(jnp.sum(tf * tf, axis=-1, keepdims=True), 1e-24))).astype(t.dtype)


def modulate(x, shift, scale):
    return rms_norm(x) * (1 + scale) + shift


def swiglu(h, w13, w2):
    gate, up = jnp.split(h @ w13, 2, axis=-1)
    return (jax.nn.silu(gate) * up) @ w2


def depthwise_conv(x, w, b, pad_left, pad_right):
    y = lax.conv_general_dilated(
        x, w[:, None, :].astype(x.dtype), window_strides=(1,),
        padding=((pad_left, pad_right),), dimension_numbers=('NWC', 'WIO', 'NWC'),
        feature_group_count=x.shape[-1])
    return y + b


def token_shift(u, mu_prev, mu_next):
    zero = jnp.zeros_like(u[:, :1])
    prev = jnp.concatenate([zero, u[:, :-1]], axis=1)
    nxt = jnp.concatenate([u[:, 1:], zero], axis=1)
    return u + mu_prev * (prev - u) + mu_next * (nxt - u)


def linear_scan(a, b, h0, reverse):
    def combine(e1, e2):
        a1, b1 = e1
        a2, b2 = e2
        return a1 * a2, a2 * b1 + b2
    a_cum, h = lax.associative_scan(combine, (a, b), reverse=reverse, axis=1)
    return h + a_cum * h0[:, None, :]


def wkv7_scan(r, decay, k, v, kk, b, s0, reverse):
    def step(s, inp):
        r_t, w_t, k_t, v_t, kk_t, b_t = inp
        sa = jnp.einsum('bhvk,bhk->bhv', s, kk_t)
        s = s * w_t[:, :, None, :] - sa[..., None] * b_t[:, :, None, :] + v_t[..., None] * k_t[:, :, None, :]
        return s, jnp.einsum('bhvk,bhk->bhv', s, r_t)
    xs = tuple(jnp.moveaxis(t, 1, 0) for t in (r, decay, k, v, kk, b))
    s_final, ys = lax.scan(step, s0, xs, reverse=reverse)
    return jnp.moveaxis(ys, 0, 1), s_final


def axial_rope_tables(t_len, dtype):
    t = jnp.arange(t_len, dtype=jnp.int32)
    rows = (t // GRID_W).astype(jnp.float32)
    cols = (t % GRID_W).astype(jnp.float32)
    n_freq = QK_ROPE // 4
    inv_freq = ROPE_BASE ** (-jnp.arange(n_freq, dtype=jnp.float32) / n_freq)
    ang = jnp.stack([rows[:, None] * inv_freq, cols[:, None] * inv_freq], axis=1)
    ang = jnp.concatenate([ang, ang], axis=-1).reshape(t_len, QK_ROPE)
    return jnp.cos(ang).astype(dtype), jnp.sin(ang).astype(dtype)


def apply_rope(x, cos, sin):
    xs = x.reshape(x.shape[:-1] + (2, 2, QK_ROPE // 4))
    rot = jnp.stack([-xs[..., 1, :], xs[..., 0, :]], axis=-2).reshape(x.shape)
    return x * cos + rot * sin


def conformer_conv(u, dw_w, dw_b, ln_g, ln_b):
    val, gate = jnp.split(u, 2, axis=-1)
    z = val * jax.nn.sigmoid(gate)
    z = depthwise_conv(z, dw_w, dw_b, CONV_K // 2, CONV_K // 2)
    return jax.nn.silu(layer_norm(z, ln_g, ln_b))


def rglru_mixer(u, uc, need_ctx_out, conv_w, conv_b, w_a, b_a, w_x, b_x, lam):
    pad_l = LRU_CONV_K // 2
    pad_r = LRU_CONV_K - 1 - pad_l

    def prepare(v):
        xb, gb = jnp.split(v, 2, axis=-1)
        return depthwise_conv(xb, conv_w, conv_b, pad_l, pad_r), gb

    def gates(xv, d):
        xh = xv.reshape(xv.shape[:-1] + (LRU_BLOCKS, GROUP_W // LRU_BLOCKS))
        r = jax.nn.sigmoid(jnp.einsum('bthi,hij->bthj', xh, w_a[d]).reshape(xv.shape) + b_a[d])
        i = jax.nn.sigmoid(jnp.einsum('bthi,hij->bthj', xh, w_x[d]).reshape(xv.shape) + b_x[d])
        log_a = -LRU_C * r * jax.nn.softplus(-lam[d])
        return jnp.exp(log_a), jnp.sqrt(-jnp.expm1(2 * log_a)) * (i * xv)

    xl, gl = prepare(u)
    xc, gc = prepare(uc)
    outs_l, outs_c = [], []
    for d, rev in ((0, False), (1, True)):
        a_c, b_c = gates(xc, d)
        h_c = linear_scan(a_c, b_c, jnp.zeros_like(xc[:, 0]), rev)
        h_final = h_c[:, 0] if rev else h_c[:, -1]
        a_l, b_l = gates(xl, d)
        outs_l.append(linear_scan(a_l, b_l, h_final, rev))
        outs_c.append(h_c)
    y_l = (outs_l[0] + outs_l[1]) * jax.nn.gelu(gl)
    y_c = (outs_c[0] + outs_c[1]) * jax.nn.gelu(gc) if need_ctx_out else None
    return y_l, y_c


def rwkv7_mixer(u, uc, need_ctx_out, mu_prev, mu_next, w0, w_up, a0, a_up, g_up,
                k_k, k_a, r_k, gn_g, gn_b):
    def heads(t):
        return t.reshape(t.shape[:-1] + (RWKV_HEADS, RWKV_HEAD))

    def prepare(v):
        v = token_shift(v, mu_prev, mu_next)
        r, k, val, wd, ad, gd = jnp.split(v, RWKV_SPLITS, axis=-1)
        return heads(r), k, heads(val), jnp.tanh(wd), ad, gd, l2_normalize(heads(k * k_k))

    def run(p, d, reverse, s0):
        r, k, v, wd, ad, _, kk = p
        decay = jnp.exp(-jnp.exp(-jax.nn.softplus(-(w0[d] + wd @ w_up[d])) - 0.5))
        a = jax.nn.sigmoid(a0[d] + ad @ a_up[d])
        k_d = heads(k * (1 + (a - 1) * k_a))
        y, s = wkv7_scan(r, heads(decay), k_d, v, kk, kk * heads(a), s0, reverse)
        bonus = jnp.sum(r * k_d * r_k, axis=-1, keepdims=True) * v
        return y, bonus, s

    def readout(p, ys, bonuses):
        g = jax.nn.sigmoid(p[5]) @ g_up
        o = group_norm_heads(ys[0] + ys[1], gn_g, gn_b) + bonuses[0] + bonuses[1]
        return o.reshape(o.shape[:-2] + (GROUP_W,)) * g

    p_l = prepare(u)
    p_c = prepare(uc)
    s0 = jnp.zeros((u.shape[0], RWKV_HEADS, RWKV_HEAD, RWKV_HEAD), u.dtype)
    ys_l, bs_l, ys_c, bs_c = [], [], [], []
    for d, rev in ((0, False), (1, True)):
        y_c, b_c, s_c = run(p_c, d, rev, s0)
        y_l, b_l, _ = run(p_l, d, rev, s_c)
        ys_l.append(y_l)
        bs_l.append(b_l)
        ys_c.append(y_c)
        bs_c.append(b_c)
    out_l = readout(p_l, ys_l, bs_l)
    out_c = readout(p_c, ys_c, bs_c) if need_ctx_out else None
    return out_l, out_c


def mla_mixer(u, uc, need_ctx_out, q_norm, w_uq, kv_norm, w_ukv, cos, sin):
    def queries(v):
        cq = v[..., :Q_LORA]
        return (rms_norm(cq, q_norm) @ w_uq).reshape(v.shape[:2] + (MLA_HEADS, QK_NOPE + QK_ROPE))

    def keys_values(v):
        ckv = v[..., Q_LORA:Q_LORA + KV_LORA]
        k_rope = v[..., Q_LORA + KV_LORA:]
        kv = (rms_norm(ckv, kv_norm) @ w_ukv).reshape(v.shape[:2] + (MLA_HEADS, QK_NOPE + V_HEAD))
        return kv[..., :QK_NOPE], kv[..., QK_NOPE:], k_rope

    def assemble_k(k_nope, k_rope):
        k_rope = jnp.broadcast_to(k_rope[:, :, None, :], k_nope.shape[:-1] + (QK_ROPE,))
        return jnp.concatenate([k_nope, k_rope], axis=-1)

    def attend(q, k, v):
        s = jnp.einsum('bqhd,bkhd->bhqk', q, k, preferred_element_type=jnp.float32) * SM_SCALE
        p = jax.nn.softmax(s, axis=-1).astype(v.dtype)
        return jnp.einsum('bhqk,bkhd->bqhd', p, v)

    q = queries(u)
    q = jnp.concatenate([q[..., :QK_NOPE],
                         apply_rope(q[..., QK_NOPE:], cos[:, None, :], sin[:, None, :])], axis=-1)
    k_nope, v, k_rope = keys_values(u)
    k = assemble_k(k_nope, apply_rope(k_rope, cos, sin))
    k_nope_c, v_c, k_rope_c = keys_values(uc)
    k_c = assemble_k(k_nope_c, k_rope_c)
    k_all = jnp.concatenate([k, k_c], axis=1)
    v_all = jnp.concatenate([v, v_c], axis=1)
    bsz, t_len = q.shape[0], q.shape[1]
    q_blocks = jnp.moveaxis(q.reshape(bsz, t_len // Q_BLOCK, Q_BLOCK, MLA_HEADS, QK_NOPE + QK_ROPE), 1, 0)
    o = lax.map(lambda qb: attend(qb, k_all, v_all), q_blocks)
    o = jnp.moveaxis(o, 0, 1).reshape(bsz, t_len, MLA_HEADS * V_HEAD)
    o_c = None
    if need_ctx_out:
        o_c = attend(queries(uc), k_c, v_c).reshape(bsz, uc.shape[1], MLA_HEADS * V_HEAD)
    return o, o_c


def setup_inputs(seed: int = 0) -> dict:
    key = jax.random.key(seed)
    keys = list(jax.random.split(key, 48))

    def nrm(shape, scale):
        return jax.random.normal(keys.pop(), shape, jnp.float32) * scale

    def unif(shape, lo, hi):
        return jax.random.uniform(keys.pop(), shape, jnp.float32, lo, hi)

    L, D, G = DEPTH, D_MODEL, GROUP_W
    a8 = unif((L, 2, G), 0.9, 0.999)
    s = a8 ** (1.0 / LRU_C)
    lru_lambda = jnp.log(s) - jnp.log1p(-s)
    return {
        'x': nrm((BATCH, SEQ, D), 1.0),
        'c': nrm((BATCH, D), 1.0),
        'ctx': nrm((BATCH, CTX_LEN, D), 1.0),
        'c_ctx': nrm((D,), 1.0),
        'ada_w': nrm((L, D, N_MOD * D), 0.5 * D ** -0.5),
        'ada_b': nrm((L, N_MOD * D), 0.02),
        'ffn1_w13': nrm((L, D, 2 * D_FF), D ** -0.5),
        'ffn1_w2': nrm((L, D_FF, D), D_FF ** -0.5),
        'ffn2_w13': nrm((L, D, 2 * D_FF), D ** -0.5),
        'ffn2_w2': nrm((L, D_FF, D), D_FF ** -0.5),
        'w_in': nrm((L, D, IN_COLS), D ** -0.5),
        'w_out': nrm((L, MIX_W, D), MIX_W ** -0.5),
        'cv_dw_w': nrm((L, CONV_K, G), CONV_K ** -0.5),
        'cv_dw_b': nrm((L, G), 0.02),
        'cv_ln_g': 1.0 + nrm((L, G), 0.02),
        'cv_ln_b': nrm((L, G), 0.02),
        'lru_conv_w': nrm((L, LRU_CONV_K, G), LRU_CONV_K ** -0.5),
        'lru_conv_b': nrm((L, G), 0.02),
        'lru_wa': nrm((L, 2, LRU_BLOCKS, G // LRU_BLOCKS, G // LRU_BLOCKS), (G // LRU_BLOCKS) ** -0.5),
        'lru_ba': nrm((L, 2, G), 0.1),
        'lru_wx': nrm((L, 2, LRU_BLOCKS, G // LRU_BLOCKS, G // LRU_BLOCKS), (G // LRU_BLOCKS) ** -0.5),
        'lru_bx': nrm((L, 2, G), 0.1),
        'lru_lambda': lru_lambda,
        'rwkv_mu_prev': unif((L, C_COLS), 0.0, 0.5),
        'rwkv_mu_next': unif((L, C_COLS), 0.0, 0.5),
        'rwkv_w0': unif((L, 2, G), -6.0, 0.0),
        'rwkv_w_up': nrm((L, 2, DECAY_LORA, G), 0.1),
        'rwkv_a0': nrm((L, 2, G), 0.1),
        'rwkv_a_up': nrm((L, 2, AAA_LORA, G), 0.5 * AAA_LORA ** -0.5),
        'rwkv_g_up': nrm((L, GATE_LORA, G), GATE_LORA ** -0.5),
        'rwkv_k_k': 0.85 + nrm((L, G), 0.02),
        'rwkv_k_a': 1.0 + nrm((L, G), 0.02),
        'rwkv_r_k': nrm((L, RWKV_HEADS, RWKV_HEAD), 0.1),
        'rwkv_gn_g': 1.0 + nrm((L, G), 0.02),
        'rwkv_gn_b': nrm((L, G), 0.02),
        'mla_q_norm': 1.0 + nrm((L, Q_LORA), 0.02),
        'mla_w_uq': nrm((L, Q_LORA, MLA_HEADS * (QK_NOPE + QK_ROPE)), Q_LORA ** -0.5),
        'mla_kv_norm': 1.0 + nrm((L, KV_LORA), 0.02),
        'mla_w_ukv': nrm((L, KV_LORA, MLA_HEADS * (QK_NOPE + V_HEAD)), KV_LORA ** -0.5),
        'final_norm': 1.0 + nrm((D,), 0.02),
    }


def reference(x, c, ctx, c_ctx, ada_w, ada_b, ffn1_w13, ffn1_w2, ffn2_w13, ffn2_w2, w_in, w_out,
              cv_dw_w, cv_dw_b, cv_ln_g, cv_ln_b,
              lru_conv_w, lru_conv_b, lru_wa, lru_ba, lru_wx, lru_bx, lru_lambda,
              rwkv_mu_prev, rwkv_mu_next, rwkv_w0, rwkv_w_up, rwkv_a0, rwkv_a_up, rwkv_g_up,
              rwkv_k_k, rwkv_k_a, rwkv_r_k, rwkv_gn_g, rwkv_gn_b,
              mla_q_norm, mla_w_uq, mla_kv_norm, mla_w_ukv, final_norm):
    bsz, t_len, d_model = x.shape
    cos, sin = axial_rope_tables(t_len, x.dtype)
    silu_c = jax.nn.silu(c)
    silu_cc = jax.nn.silu(c_ctx)
    xc = ctx
    for l in range(DEPTH):
        need_ctx_out = l < DEPTH - 1
        mod = (silu_c @ ada_w[l] + ada_b[l]).reshape(bsz, N_MOD, 1, d_model)
        sh1, s1, g1, sh2, s2, g2, sh3, s3, g3 = [mod[:, i] for i in range(N_MOD)]
        mod_c = (silu_cc @ ada_w[l] + ada_b[l]).reshape(N_MOD, d_model)
        sh1c, s1c, g1c, sh2c, s2c, g2c, sh3c, s3c, g3c = [mod_c[i] for i in range(N_MOD)]

        x = x + 0.5 * g1 * swiglu(modulate(x, sh1, s1), ffn1_w13[l], ffn1_w2[l])
        xc = xc + 0.5 * g1c * swiglu(modulate(xc, sh1c, s1c), ffn1_w13[l], ffn1_w2[l])

        u = modulate(x, sh2, s2) @ w_in[l]
        uc = modulate(xc, sh2c, s2c) @ w_in[l]
        u_a, u_b, u_c, u_d = jnp.split(u, IN_SPLITS, axis=-1)
        uc_a, uc_b, uc_c, uc_d = jnp.split(uc, IN_SPLITS, axis=-1)

        y_a = conformer_conv(u_a, cv_dw_w[l], cv_dw_b[l], cv_ln_g[l], cv_ln_b[l])
        y_b, yc_b = rglru_mixer(u_b, uc_b, need_ctx_out, lru_conv_w[l], lru_conv_b[l],
                                lru_wa[l], lru_ba[l], lru_wx[l], lru_bx[l], lru_lambda[l])
        y_c, yc_c = rwkv7_mixer(u_c, uc_c, need_ctx_out, rwkv_mu_prev[l], rwkv_mu_next[l],
                                rwkv_w0[l], rwkv_w_up[l], rwkv_a0[l], rwkv_a_up[l], rwkv_g_up[l],
                                rwkv_k_k[l], rwkv_k_a[l], rwkv_r_k[l], rwkv_gn_g[l], rwkv_gn_b[l])
        y_d, yc_d = mla_mixer(u_d, uc_d, need_ctx_out, mla_q_norm[l], mla_w_uq[l],
                              mla_kv_norm[l], mla_w_ukv[l], cos, sin)
        y = jnp.concatenate([y_a, y_b, y_c, y_d], axis=-1) @ w_out[l]
        x = x + g2 * y

        if need_ctx_out:
            yc_a = conformer_conv(uc_a, cv_dw_w[l], cv_dw_b[l], cv_ln_g[l], cv_ln_b[l])
            yc = jnp.concatenate([yc_a, yc_b, yc_c, yc_d], axis=-1) @ w_out[l]
            xc = xc + g2c * yc
            xc = xc + 0.5 * g3c * swiglu(modulate(xc, sh3c, s3c), ffn2_w13[l], ffn2_w2[l])

        x = x + 0.5 * g3 * swiglu(modulate(x, sh3, s3), ffn2_w13[l], ffn2_w2[l])
    return rms_norm(x, final_norm)
```

```python
import numpy as np
import concourse.bass as bass
import concourse.mybir as mybir
from concourse.bass_utils import run_bass_kernel_spmd

F32 = mybir.dt.float32
BF16 = mybir.dt.bfloat16
AF = mybir.ActivationFunctionType
ALU = mybir.AluOpType
AX = mybir.AxisListType

ENGS = ("pe", "act", "dve", "pool", "sp")
DMA_K = 6


class Buf:
    __slots__ = ("name", "t", "last_w", "readers", "parent")

    def __init__(self, name, t=None, parent=None):
        self.name = name
        self.t = t
        self.parent = parent
        self.last_w = None
        self.readers = []

    def __getitem__(self, idx):
        return self.t[idx]

    def root(self):
        b = self
        while b.parent is not None:
            b = b.parent
        return b

    def view(self, name, t):
        return Buf(name, t, parent=self)


class Op:
    __slots__ = ("eng", "idx", "fn", "waits", "is_dma", "needed", "semval", "dma_n", "is_cc")

    def __init__(self, eng, idx, fn, is_dma):
        self.eng = eng
        self.idx = idx
        self.fn = fn
        self.waits = []
        self.is_dma = is_dma
        self.needed = False
        self.semval = None
        self.dma_n = None
        self.is_cc = False


class Prog:
    def __init__(self, nc, same_engine_sync=True):
        self.nc = nc
        self.ops = {e: [] for e in ENGS}
        self.seen = {e: {} for e in ENGS}
        self.dma_cnt = {e: 0 for e in ENGS}
        self.same_engine_sync = same_engine_sync
        self.out_dma_ops = []
        self._dma_list = {}
        self.cc_cnt = 0
        self.pending = {e: [] for e in ENGS}

    def _key(self, op):
        if op.is_cc:
            return ("cc",)
        if op.is_dma:
            return ("dma", op.eng, op.dma_n % DMA_K)
        return ("eng", op.eng)

    def _ord(self, op):
        if op.is_cc:
            return op.dma_n
        return op.dma_n // DMA_K if op.is_dma else op.idx

    def _dep(self, op, dep):
        if dep is None or dep is op:
            return
        if (not dep.is_dma) and (not dep.is_cc) and dep.eng == op.eng and not op.is_dma and not op.is_cc:
            if op.eng == "pe" or not self.same_engine_sync:
                return
        k = self._key(dep)
        o = self._ord(dep)
        if self.seen[op.eng].get(k, -1) >= o:
            return
        self.seen[op.eng][k] = o
        dep.needed = True
        op.waits.append(dep)

    def op(self, eng, fn, reads=(), writes=(), dma=False, cc=False):
        op = Op(eng, len(self.ops[eng]), fn, dma)
        if cc:
            op.is_cc = True
            op.dma_n = self.cc_cnt
            self.cc_cnt += 1
            op.needed = True
        reads = [b.root() for b in reads]
        writes = [b.root() for b in writes]
        if self.pending[eng]:
            for d_ in self.pending[eng]:
                self._dep(op, d_)
            self.pending[eng] = []
        if dma:
            op.dma_n = self.dma_cnt[eng]
            self.dma_cnt[eng] += 1
            op.needed = True
            if op.dma_n >= DMA_K:
                prev = self._dma_ops(eng)[op.dma_n - DMA_K]
                self._dep(op, prev)
        for b in reads:
            self._dep(op, b.last_w)
        for b in writes:
            self._dep(op, b.last_w)
            for r in b.readers:
                self._dep(op, r)
        for b in reads:
            b.readers.append(op)
        for b in writes:
            b.last_w = op
            b.readers = []
        self.ops[eng].append(op)
        if dma:
            self._dma_list.setdefault(eng, []).append(op)
        return op

    def _dma_ops(self, eng):
        return self._dma_list.setdefault(eng, [])

    def barrier(self):
        deps = []
        for e in ENGS:
            comp = [o for o in self.ops[e] if not o.is_dma and not o.is_cc]
            if comp:
                deps.append(comp[-1])
            lst = self._dma_ops(e)
            deps.extend(lst[-DMA_K:])
            ccs = [o for o in self.ops[e] if o.is_cc]
            if ccs:
                deps.append(ccs[-1])
        for e in ENGS:
            self.pending[e] = list(deps)

    def cc(self, kind, ins, outs, groups, scratch, reads=(), writes=()):
        ccb = Buf("ccdone")
        self.op("pool", lambda e: e.collective_compute(kind, ALU.bypass, replica_groups=groups, ins=ins, outs=outs),
                reads, [ccb], cc=True)
        return self.op("pool", lambda e: e.memset(scratch[:], 0.0), [ccb], list(writes) + [scratch])

    def dma(self, eng, out, in_, reads=(), writes=(), is_output=False, **kw):
        op = self.op(eng, lambda e: e.dma_start(out=out, in_=in_, **kw), reads, writes, dma=True)
        if is_output:
            self.out_dma_ops.append(op)
        return op

    def mm(self, out, lhsT, rhs, start, stop, reads=(), writes=(), **kw):
        nc = self.nc
        return self.op("pe", lambda e: e.matmul(out, lhsT, rhs, start=start, stop=stop, **kw), reads, writes)

    def tr(self, out, in_, ident, reads=(), writes=()):
        nc = self.nc
        return self.op("pe", lambda e: e.transpose(out, in_, ident), reads, writes)

    def act(self, out, in_, func, reads=(), writes=(), **kw):
        nc = self.nc
        return self.op("act", lambda e: e.activation(out=out, in_=in_, func=func, **kw), reads, writes)

    def V(self, eng, name, *args, reads=(), writes=(), **kw):
        return self.op(eng, lambda e: getattr(e, name)(*args, **kw), reads, writes)

    def emit(self):
        nc = self.nc
        for e in ENGS:
            c = 0
            for op in self.ops[e]:
                if op.is_cc:
                    op.semval = op.dma_n + 1
                elif op.is_dma:
                    op.semval = 16 * (op.dma_n // DMA_K + 1)
                elif op.needed:
                    c += 1
                    op.semval = c
        self.sem_counts = {e: sum(1 for op in self.ops[e] if (op.needed and not op.is_dma)) for e in ENGS}
        self.sem_counts.update({"dma_" + e: self.dma_cnt[e] for e in ENGS})
        import os
        if os.environ.get("PROG_DEBUG"):
            print("sem counts", self.sem_counts)
            print("instr counts", {e: (len(self.ops[e]), sum(len(o.waits) for o in self.ops[e])) for e in ENGS})
        import contextlib
        with contextlib.ExitStack() as st:
            esem = {e: st.enter_context(nc.semaphore("s_" + e)) for e in ENGS}
            dsem = {}
            for e in ENGS:
                if self.dma_cnt[e]:
                    for k in range(min(DMA_K, self.dma_cnt[e])):
                        dsem[(e, k)] = st.enter_context(nc.semaphore("d_%s%d" % (e, k)))
            ccsem = st.enter_context(nc.semaphore("s_cc")) if self.cc_cnt else None
            block = st.enter_context(nc.Block())

            def semof(op):
                if op.is_cc:
                    return ccsem
                if op.is_dma:
                    return dsem[(op.eng, op.dma_n % DMA_K)]
                return esem[op.eng]

            def run(e, engine):
                for op in self.ops[e]:
                    for d in op.waits:
                        engine.wait_ge(semof(d), d.semval)
                    ins = op.fn(engine)
                    if op.is_cc:
                        ins.then_inc(semof(op), 1)
                    elif op.is_dma:
                        ins.then_inc(semof(op), 16)
                    elif op.needed:
                        ins.then_inc(semof(op), 1)
                if e == "sp":
                    for d in self.out_dma_ops:
                        engine.wait_ge(semof(d), d.semval)
                    for q in ENGS:
                        lst = self._dma_ops(q)
                        for d in lst[-DMA_K:]:
                            engine.wait_ge(semof(d), d.semval)

            block.sync(lambda eng: run("sp", eng))
            block.tensor(lambda eng: run("pe", eng))
            block.scalar(lambda eng: run("act", eng))
            block.vector(lambda eng: run("dve", eng))
            block.gpsimd(lambda eng: run("pool", eng))


import contextlib


class Scope:
    def __init__(self, P, parent=None):
        self.P = P
        self.nc = P.nc
        self.st = contextlib.ExitStack()
        self.bufs = []

    def __enter__(self):
        self.st.__enter__()
        return self

    def __exit__(self, *a):
        for b in self.bufs:
            if b.last_w is not None:
                self.P.freed.append(b.last_w)
            self.P.freed.extend(b.readers)
        best = {}
        for op in self.P.freed:
            k = self.P._key(op)
            if k not in best or self.P._ord(op) > self.P._ord(best[k]):
                best[k] = op
        self.P.freed = list(best.values())
        return self.st.__exit__(*a)

    _ctr = [0]

    def _uniq(self, name):
        Scope._ctr[0] += 1
        return "%s_%d" % (name, Scope._ctr[0])

    def _new(self, name, t):
        b = Buf(name, t)
        b.readers = list(self.P.freed)
        self.bufs.append(b)
        return b

    def sb(self, name, shape, dt=F32):
        return self._new(name, self.st.enter_context(self.nc.sbuf_tensor(self._uniq(name), list(shape), dt)))

    def ps(self, name, shape, dt=F32):
        return self._new(name, self.st.enter_context(self.nc.psum_tensor(self._uniq(name), list(shape), dt)))


SAME_ENGINE_SYNC = True


def new_prog(nc, **kw):
    kw.setdefault('same_engine_sync', SAME_ENGINE_SYNC)
    P = Prog(nc, **kw)
    P.freed = []
    return P


def make_ident(P, S, name="ident", n=128, dt=F32):
    ident = S.sb(name, [n, n], dt)
    P.V("pool", "memset", ident[:], 1.0, writes=[ident])
    P.op("pool", lambda e: e.affine_select(out=ident[:], in_=ident[:], pattern=[[-1, n]],
                                           compare_op=ALU.is_equal, fill=0.0, base=0,
                                           channel_multiplier=1), reads=[ident], writes=[ident])
    return ident


D = 1024
DFF = 2816
NMOD = 9
INC = 2464
EPS = 1e-6
KC = D // 128
FC = DFF // 128


def token_blocks(n_lat, n_ctx):
    blks = []
    c = 0
    while c < n_lat:
        w = min(512, n_lat - c)
        blks.append((c, w, 0))
        c += w
    if n_ctx:
        blks.append((n_lat, n_ctx, 1))
    return blks


class StageCtx:
    pass


def emit_modulate(P, S, C, src_view, src_bufs, mi_shift, mi_scale, hT, blks, psb):
    with Scope(P) as A:
        xblk = [A.sb("xblk%d" % i, [128, KC, 512]) for i in range(2)]
        sq = A.sb("sq", [128, KC, 512], BF16)
        rs = A.sb("rs", [128, 512])
        tmp = [A.sb("mtmp%d" % i, [128, 512]) for i in range(2)]
        for bi, (c0, w, mj) in enumerate(blks):
            xb = xblk[bi % 2]
            P.dma("sp", xb[:, :, 0:w], src_view[:, :, c0:c0 + w], reads=src_bufs(bi), writes=[xb])
            P.act(sq[:, :, 0:w], xb[:, :, 0:w], AF.Square, reads=[xb], writes=[sq])
            ss = psb[bi % 2]
            for c in range(KC):
                P.mm(ss[:, 0:w], C.ones_bf[:], sq[:, c, 0:w], c == 0, c == KC - 1, reads=[C.ones_bf, sq], writes=[ss])
            P.act(rs[:, 0:w], ss[:, 0:w], AF.Sqrt, reads=[ss, C.eps_col], writes=[rs], scale=1.0 / D, bias=C.eps_col[:, 0:1])
            P.V("dve", "reciprocal", rs[:, 0:w], rs[:, 0:w], reads=[rs], writes=[rs])
            for c in range(KC):
                t = tmp[c % 2]
                P.V("dve", "tensor_tensor", t[:, 0:w], xb[:, c, 0:w], rs[:, 0:w], ALU.mult, reads=[xb, rs], writes=[t])
                P.act(hT[:, c, c0:c0 + w], t[:, 0:w], AF.Identity, reads=[t, C.modp], writes=[hT],
                      scale=C.modp[:, mi_scale * KC + c, mj:mj + 1], bias=C.modp[:, mi_shift * KC + c, mj:mj + 1])


def emit_ffn(P, S, C, w13_d, w2_d, hT, blks, psb, mi_gate, x_view, x_bufs, out_view, out_bufs, NT):
    w13v = w13_d.rearrange("(c p) (two f) -> p c two f", p=128, two=2)
    w2v = w2_d.rearrange("(k p) f -> p k f", p=128)
    with Scope(P) as B:
        act = B.sb("act", [128, FC, NT], BF16)
        with Scope(P) as B2:
            wgu = [B2.sb("wgu%d" % i, [128, KC, 2, 128], BF16) for i in range(3)]
            sg = [B2.sb("sg%d" % i, [128, 512]) for i in range(2)]
            n = 0
            for i in range(FC):
                wt = wgu[i % 3]
                P.dma("pool", wt[:, :, 0, :], w13v[:, :, 0, i * 128:(i + 1) * 128], writes=[wt])
                P.dma("pool", wt[:, :, 1, :], w13v[:, :, 1, i * 128:(i + 1) * 128], writes=[wt])
                for bi, (c0, w, mj) in enumerate(blks):
                    pg = psb[2 + (n % 2) * 2]
                    pu = psb[3 + (n % 2) * 2]
                    for c in range(KC):
                        P.mm(pg[:, 0:w], wt[:, c, 0, :], hT[:, c, c0:c0 + w], c == 0, c == KC - 1, reads=[wt, hT], writes=[pg])
                    for c in range(KC):
                        P.mm(pu[:, 0:w], wt[:, c, 1, :], hT[:, c, c0:c0 + w], c == 0, c == KC - 1, reads=[wt, hT], writes=[pu])
                    s = sg[n % 2]
                    P.act(s[:, 0:w], pg[:, 0:w], AF.Silu, reads=[pg], writes=[s])
                    P.V("dve", "tensor_tensor", act[:, i, c0:c0 + w], s[:, 0:w], pu[:, 0:w], ALU.mult, reads=[s, pu], writes=[act])
                    n += 1
        with Scope(P) as B3:
            wd = [B3.sb("wd%d" % i, [128, FC, 128], BF16) for i in range(2)]
            xo = [B3.sb("xo%d" % i, [128, 512]) for i in range(3)]
            xn = [B3.sb("xn%d" % i, [128, 512]) for i in range(3)]
            n = 0
            for o in range(KC):
                wt = wd[o % 2]
                P.dma("pool", wt[:], w2v[:, :, o * 128:(o + 1) * 128], writes=[wt])
                for bi, (c0, w, mj) in enumerate(blks):
                    py = psb[n % 2]
                    for k in range(FC):
                        P.mm(py[:, 0:w], wt[:, k, :], act[:, k, c0:c0 + w], k == 0, k == FC - 1, reads=[wt, act], writes=[py])
                    xi = xo[n % 3]
                    xw = xn[n % 3]
                    P.dma("sp", xi[:, 0:w], x_view[:, o, c0:c0 + w], reads=x_bufs(bi), writes=[xi])
                    P.V("dve", "scalar_tensor_tensor", xw[:, 0:w], py[:, 0:w], C.modh[:, mi_gate * KC + o, mj:mj + 1], xi[:, 0:w],
                        ALU.mult, ALU.add, reads=[py, xi, C.modh], writes=[xw])
                    P.dma("sp", out_view[:, o, c0:c0 + w], xw[:, 0:w], reads=[xw], writes=[out_bufs[(o, bi)]],
                          is_output=True)
                    n += 1


def emit_proj(P, S, C, w_d, ncols, hT, blks, psb, out_fn):
    wv = w_d.rearrange("(c p) f -> p c f", p=128)
    nf = (ncols + 127) // 128
    with Scope(P) as E:
        wt_ = [E.sb("wpr%d" % i, [128, KC, 128], BF16) for i in range(3)]
        n = 0
        for f in range(nf):
            fw = min(128, ncols - f * 128)
            wt = wt_[f % 3]
            P.dma("pool", wt[:, :, 0:fw], wv[:, :, f * 128:f * 128 + fw], writes=[wt])
            for bi, blk in enumerate(blks):
                c0, w, mj = blk
                pp = psb[4 + n % 4]
                for c in range(KC):
                    P.mm(pp[0:fw, 0:w], wt[:, c, 0:fw], hT[:, c, c0:c0 + w], c == 0, c == KC - 1, reads=[wt, hT], writes=[pp])
                out_fn(E, n, f, fw, bi, blk, pp)
                n += 1


def build_stage(kind, NT_lat, NT_ctx, final=False):
    NT = NT_lat + NT_ctx
    nc = bass.Bass("TRN2", target_bir_lowering=False)
    P = new_prog(nc)

    def din(name, shape):
        return nc.dram_tensor(name, list(shape), F32, kind="ExternalInput").ap()

    def dout(name, shape):
        return nc.dram_tensor(name, list(shape), F32, kind="ExternalOutput").ap()

    T = {"xT": din("xT", [D, NT]), "w13": din("w13", [D, 2 * DFF]), "w2": din("w2", [DFF, D]), "xnew": dout("xnew", [D, NT])}
    if kind == "P":
        T.update(cT=din("cT", [128, KC, 2]), ada_w=din("ada_w", [D, NMOD * D]), ada_b=din("ada_b", [128, NMOD * KC]),
                 w_in=din("w_in", [D, INC]), modo=dout("modo", [128, NMOD * KC, 2]), uT=dout("uT", [INC, NT]))
    else:
        T.update(modi=din("modi", [128, NMOD * KC, 2]), yT=din("yT", [D, NT]), w_out=din("w_out", [D, D]), xmid=dout("xmid", [D, NT]))
        if final:
            T.update(fin_g=din("fin_g", [128, KC]), xfin=dout("xfin", [D, NT]))
    emit_stage(P, nc, kind, NT_lat, NT_ctx, final, T)
    P.emit()
    return nc


def emit_stage(P, nc, kind, NT_lat, NT_ctx, final, T):
    NT = NT_lat + NT_ctx
    blks = token_blocks(NT_lat, NT_ctx)
    nb = len(blks)
    xT = T["xT"]; w13 = T["w13"]; w2 = T["w2"]; xnew = T["xnew"]
    xv = xT.rearrange("(c p) t -> p c t", p=128)
    xnv = xnew.rearrange("(c p) t -> p c t", p=128)
    if kind == "P":
        cT = T["cT"]; ada_w = T["ada_w"]; ada_b = T["ada_b"]; w_in = T["w_in"]; modo = T["modo"]; uT = T["uT"]
    else:
        modi = T["modi"]; yT = T["yT"]; w_out = T["w_out"]; xmid = T["xmid"]
        yv = yT.rearrange("(c p) t -> p c t", p=128)
        xmv = xmid.rearrange("(c p) t -> p c t", p=128)
        if final:
            fin_g = T["fin_g"]; xfin = T["xfin"]
            xfv = xfin.rearrange("(c p) t -> p c t", p=128)

    with Scope(P) as S:
        C = StageCtx()
        C.ones_bf = S.sb("ones_bf", [128, 128], BF16)
        P.V("dve", "memset", C.ones_bf[:], 1.0, writes=[C.ones_bf])
        C.eps_col = S.sb("eps_col", [128, 1])
        P.V("dve", "memset", C.eps_col[:], EPS, writes=[C.eps_col])
        C.mod = S.sb("mod", [128, NMOD * KC, 2])
        C.modp = S.sb("modp", [128, NMOD * KC, 2])
        C.modh = S.sb("modh", [128, NMOD * KC, 2])
        psb = [S.ps("psb%d" % i, [128, 512]) for i in range(8)]
        hT = S.sb("hT", [128, KC, NT], BF16)
        none_bufs = lambda bi: []

        if kind == "P":
            with Scope(P) as M:
                cs = M.sb("cs", [128, KC, 2])
                cb = M.sb("cb", [128, KC, 2], BF16)
                ab = M.sb("ab", [128, NMOD * KC])
                P.dma("sp", cs[:], cT, writes=[cs])
                P.dma("sp", ab[:], ada_b, writes=[ab])
                P.act(cb[:], cs[:], AF.Silu, reads=[cs], writes=[cb])
                GRP = 4
                aw = [M.sb("aw%d" % i, [128, KC, GRP * 128], BF16) for i in range(2)]
                av = ada_w.rearrange("(c p) f -> p c f", p=128)
                pm = psb[0]
                ng = NMOD * KC // GRP
                for g in range(ng):
                    a = aw[g % 2]
                    P.dma("pool", a[:], av[:, :, g * GRP * 128:(g + 1) * GRP * 128], writes=[a])
                    for mm_ in range(GRP):
                        m = g * GRP + mm_
                        for c in range(KC):
                            P.mm(pm[:, 2 * m:2 * m + 2], a[:, c, mm_ * 128:(mm_ + 1) * 128], cb[:, c, :], c == 0, c == KC - 1,
                                 reads=[a, cb], writes=[pm])
                for j in range(2):
                    P.V("dve", "tensor_tensor", C.mod[:, :, j], pm[:, j:2 * NMOD * KC:2], ab[:], ALU.add,
                        reads=[pm, ab], writes=[C.mod])
            P.dma("sp", modo, C.mod[:], reads=[C.mod], is_output=True)
        else:
            P.dma("sp", C.mod[:], modi, writes=[C.mod])
        P.V("dve", "tensor_scalar_add", C.modp[:], C.mod[:], 1.0, reads=[C.mod], writes=[C.modp])
        for i in (0, 3, 6):
            P.V("dve", "tensor_copy", C.modp[:, i * KC:(i + 1) * KC, :], C.mod[:, i * KC:(i + 1) * KC, :], reads=[C.mod, C.modp], writes=[C.modp])
        P.V("dve", "tensor_scalar_mul", C.modh[:], C.mod[:], 0.5, reads=[C.mod], writes=[C.modh])
        P.V("dve", "tensor_copy", C.modh[:, 5 * KC:6 * KC, :], C.mod[:, 5 * KC:6 * KC, :], reads=[C.mod, C.modh], writes=[C.modh])

        if kind == "P":
            out1 = {(o, bi): Buf("x1_%d_%d" % (o, bi)) for o in range(KC) for bi in range(nb)}
            emit_modulate(P, S, C, xv, none_bufs, 0, 1, hT, blks, psb)
            emit_ffn(P, S, C, w13, w2, hT, blks, psb, 2, xv, none_bufs, xnv, out1, NT)
            rd1 = lambda bi: [out1[(o, bi)] for o in range(KC)]
            emit_modulate(P, S, C, xnv, rd1, 3, 4, hT, blks, psb)
            with Scope(P) as U:
                us = [U.sb("us%d" % i, [128, 512]) for i in range(4)]

                def out_u(E, n, f, fw, bi, blk, pp):
                    c0, w, mj = blk
                    t = us[n % 4]
                    if n % 2 == 0:
                        P.V("dve", "tensor_copy", t[0:fw, 0:w], pp[0:fw, 0:w], reads=[pp], writes=[t])
                    else:
                        P.act(t[0:fw, 0:w], pp[0:fw, 0:w], AF.Copy, reads=[pp], writes=[t])
                    P.dma("sp", uT[f * 128:f * 128 + fw, c0:c0 + w], t[0:fw, 0:w], reads=[t], is_output=True)

                emit_proj(P, S, C, w_in, INC, hT, blks, psb, out_u)
        else:
            mid = {(o, bi): Buf("xm_%d_%d" % (o, bi)) for o in range(KC) for bi in range(nb)}
            with Scope(P) as W:
                ybf = W.sb("ybf", [128, KC, NT], BF16)
                for bi, (c0, w, mj) in enumerate(blks):
                    P.dma("pool", ybf[:, :, c0:c0 + w], yv[:, :, c0:c0 + w], writes=[ybf])
                xo = [W.sb("wxo%d" % i, [128, 512]) for i in range(3)]
                xn = [W.sb("wxn%d" % i, [128, 512]) for i in range(3)]

                def out_w(E, n, f, fw, bi, blk, pp):
                    c0, w, mj = blk
                    xi = xo[n % 3]
                    xw = xn[n % 3]
                    P.dma("sp", xi[:, 0:w], xv[:, f, c0:c0 + w], writes=[xi])
                    P.V("dve", "scalar_tensor_tensor", xw[:, 0:w], pp[:, 0:w], C.modh[:, 5 * KC + f, mj:mj + 1], xi[:, 0:w],
                        ALU.mult, ALU.add, reads=[pp, xi, C.modh], writes=[xw])
                    P.dma("sp", xmv[:, f, c0:c0 + w], xw[:, 0:w], reads=[xw], writes=[mid[(f, bi)]], is_output=True)

                emit_proj(P, S, C, w_out, D, ybf, blks, psb, out_w)
            rdm = lambda bi: [mid[(o, bi)] for o in range(KC)]
            emit_modulate(P, S, C, xmv, rdm, 6, 7, hT, blks, psb)
            out2 = {(o, bi): Buf("x2_%d_%d" % (o, bi)) for o in range(KC) for bi in range(nb)}
            emit_ffn(P, S, C, w13, w2, hT, blks, psb, 8, xmv, rdm, xnv, out2, NT)
            if final:
                rd2 = lambda bi: [out2[(o, bi)] for o in range(KC)]
                with Scope(P) as Fz:
                    fg = Fz.sb("fg", [128, KC])
                    P.dma("sp", fg[:], fin_g, writes=[fg])
                    xblk = [Fz.sb("fxb%d" % i, [128, KC, 512]) for i in range(2)]
                    sq = Fz.sb("fsq", [128, KC, 512], BF16)
                    rs = Fz.sb("frs", [128, 512])
                    ob = [Fz.sb("fob%d" % i, [128, KC, 512]) for i in range(2)]
                    for bi, (c0, w, mj) in enumerate(blks):
                        xb = xblk[bi % 2]
                        oo = ob[bi % 2]
                        P.dma("sp", xb[:, :, 0:w], xnv[:, :, c0:c0 + w], reads=rd2(bi), writes=[xb])
                        P.act(sq[:, :, 0:w], xb[:, :, 0:w], AF.Square, reads=[xb], writes=[sq])
                        ss = psb[bi % 2]
                        for c in range(KC):
                            P.mm(ss[:, 0:w], C.ones_bf[:], sq[:, c, 0:w], c == 0, c == KC - 1, reads=[C.ones_bf, sq], writes=[ss])
                        P.act(rs[:, 0:w], ss[:, 0:w], AF.Sqrt, reads=[ss, C.eps_col], writes=[rs], scale=1.0 / D, bias=C.eps_col[:, 0:1])
                        P.V("dve", "reciprocal", rs[:, 0:w], rs[:, 0:w], reads=[rs], writes=[rs])
                        for c in range(KC):
                            P.V("dve", "scalar_tensor_tensor", oo[:, c, 0:w], xb[:, c, 0:w], fg[:, c:c + 1], rs[:, 0:w],
                                ALU.mult, ALU.mult, reads=[xb, rs, fg], writes=[oo])
                        P.dma("sp", xfv[:, :, c0:c0 + w], oo[:, :, 0:w], reads=[oo], is_output=True)


TT = 8448
NCTX = 256
NTILE = TT // 128
SM_SCALE_F = float((64 + 32) ** -0.5)
CONVK = 31
LN_EPS_F = 1e-5
GN_EPS_F = 64e-5


def seq_blocks():
    b = [(0, NCTX, True)]
    c = NCTX
    while c < TT:
        b.append((c, 512, False))
        c += 512
    return b


def emit_conv(P, S, nc, io, psb):
    ua_lat, ua_ctx, cvw, cvb, lng, lnb, ya = io["ua_lat"], io["ua_ctx"], io["cv_w"], io["cv_b"], io["cv_lng"], io["cv_lnb"], io["ya"]
    with Scope(P) as A:
        w = A.sb("cvw", [128, 2, CONVK]); bb = A.sb("cvb", [128, 2]); g = A.sb("cvg", [128, 2]); be = A.sb("cvbe", [128, 2])
        P.dma("sp", w[:], cvw, writes=[w]); P.dma("sp", bb[:], cvb, writes=[bb])
        P.dma("sp", g[:], lng, writes=[g]); P.dma("sp", be[:], lnb, writes=[be])
        onesf = A.sb("cv_ones", [128, 128])
        P.V("dve", "memset", onesf[:], 1.0 / 256.0, writes=[onesf])
        epsc = A.sb("cv_eps", [128, 1])
        P.V("dve", "memset", epsc[:], LN_EPS_F, writes=[epsc])
        for (src, N, ocol) in ((ua_lat, 2048, 0), (ua_ctx, 64, 2048)):
            NP = N + 30
            sv = src.rearrange("(two c p) t -> p two c t", p=128, two=2)
            val = A.sb("cv_val", [128, 2, NP]); gate = A.sb("cv_gate", [128, 2, NP])
            P.dma("sp", val[:], sv[:, 0], writes=[val]); P.dma("sp", gate[:], sv[:, 1], writes=[gate])
            P.act(gate[:], gate[:], AF.Sigmoid, reads=[gate], writes=[gate])
            P.V("dve", "tensor_tensor", val[:], val[:], gate[:], ALU.mult, reads=[val, gate], writes=[val])
            acc = A.sb("cv_acc", [128, 2, N]); sq = A.sb("cv_sq", [128, 2, N])
            for c in range(2):
                P.V("dve", "tensor_scalar", acc[:, c, :], val[:, c, 0:N], w[:, c, 0:1], bb[:, c:c + 1], ALU.mult, ALU.add,
                    reads=[val, w, bb], writes=[acc])
                for j in range(1, CONVK):
                    P.V("dve", "scalar_tensor_tensor", acc[:, c, :], val[:, c, j:j + N], w[:, c, j:j + 1], acc[:, c, :],
                        ALU.mult, ALU.add, reads=[val, w, acc], writes=[acc])
            P.act(sq[:], acc[:], AF.Square, reads=[acc], writes=[sq])
            c0 = 0
            ot = [A.sb("cv_o%d" % i, [128, 2, 512]) for i in range(2)]
            mu = A.sb("cv_mu", [128, 512]); var = A.sb("cv_var", [128, 512]); tmp = A.sb("cv_tmp", [128, 512])
            bi = 0
            while c0 < N:
                wd_ = min(512, N - c0)
                p1 = psb[0]; p2 = psb[1]
                for c in range(2):
                    P.mm(p1[:, 0:wd_], onesf[:], acc[:, c, c0:c0 + wd_], c == 0, c == 1, reads=[onesf, acc], writes=[p1])
                for c in range(2):
                    P.mm(p2[:, 0:wd_], onesf[:], sq[:, c, c0:c0 + wd_], c == 0, c == 1, reads=[onesf, sq], writes=[p2])
                P.act(mu[:, 0:wd_], p1[:, 0:wd_], AF.Copy, reads=[p1], writes=[mu])
                P.V("dve", "tensor_tensor", tmp[:, 0:wd_], mu[:, 0:wd_], mu[:, 0:wd_], ALU.mult, reads=[mu], writes=[tmp])
                P.V("dve", "tensor_tensor", var[:, 0:wd_], p2[:, 0:wd_], tmp[:, 0:wd_], ALU.subtract, reads=[p2, tmp], writes=[var])
                P.act(var[:, 0:wd_], var[:, 0:wd_], AF.Sqrt, reads=[var, epsc], writes=[var], bias=epsc[:, 0:1])
                P.V("dve", "reciprocal", var[:, 0:wd_], var[:, 0:wd_], reads=[var], writes=[var])
                o = ot[bi % 2]
                for c in range(2):
                    P.V("dve", "tensor_tensor", tmp[:, 0:wd_], acc[:, c, c0:c0 + wd_], mu[:, 0:wd_], ALU.subtract, reads=[acc, mu], writes=[tmp])
                    P.V("dve", "tensor_tensor", tmp[:, 0:wd_], tmp[:, 0:wd_], var[:, 0:wd_], ALU.mult, reads=[tmp, var], writes=[tmp])
                    P.V("dve", "tensor_scalar", tmp[:, 0:wd_], tmp[:, 0:wd_], g[:, c:c + 1], be[:, c:c + 1], ALU.mult, ALU.add,
                        reads=[tmp, g, be], writes=[tmp])
                    P.act(o[:, c, 0:wd_], tmp[:, 0:wd_], AF.Silu, reads=[tmp], writes=[o])
                P.dma("sp", ya.rearrange("(c p) t -> p c t", p=128)[:, :, ocol + c0:ocol + c0 + wd_], o[:, :, 0:wd_], reads=[o], is_output=True)
                c0 += wd_
                bi += 1


LRU_STOP = 0


def emit_lru(P, S, nc, io, psb):
    ub = io["ub"]
    blocks = seq_blocks()
    with Scope(P) as A:
        cw = A.sb("lr_cw", [128, 4]); cb = A.sb("lr_cb", [128, 1])
        Wa = A.sb("lr_wa", [64, 128]); Wx = A.sb("lr_wx", [64, 128])
        ba = A.sb("lr_ba", [128, 1]); bx = A.sb("lr_bx", [128, 1]); lam = A.sb("lr_lam", [128, 1])
        for t_, n_ in ((cw, "lru_cw"), (cb, "lru_cb"), (Wa, "lru_wa"), (Wx, "lru_wx"), (ba, "lru_ba"), (bx, "lru_bx"), (lam, "lru_lam")):
            P.dma("sp", t_[:], io[n_], writes=[t_])
        cc = A.sb("lr_c", [128, 1]); c2 = A.sb("lr_c2", [128, 1])
        P.act(cc[:], lam[:], AF.Exp, reads=[lam], writes=[cc], scale=-1.0)
        P.act(cc[:], cc[:], AF.Ln, reads=[cc], writes=[cc], bias=1.0)
        P.V("dve", "tensor_scalar_mul", c2[:], cc[:], -16.0, reads=[cc], writes=[c2])
        P.V("dve", "tensor_scalar_mul", cc[:], cc[:], -8.0, reads=[cc], writes=[cc])
        ident = make_ident(P, A, "lr_id")
        stk = A.sb("lr_stk", [128, 64])
        P.V("dve", "tensor_copy", stk[0:64, :], ident[0:64, 0:64], reads=[ident], writes=[stk])
        P.V("dve", "tensor_copy", stk[64:128, :], ident[64:128, 64:128], reads=[ident, stk], writes=[stk])
        X = A.sb("lr_x", [128, TT]); XV = A.sb("lr_xv", [128, TT])
        Aa = A.sb("lr_a", [128, TT]); Bb = A.sb("lr_b", [128, TT])
        P.dma("sp", X[0:64, :], ub[0:64, :], writes=[X])
        P.dma("sp", X[64:128, :], ub[0:64, :], writes=[X])
        P.V("dve", "tensor_scalar", XV[:], X[:], cw[:, 2:3], cb[:, 0:1], ALU.mult, ALU.add, reads=[X, cw, cb], writes=[XV])
        for (lo, hi) in ((0, NCTX), (NCTX, TT)):
            for i, s in ((0, -2), (1, -1), (3, 1)):
                a = max(lo, lo - s); b = min(hi, hi - s)
                P.V("dve", "scalar_tensor_tensor", XV[:, a:b], X[:, a + s:b + s], cw[:, i:i + 1], XV[:, a:b], ALU.mult, ALU.add,
                    reads=[X, cw, XV], writes=[XV])
        for bi, (c0, w, isc) in enumerate(blocks):
            pa = psb[(bi % 2) * 2]; px = psb[(bi % 2) * 2 + 1]
            P.mm(pa[:, 0:w], Wa[:], XV[0:64, c0:c0 + w], True, True, reads=[Wa, XV], writes=[pa])
            P.mm(px[:, 0:w], Wx[:], XV[0:64, c0:c0 + w], True, True, reads=[Wx, XV], writes=[px])
            P.act(Aa[:, c0:c0 + w], pa[:, 0:w], AF.Sigmoid, reads=[pa, ba], writes=[Aa], bias=ba[:, 0:1])
            P.act(Bb[:, c0:c0 + w], px[:, 0:w], AF.Sigmoid, reads=[px, bx], writes=[Bb], bias=bx[:, 0:1])
        t1 = [A.sb("lr_t%d" % i, [128, 512]) for i in range(2)]
        for bi, (c0, w, isc) in enumerate(blocks):
            t = t1[bi % 2]
            P.act(t[:, 0:w], Aa[:, c0:c0 + w], AF.Exp, reads=[Aa, c2], writes=[t], scale=c2[:, 0:1])
            P.act(Aa[:, c0:c0 + w], Aa[:, c0:c0 + w], AF.Exp, reads=[Aa, cc], writes=[Aa], scale=cc[:, 0:1])
            P.V("dve", "tensor_scalar", t[:, 0:w], t[:, 0:w], -1.0, 1.0, ALU.mult, ALU.add, reads=[t], writes=[t])
            P.V("dve", "tensor_scalar_max", t[:, 0:w], t[:, 0:w], 1e-30, reads=[t], writes=[t])
            P.act(t[:, 0:w], t[:, 0:w], AF.Ln, reads=[t], writes=[t])
            P.act(t[:, 0:w], t[:, 0:w], AF.Exp, reads=[t], writes=[t], scale=0.5)
            P.V("dve", "tensor_tensor", Bb[:, c0:c0 + w], Bb[:, c0:c0 + w], XV[:, c0:c0 + w], ALU.mult, reads=[Bb, XV], writes=[Bb])
            P.V("dve", "tensor_tensor", Bb[:, c0:c0 + w], Bb[:, c0:c0 + w], t[:, 0:w], ALU.mult, reads=[Bb, t], writes=[Bb])
        H = X
        P.V("dve", "tensor_tensor_scan", H[0:64, 0:NCTX], Aa[0:64, 0:NCTX], Bb[0:64, 0:NCTX], 0.0, ALU.mult, ALU.add,
            reads=[Aa, Bb, XV], writes=[H])
        P.V("dve", "tensor_tensor_scan", H[0:64, NCTX:TT], Aa[0:64, NCTX:TT], Bb[0:64, NCTX:TT], H[0:64, NCTX - 1:NCTX], ALU.mult, ALU.add,
            reads=[Aa, Bb, H], writes=[H])
        P.V("dve", "tensor_tensor_scan", H[64:128, NCTX - 1::-1], Aa[64:128, NCTX - 1::-1], Bb[64:128, NCTX - 1::-1], 0.0, ALU.mult, ALU.add,
            reads=[Aa, Bb, H], writes=[H])
        P.V("dve", "tensor_tensor_scan", H[64:128, TT - 1:NCTX - 1:-1], Aa[64:128, TT - 1:NCTX - 1:-1], Bb[64:128, TT - 1:NCTX - 1:-1], H[64:128, 0:1], ALU.mult, ALU.add,
            reads=[Aa, Bb, H], writes=[H])
        G = XV
        P.dma("sp", G[0:64, :], ub[64:128, :], reads=[Bb], writes=[G])
        o2 = [A.sb("lr_o%d" % i, [64, 512]) for i in range(2)]
        for bi, (c0, w, isc) in enumerate(blocks):
            t = t1[bi % 2]
            ph = psb[4 + bi % 2]
            P.mm(ph[0:64, 0:w], stk[:], H[:, c0:c0 + w], True, True, reads=[stk, H], writes=[ph])
            g_ = G[0:64, c0:c0 + w]
            tt = t[0:64, 0:w]
            P.V("dve", "tensor_tensor", tt, g_, g_, ALU.mult, reads=[G], writes=[t])
            P.V("dve", "tensor_scalar", tt, tt, 0.044715, 1.0, ALU.mult, ALU.add, reads=[t], writes=[t])
            P.V("dve", "tensor_tensor", tt, tt, g_, ALU.mult, reads=[t, G], writes=[t])
            P.act(tt, tt, AF.Sigmoid, reads=[t], writes=[t], scale=1.5957691216057308)
            P.V("dve", "tensor_tensor", tt, tt, g_, ALU.mult, reads=[t, G], writes=[t])
            o = o2[bi % 2]
            P.V("dve", "tensor_tensor", o[:, 0:w], ph[0:64, 0:w], tt, ALU.mult, reads=[ph, t], writes=[o])
            for (dst, off, ww) in io["ypieces"]("yb", c0, w):
                P.dma("sp", dst, o[:, off:off + ww], reads=[o], is_output=True)


def emit_mla(P, S, nc, io, psb, need_ctx_q):
    ud = io["ud"]
    blocks = seq_blocks()
    with Scope(P) as A:
        wuq = A.sb("ml_wuq", [128, 2, 96], BF16); wkn = A.sb("ml_wkn", [128, 64], BF16); wv = A.sb("ml_wv", [128, 64], BF16)
        qn = A.sb("ml_qn", [128, 2]); kvn = A.sb("ml_kvn", [128, 1]); Rp = A.sb("ml_rp", [96, 96])
        P.dma("pool", wuq[:], io["mla_wuq"].rearrange("(c p) f -> p c f", p=128), writes=[wuq])
        P.dma("pool", wkn[:], io["mla_wkn"], writes=[wkn]); P.dma("pool", wv[:], io["mla_wv"], writes=[wv])
        P.dma("sp", qn[:], io["mla_qn"], writes=[qn]); P.dma("sp", kvn[:], io["mla_kvn"], writes=[kvn])
        P.dma("sp", Rp[:], io["rope_R"], writes=[Rp])
        ones_bf = A.sb("ml_ones", [128, 128], BF16)
        P.V("dve", "memset", ones_bf[:], 1.0, writes=[ones_bf])
        onesf = A.sb("ml_onesf", [128, 64])
        P.V("dve", "memset", onesf[:], 1.0, writes=[onesf])
        epsc = A.sb("ml_eps", [128, 1])
        P.V("dve", "memset", epsc[:], EPS, writes=[epsc])
        KT = A.sb("ml_KT", [96, TT], BF16); QT = A.sb("ml_QT", [96, TT], BF16)
        Va = A.sb("ml_Va", [128, NTILE, 65], BF16)
        P.V("pool", "memset", Va[:], 1.0, writes=[Va])
        qm = A.sb("ml_qm", [128, 1]); km = A.sb("ml_km", [128, 1]); negM = A.sb("ml_negM", [128, 1])
        P.V("dve", "memset", qm[:], 0.0, writes=[qm]); P.V("dve", "memset", km[:], 0.0, writes=[km])
        udv = ud[0:384, :].rearrange("(c p) t -> p c t", p=128)
        with Scope(P) as B:
            xin = [B.sb("ml_x%d" % i, [128, 3, 512]) for i in range(2)]
            kr = [B.sb("ml_kr%d" % i, [96, 512]) for i in range(2)]
            for k_ in kr:
                P.V("pool", "memset", k_[:], 0.0, writes=[k_])
            sq_2 = [B.sb("ml_sq%d" % _i, [128, 3, 512], BF16) for _i in range(2)]
            rq_2 = [B.sb("ml_rq%d" % _i, [128, 512]) for _i in range(2)]; rk_2 = [B.sb("ml_rk%d" % _i, [128, 512]) for _i in range(2)]
            cqn_2 = [B.sb("ml_cqn%d" % _i, [128, 2, 512], BF16) for _i in range(2)]; ckvn_2 = [B.sb("ml_ckvn%d" % _i, [128, 512], BF16) for _i in range(2)]
            qsb_2 = [B.sb("ml_qsb%d" % _i, [96, 512]) for _i in range(2)]; t1_2 = [B.sb("ml_t1%d" % _i, [96, 512]) for _i in range(2)]; t2_2 = [B.sb("ml_t2%d" % _i, [96, 512]) for _i in range(2)]
            ct_2 = [B.sb("ml_ct%d" % _i, [96, 512]) for _i in range(2)]; st__2 = [B.sb("ml_st%d" % _i, [96, 512]) for _i in range(2)]
            sqq = B.sb("ml_sqq", [96, 512], BF16); mx = B.sb("ml_mx", [128, 1])
            for bi, (c0, w, isc) in enumerate(blocks):
                x = xin[bi % 2]; krt = kr[bi % 2]
                sq = sq_2[bi % 2]; rq = rq_2[bi % 2]; rk = rk_2[bi % 2]; cqn = cqn_2[bi % 2]; ckvn = ckvn_2[bi % 2]
                qsb = qsb_2[bi % 2]; t1 = t1_2[bi % 2]; t2 = t2_2[bi % 2]; ct = ct_2[bi % 2]; st_ = st__2[bi % 2]
                P.dma("sp", x[:, :, 0:w], udv[:, :, c0:c0 + w], writes=[x])
                P.dma("sp", krt[64:96, 0:w], ud[384:416, c0:c0 + w], writes=[krt])
                if not isc:
                    P.dma("sp", ct[:, 0:w], io["rope_C"][:, c0 - NCTX:c0 - NCTX + w], writes=[ct])
                    P.dma("sp", st_[:, 0:w], io["rope_S"][:, c0 - NCTX:c0 - NCTX + w], writes=[st_])
                P.act(sq[:, :, 0:w], x[:, :, 0:w], AF.Square, reads=[x], writes=[sq])
                pq = psb[0]; pk = psb[1]
                for c in range(2):
                    P.mm(pq[:, 0:w], ones_bf[:], sq[:, c, 0:w], c == 0, c == 1, reads=[ones_bf, sq], writes=[pq])
                P.mm(pk[:, 0:w], ones_bf[:], sq[:, 2, 0:w], True, True, reads=[ones_bf, sq], writes=[pk])
                P.act(rq[:, 0:w], pq[:, 0:w], AF.Sqrt, reads=[pq, epsc], writes=[rq], scale=1.0 / 256.0, bias=epsc[:, 0:1])
                P.V("dve", "reciprocal", rq[:, 0:w], rq[:, 0:w], reads=[rq], writes=[rq])
                P.act(rk[:, 0:w], pk[:, 0:w], AF.Sqrt, reads=[pk, epsc], writes=[rk], scale=1.0 / 128.0, bias=epsc[:, 0:1])
                P.V("dve", "reciprocal", rk[:, 0:w], rk[:, 0:w], reads=[rk], writes=[rk])
                for c in range(2):
                    P.V("dve", "scalar_tensor_tensor", cqn[:, c, 0:w], x[:, c, 0:w], qn[:, c:c + 1], rq[:, 0:w], ALU.mult, ALU.mult,
                        reads=[x, qn, rq], writes=[cqn])
                P.V("dve", "scalar_tensor_tensor", ckvn[:, 0:w], x[:, 2, 0:w], kvn[:, 0:1], rk[:, 0:w], ALU.mult, ALU.mult,
                    reads=[x, kvn, rk], writes=[ckvn])
                pqq = psb[2]
                for c in range(2):
                    P.mm(pqq[0:96, 0:w], wuq[:, c, :], cqn[:, c, 0:w], c == 0, c == 1, reads=[wuq, cqn], writes=[pqq])
                if isc:
                    P.act(QT[:, c0:c0 + w], pqq[0:96, 0:w], AF.Copy, reads=[pqq], writes=[QT])
                else:
                    P.act(qsb[:, 0:w], pqq[0:96, 0:w], AF.Copy, reads=[pqq], writes=[qsb])
                    pr = psb[3]
                    P.mm(pr[0:96, 0:w], Rp[:], qsb[:, 0:w], True, True, reads=[Rp, qsb], writes=[pr])
                    P.V("dve", "tensor_tensor", t1[:, 0:w], qsb[:, 0:w], ct[:, 0:w], ALU.mult, reads=[qsb, ct], writes=[t1])
                    P.V("dve", "tensor_tensor", t2[:, 0:w], pr[0:96, 0:w], st_[:, 0:w], ALU.mult, reads=[pr, st_], writes=[t2])
                    P.V("dve", "tensor_tensor", QT[:, c0:c0 + w], t1[:, 0:w], t2[:, 0:w], ALU.add, reads=[t1, t2], writes=[QT])
                pkn = psb[4]
                P.mm(pkn[0:64, 0:w], wkn[:], ckvn[:, 0:w], True, True, reads=[wkn, ckvn], writes=[pkn])
                P.act(KT[0:64, c0:c0 + w], pkn[0:64, 0:w], AF.Copy, reads=[pkn], writes=[KT])
                if isc:
                    P.V("dve", "tensor_copy", KT[64:96, c0:c0 + w], krt[64:96, 0:w], reads=[krt], writes=[KT])
                else:
                    prk = psb[5]
                    P.mm(prk[0:96, 0:w], Rp[:], krt[:, 0:w], True, True, reads=[Rp, krt], writes=[prk])
                    P.V("dve", "tensor_tensor", t1[64:96, 0:w], krt[64:96, 0:w], ct[64:96, 0:w], ALU.mult, reads=[krt, ct], writes=[t1])
                    P.V("dve", "tensor_tensor", t2[64:96, 0:w], prk[64:96, 0:w], st_[64:96, 0:w], ALU.mult, reads=[prk, st_], writes=[t2])
                    P.V("dve", "tensor_tensor", KT[64:96, c0:c0 + w], t1[64:96, 0:w], t2[64:96, 0:w], ALU.add, reads=[t1, t2], writes=[KT])
                for ti in range(w // 128):
                    kt = c0 // 128 + ti
                    pv = psb[6 + ti % 2]
                    P.mm(pv[:, 0:64], ckvn[:, ti * 128:(ti + 1) * 128], wv[:], True, True, reads=[ckvn, wv], writes=[pv])
                    if ti % 2 == 0:
                        P.V("dve", "tensor_copy", Va[:, kt, 0:64], pv[:, 0:64], reads=[pv], writes=[Va])
                    else:
                        P.act(Va[:, kt, 0:64], pv[:, 0:64], AF.Copy, reads=[pv], writes=[Va])
                for (SRC, mmx) in ((QT, qm), (KT, km)):
                    P.act(sqq[:, 0:w], SRC[:, c0:c0 + w], AF.Square, reads=[SRC], writes=[sqq])
                    pn = psb[0]
                    P.mm(pn[:, 0:w], ones_bf[0:96, :], sqq[:, 0:w], True, True, reads=[ones_bf, sqq], writes=[pn])
                    P.V("dve", "reduce_max", mx[:], pn[:, 0:w], AX.X, reads=[pn], writes=[mx])
                    P.V("dve", "tensor_max", mmx[:], mmx[:], mx[:], reads=[mmx, mx], writes=[mmx])
        P.V("dve", "tensor_tensor", negM[:], qm[:], km[:], ALU.mult, reads=[qm, km], writes=[negM])
        P.act(negM[:], negM[:], AF.Sqrt, reads=[negM], writes=[negM])
        P.V("dve", "tensor_scalar_mul", negM[:], negM[:], -SM_SCALE_F, reads=[negM], writes=[negM])
        with Scope(P) as C_:
            PT = [C_.sb("ml_pt%d" % i, [128, 512], BF16) for i in range(3)]
            osb = [C_.sb("ml_osb%d" % i, [65, 512]) for i in range(2)]
            rden = [C_.sb("ml_rden%d" % i, [64, 512]) for i in range(2)]
            yo = [C_.sb("ml_yo%d" % i, [64, 512]) for i in range(2)]
            qblocks = []
            if need_ctx_q:
                qblocks.append((0, NCTX, list(range(NCTX // 128))))
            for (c0, w, isc) in blocks[1:]:
                qblocks.append((c0, w, list(range(NTILE))))
            items = []
            for qi, (c0, w, kts) in enumerate(qblocks):
                for ki, kt in enumerate(kts):
                    items.append((qi, c0, w, ki, kt, len(kts)))
            AHEAD = 2

            def issue_S(n):
                qi, c0, w, ki, kt, nk = items[n]
                ps_ = psb[n % 3]
                P.mm(ps_[:, 0:w], KT[:, kt * 128:(kt + 1) * 128], QT[:, c0:c0 + w], True, True, reads=[KT, QT], writes=[ps_])

            for n in range(min(AHEAD, len(items))):
                issue_S(n)
            for n, (qi, c0, w, ki, kt, nk) in enumerate(items):
                if n + AHEAD < len(items):
                    issue_S(n + AHEAD)
                ps_ = psb[n % 3]
                pt = PT[n % 3]
                po = psb[6 + qi % 2]
                P.act(pt[:, 0:w], ps_[:, 0:w], AF.Exp, reads=[ps_, negM], writes=[pt], scale=SM_SCALE_F, bias=negM[:, 0:1])
                P.mm(po[0:65, 0:w], Va[:, kt, :], pt[:, 0:w], ki == 0, ki == nk - 1, reads=[Va, pt], writes=[po])
                if ki == nk - 1:
                    ob = osb[qi % 2]
                    P.V("dve", "tensor_copy", ob[:, 0:w], po[0:65, 0:w], reads=[po], writes=[ob])
                    pd = psb[3 + qi % 2]
                    P.mm(pd[0:64, 0:w], onesf[64:65, 0:64], ob[64:65, 0:w], True, True, reads=[onesf, ob], writes=[pd])
                    rd = rden[qi % 2]
                    P.V("dve", "reciprocal", rd[:, 0:w], pd[0:64, 0:w], reads=[pd], writes=[rd])
                    y_ = yo[qi % 2]
                    P.V("dve", "tensor_tensor", y_[:, 0:w], ob[0:64, 0:w], rd[:, 0:w], ALU.mult, reads=[ob, rd], writes=[y_])
                    for (dst, off, ww) in io["ypieces"]("yd", c0, w):
                        P.dma("sp", dst, y_[:, off:off + ww], reads=[y_], is_output=True)


def build_mixer(layer_has_ctx_out, which=("conv", "lru", "rwkv", "mla")):
    nc = bass.Bass("TRN2", target_bir_lowering=False)
    P = new_prog(nc)
    io = {}

    def din(name, shape):
        io[name] = nc.dram_tensor(name, list(shape), F32, kind="ExternalInput").ap()

    def dout(name, shape):
        io[name] = nc.dram_tensor(name, list(shape), F32, kind="ExternalOutput").ap()

    if "conv" in which:
        din("ua_lat", [512, 2048 + 30]); din("ua_ctx", [512, 64 + 30])
        din("cv_w", [128, 2, CONVK]); din("cv_b", [128, 2]); din("cv_lng", [128, 2]); din("cv_lnb", [128, 2])
        dout("ya", [256, 2112])
    if "lru" in which:
        din("ub", [128, TT])
        din("lru_cw", [128, 4]); din("lru_cb", [128, 1]); din("lru_wa", [64, 128]); din("lru_wx", [64, 128])
        din("lru_ba", [128, 1]); din("lru_bx", [128, 1]); din("lru_lam", [128, 1])
        dout("yb", [64, TT])
    if "mla" in which:
        din("ud", [416, TT])
        din("mla_wuq", [256, 96]); din("mla_wkn", [128, 64]); din("mla_wv", [128, 64])
        din("mla_qn", [128, 2]); din("mla_kvn", [128, 1])
        din("rope_R", [96, 96]); din("rope_C", [96, 8192]); din("rope_S", [96, 8192])
        dout("yd", [64, TT])
    if "rwkv" in which:
        rwkv_io(din, dout)
    io["ypieces"] = lambda name, c0, w: [(io[name][:, c0:c0 + w], 0, w)]
    emit_mixers(P, nc, io, layer_has_ctx_out, which)
    P.emit()
    return nc


def emit_mixers(P, nc, io, need_ctx_q, which=("conv", "lru", "rwkv", "mla")):
    with Scope(P) as S:
        for nm, fn in (("conv", emit_conv), ("lru", emit_lru), ("mla", emit_mla)):
            if nm in which:
                with Scope(P) as PS_:
                    psb = [PS_.ps("psb%d" % i, [128, 512]) for i in range(8)]
                    if nm == "mla":
                        fn(P, S, nc, io, psb, need_ctx_q)
                    else:
                        fn(P, S, nc, io, psb)
        if "rwkv" in which:
            emit_rwkv(P, S, nc, io)


def rwkv_io(din, dout):
    din("uc", [512, TT])
    din("rw_mup", [128, 4]); din("rw_mun", [128, 4])
    din("rw_wup", [128, 2, 64])
    din("rw_aup", [128, 2, 64])
    din("rw_gup", [128, 64])
    din("rw_w0b", [128, 2, 64]); din("rw_a0b", [128, 2, 64])
    din("rw_kkb", [128, 64]); din("rw_kab", [128, 64]); din("rw_rkb", [128, 64])
    din("rw_gng", [128, 64]); din("rw_gnb", [128, 64])
    din("rw_masks", [128, 4, 128])
    dout("yc", [64, TT])


RWKV_STOP = 999


def emit_rwkv(P, S, nc, io):
    uc = io["uc"]
    ucv = uc.rearrange("(c p) t -> p c t", p=128)
    with Scope(P) as A:
        def load(name, shape, src=None):
            t = A.sb(name, shape)
            P.dma("sp", t[:], io[name] if src is None else src, writes=[t])
            return t
        mup = load("rw_mup", [128, 4]); mun = load("rw_mun", [128, 4])
        wup = load("rw_wup", [128, 2, 64]); aup = load("rw_aup", [128, 2, 64]); gup = load("rw_gup", [128, 64])
        w0b = load("rw_w0b", [128, 2, 64]); a0b = load("rw_a0b", [128, 2, 64])
        kkb = load("rw_kkb", [128, 64]); kab = load("rw_kab", [128, 64]); rkb = load("rw_rkb", [128, 64])
        gng = load("rw_gng", [128, 64]); gnb = load("rw_gnb", [128, 64])
        masks = load("rw_masks", [128, 4, 128])
        UPs, LOs, UPi, LOi = 0, 1, 2, 3
        ident = make_ident(P, A, "rw_id")
        omm = A.sb("rw_omm", [128, 4])
        P.V("dve", "tensor_tensor", omm[:], mup[:], mun[:], ALU.add, reads=[mup, mun], writes=[omm])
        P.V("dve", "tensor_scalar", omm[:], omm[:], -1.0, 1.0, ALU.mult, ALU.add, reads=[omm], writes=[omm])
        omka = A.sb("rw_omka", [128, 64])
        P.V("dve", "tensor_scalar", omka[:], kab[:], -1.0, 1.0, ALU.mult, ALU.add, reads=[kab], writes=[omka])
        ones1 = A.sb("rw_ones1", [128, 1])
        P.V("dve", "memset", ones1[:], 1.0, writes=[ones1])
        mask4 = []
        for d in range(2):
            m4 = A.sb("rw_mask4_%d" % d, [128, 4, 128])
            sbi = UPs if d == 0 else LOs
            ibi = UPi if d == 0 else LOi
            for q, mi in enumerate((sbi, ibi, sbi, ibi)):
                P.V("dve", "tensor_copy", m4[:, q, :], masks[:, mi, :], reads=[masks], writes=[m4])
            mask4.append(m4)
        Ysum = A.sb("rw_Ysum", [128, NTILE, 64]); Bsum = A.sb("rw_Bsum", [128, NTILE, 64]); Gall = A.sb("rw_Gall", [128, NTILE, 64])
        P.V("pool", "memset", Ysum[:], 0.0, writes=[Ysum]); P.V("pool", "memset", Bsum[:], 0.0, writes=[Bsum])
        Hs = [[A.sb("rw_H%d_%d" % (d, i), [128, 64]) for i in range(2)] for d in range(2)]
        for d in range(2):
            for i in range(2):
                P.V("dve", "memset", Hs[d][i][:], 0.0, writes=[Hs[d][i]])

        class WS:
            pass
        PSd = []
        for d in range(2):
            ps = WS()
            bk = [A.ps("rwp_bank%d_%d" % (d, i), [128, 512]) for i in range(4)]
            ps.a = bk[0].view("rwp_a", bk[0].t[:, 0:256]); ps.b = bk[0].view("rwp_b", bk[0].t[:, 256:384]); ps.c = bk[0].view("rwp_c", bk[0].t[:, 384:512])
            ps.d = bk[1]; ps.e = bk[2]
            ps.g = bk[3].view("rwp_g", bk[3].t[:, 0:256]); ps.h = bk[3].view("rwp_h", bk[3].t[:, 256:384]); ps.f = bk[3].view("rwp_f", bk[3].t[:, 384:512])
            PSd.append(ps)
        WSd = [[None, None], [None, None]]
        for d in range(2):
            for s_ in range(2):
                w = WS()
                n = "%d%d" % (d, s_)
                w.uin = A.sb("rw_uin" + n, [128, 4, 130]); w.usc = [A.sb("rw_us%d_" % c_ + n, [128, 128]) for c_ in range(4)]
                w.rkv = A.sb("rw_rkv" + n, [128, 3, 64])
                w.kk = A.sb("rw_kk" + n, [128, 64]); w.rrk = A.sb("rw_rrk" + n, [128, 64]); w.sm = A.sb("rw_sm" + n, [128, 4])
                w.t = [A.sb("rw_t%d_" % i + n, [128, 64]) for i in range(13)]
                w.sm2 = A.sb("rw_sm2" + n, [128, 1])
                w.M3 = A.sb("rw_M3" + n, [128, 3, 128])
                w.AQin = A.sb("rw_AQin" + n, [128, 128])
                w.F4 = A.sb("rw_F4" + n, [64, 4, 128])
                w.GM = A.sb("rw_GM" + n, [128, 4, 128])
                w.L = A.sb("rw_L" + n, [128, 128])
                w.XX = [A.sb("rw_XX%d_" % i + n, [128, 2, 128]) for i in range(2)]
                w.W = [A.sb("rw_W%d_" % i + n, [128, 128]) for i in range(2)]
                w.AQ = A.sb("rw_AQ" + n, [128, 128])
                w.RhT = A.sb("rw_RhT" + n, [128, 128]); w.GT = A.sb("rw_GT" + n, [64, 64]); w.F0 = A.sb("rw_F0" + n, [64, 64])
                w.gC = A.sb("rw_gC" + n, [64, 1])
                P.V("pool", "memset", w.RhT[:], 0.0, writes=[w.RhT])
                P.V("pool", "memset", w.M3[:], 0.0, writes=[w.M3])
                P.V("pool", "memset", w.AQin[:], 0.0, writes=[w.AQin])
                WSd[d][s_] = w

        def visit(d, vi, tile):
            w = WSd[d][vi % 2]
            ps = PSd[d]
            c0 = tile * 128
            H0 = Hs[d][vi % 2]; H1 = Hs[d][(vi + 1) % 2]
            vprev = c0 not in (0, NCTX)
            vnext = (c0 + 128) not in (NCTX, TT)
            lo = c0 - 1 if vprev else c0
            hi = c0 + 129 if vnext else c0 + 128
            if not vprev:
                P.V("pool", "memset", w.uin[:, :, 0:1], 0.0, writes=[w.uin])
            if not vnext:
                P.V("pool", "memset", w.uin[:, :, 129:130], 0.0, writes=[w.uin])
            P.dma("sp", w.uin[:, :, lo - (c0 - 1):hi - (c0 - 1)], ucv[:, :, lo:hi], writes=[w.uin])
            for c in range(4):
                P.act(w.usc[c][:], w.uin[:, c, 1:129], AF.Identity, reads=[w.uin, omm], writes=[w.usc[c]], scale=omm[:, c:c + 1])
            yield
            for c in range(4):
                P.V("dve", "scalar_tensor_tensor", w.usc[c][:], w.uin[:, c, 0:128], mup[:, c:c + 1], w.usc[c][:], ALU.mult, ALU.add,
                    reads=[w.uin, mup, w.usc[c]], writes=[w.usc[c]])
            for c in range(4):
                P.V("dve", "scalar_tensor_tensor", w.usc[c][:], w.uin[:, c, 2:130], mun[:, c:c + 1], w.usc[c][:], ALU.mult, ALU.add,
                    reads=[w.uin, mun, w.usc[c]], writes=[w.usc[c]])
            yield
            P.tr(ps.a[:, 0:128], w.usc[0][:], ident[:], reads=[w.usc[0], ident], writes=[ps.a])
            P.tr(ps.a[:, 128:256], w.usc[1][:], ident[:], reads=[w.usc[1], ident], writes=[ps.a])
            P.act(w.rkv[:, 0:2, :].rearrange("p a b -> p (a b)"), ps.a[:, 0:128], AF.Copy, reads=[ps.a], writes=[w.rkv])
            P.act(w.rkv[:, 2, :], ps.a[:, 192:256], AF.Copy, reads=[ps.a], writes=[w.rkv])
            rt = w.rkv[:, 0, :]; kt_ = w.rkv[:, 1, :]; vt = w.rkv[:, 2, :]
            P.act(w.usc[1][0:64, :], w.usc[1][0:64, :], AF.Tanh, reads=[w.usc[1]], writes=[w.usc[1]])
            if d == 0:
                P.act(w.usc[3][:], w.usc[3][:], AF.Sigmoid, reads=[w.usc[3]], writes=[w.usc[3]])
            P.mm(ps.b[:, 0:64], w.usc[1][0:64, :], wup[0:64, d, :], True, True, reads=[w.usc[1], wup], writes=[ps.b])
            P.mm(ps.b[:, 64:128], w.usc[2][0:64, :], aup[0:64, d, :], True, True, reads=[w.usc[2], aup], writes=[ps.b])
            yield
            zt, sg, lw, asig, t5, kd, bb_, ec, enc, ecx, t6, t7, t8 = w.t
            P.V("dve", "tensor_tensor", zt[:], ps.b[:, 0:64], w0b[:, d, :], ALU.add, reads=[ps.b, w0b], writes=[zt])
            P.V("dve", "tensor_tensor", asig[:], ps.b[:, 64:128], a0b[:, d, :], ALU.add, reads=[ps.b, a0b], writes=[asig])
            P.act(sg[:], zt[:], AF.Sigmoid, reads=[zt], writes=[sg])
            P.act(asig[:], asig[:], AF.Sigmoid, reads=[asig], writes=[asig])
            yield
            P.V("dve", "tensor_scalar_mul", lw[:], sg[:], -0.6065306597126334, reads=[sg], writes=[lw])
            IB = masks[:, UPi if d == 0 else LOi, :]
            P.mm(ps.c[:, 0:64], IB, lw[:], True, True, reads=[masks, lw], writes=[ps.c])
            P.mm(ps.c[0:64, 64:65], lw[:], ones1[:], True, True, reads=[lw, ones1], writes=[ps.c])
            if d == 0:
                P.mm(ps.f[:, 0:64], w.usc[3][:], gup[:], True, True, reads=[w.usc[3], gup], writes=[ps.f])
                P.V("dve", "tensor_copy", Gall[:, tile, :], ps.f[:, 0:64], reads=[ps.f], writes=[Gall])
            yield
            P.V("dve", "tensor_tensor", w.kk[:], kt_, kkb[:], ALU.mult, reads=[w.rkv, kkb], writes=[w.kk])
            P.act(t6[:], w.kk[:], AF.Square, reads=[w.kk], writes=[t6, w.sm], accum_out=w.sm[:, 0:1])
            yield
            P.V("dve", "tensor_scalar_max", w.sm[:, 0:1], w.sm[:, 0:1], 1e-24, reads=[w.sm], writes=[w.sm])
            P.act(w.sm[:, 0:1], w.sm[:, 0:1], AF.Sqrt, reads=[w.sm], writes=[w.sm])
            P.V("dve", "reciprocal", w.sm[:, 0:1], w.sm[:, 0:1], reads=[w.sm], writes=[w.sm])
            P.V("dve", "tensor_scalar_mul", w.kk[:], w.kk[:], w.sm[:, 0:1], reads=[w.kk, w.sm], writes=[w.kk])
            yield
            P.V("dve", "tensor_tensor", w.rrk[:], rt, rkb[:], ALU.mult, reads=[w.rkv, rkb], writes=[w.rrk])
            P.V("dve", "tensor_tensor", t5[:], asig[:], kab[:], ALU.mult, reads=[asig, kab, t5], writes=[t5])
            P.V("dve", "tensor_tensor", t5[:], t5[:], omka[:], ALU.add, reads=[t5, omka], writes=[t5])
            P.V("dve", "tensor_tensor", kd[:], t5[:], kt_, ALU.mult, reads=[t5, w.rkv], writes=[kd])
            yield
            P.V("dve", "tensor_tensor", bb_[:], w.kk[:], asig[:], ALU.mult, reads=[w.kk, asig], writes=[bb_])
            P.V("dve", "tensor_tensor", t7[:], w.rrk[:], kd[:], ALU.mult, reads=[w.rrk, kd], writes=[t7])
            P.V("dve", "reduce_sum", w.sm2[:, 0:1], t7[:], AX.X, reads=[t7], writes=[w.sm2])
            P.V("dve", "scalar_tensor_tensor", Bsum[:, tile, :], vt, w.sm2[:, 0:1], Bsum[:, tile, :], ALU.mult, ALU.add,
                reads=[w.rkv, w.sm2, Bsum], writes=[Bsum])
            yield
            P.act(ec[:], ps.c[:, 0:64], AF.Exp, reads=[ps.c], writes=[ec])
            P.act(enc[:], ps.c[:, 0:64], AF.Exp, reads=[ps.c], writes=[enc], scale=-1.0)
            P.V("dve", "tensor_tensor", ecx[:], ps.c[:, 0:64], lw[:], ALU.subtract, reads=[ps.c, lw], writes=[ecx])
            P.act(ecx[:], ecx[:], AF.Exp, reads=[ecx], writes=[ecx])
            P.act(w.gC[:], ps.c[0:64, 64:65], AF.Exp, reads=[ps.c], writes=[w.gC])
            yield
            P.V("dve", "tensor_tensor", w.M3[:, 0, 0:64], rt, ec[:], ALU.mult, reads=[w.rkv, ec], writes=[w.M3])
            P.V("dve", "tensor_tensor", w.M3[:, 1, 0:64], bb_[:], enc[:], ALU.mult, reads=[bb_, enc], writes=[w.M3])
            P.V("dve", "tensor_tensor", w.M3[:, 2, 0:64], kd[:], enc[:], ALU.mult, reads=[kd, enc], writes=[w.M3])
            P.V("dve", "scalar_tensor_tensor", w.AQin[:, 0:64], w.kk[:], -1.0, ecx[:], ALU.mult, ALU.mult, reads=[w.kk, ecx], writes=[w.AQin])
            yield
            P.mm(ps.d[0:64, 0:128], w.AQin[:, 0:64], ident[:], True, True, reads=[w.AQin, ident], writes=[ps.d])
            P.mm(ps.d[0:64, 128:256], w.M3[:, 0, 0:64], ident[:], True, True, reads=[w.M3, ident], writes=[ps.d])
            P.mm(ps.d[0:64, 256:384], w.M3[:, 1, 0:64], ident[:], True, True, reads=[w.M3, ident], writes=[ps.d])
            P.mm(ps.d[0:64, 384:512], w.M3[:, 2, 0:64], ident[:], True, True, reads=[w.M3, ident], writes=[ps.d])
            P.act(w.F4[:].rearrange("p a b -> p (a b)"), ps.d[0:64, :], AF.Copy, reads=[ps.d], writes=[w.F4])
            yield
            AT = w.F4[:, 0, :]; RT = w.F4[:, 1, :]; BT = w.F4[:, 2, :]; KT_ = w.F4[:, 3, :]
            AR = w.F4[:, 0:2, :].rearrange("p a b -> p (a b)")
            P.mm(ps.e[:, 0:256], BT, AR, True, True, reads=[w.F4], writes=[ps.e])
            P.mm(ps.e[:, 256:512], KT_, AR, True, True, reads=[w.F4], writes=[ps.e])
            P.mm(ps.f[:, 0:128], AT, BT, True, True, reads=[w.F4], writes=[ps.f])
            yield
            P.V("dve", "tensor_tensor", w.GM[:].rearrange("p a b -> p (a b)"), ps.e[:, :], mask4[d][:].rearrange("p a b -> p (a b)"), ALU.mult,
                reads=[ps.e, mask4[d]], writes=[w.GM])
            SBT = masks[:, LOs if d == 0 else UPs, :]
            P.V("dve", "tensor_tensor", w.L[:], ps.f[:, 0:128], SBT, ALU.mult, reads=[ps.f, masks], writes=[w.L])
            LT = w.GM[:, 0, :]; MrbT = w.GM[:, 1, :]; LakT = w.GM[:, 2, :]; MrkT = w.GM[:, 3, :]
            P.V("dve", "tensor_tensor", w.W[0][:], LT, ident[:], ALU.add, reads=[w.GM, ident], writes=[w.W[0]])
            yield
            P.mm(ps.a[:, 128:192], LakT, vt, True, True, reads=[w.GM, w.rkv], writes=[ps.a])
            P.act(w.AQin[:, 64:128], ps.a[:, 128:192], AF.Copy, reads=[ps.a], writes=[w.AQin])
            X = w.L[:]; XT = LT
            wi = 0
            NIT = 6
            for it in range(NIT):
                xx = w.XX[it % 2]
                P.mm(ps.g[:, 0:128], XT, X, True, True, reads=[w.GM, w.L, w.XX[(it + 1) % 2]], writes=[ps.g])
                if it < NIT - 1:
                    P.mm(ps.g[:, 128:256], X, XT, True, True, reads=[w.GM, w.L, w.XX[(it + 1) % 2]], writes=[ps.g])
                    P.act(xx[:].rearrange("p a b -> p (a b)"), ps.g[:, :], AF.Copy, reads=[ps.g], writes=[xx])
                else:
                    P.act(xx[:, 0, :], ps.g[:, 0:128], AF.Copy, reads=[ps.g], writes=[xx])
                X = xx[:, 0, :]; XT = xx[:, 1, :]
                P.mm(ps.h[:, 0:128], X, w.W[wi][:], True, True, reads=[xx, w.W[wi]], writes=[ps.h])
                P.V("dve", "tensor_tensor", w.W[1 - wi][:], ps.h[:, 0:128], w.W[wi][:], ALU.add, reads=[ps.h, w.W[wi]], writes=[w.W[1 - wi]])
                wi = 1 - wi
                yield
            Wf = w.W[wi]
            P.mm(ps.a[:, 0:128], Wf[:], w.AQin[:], True, True, reads=[Wf, w.AQin], writes=[ps.a])
            P.act(w.AQ[:], ps.a[:, 0:128], AF.Copy, reads=[ps.a], writes=[w.AQ])
            Abar = w.AQ[:, 0:64]; Qm = w.AQ[:, 64:128]
            yield
            P.mm(ps.f[0:64, 0:128], Abar, MrbT, True, True, reads=[w.AQ, w.GM], writes=[ps.f])
            P.V("dve", "tensor_tensor", w.RhT[0:64, :], ps.f[0:64, 0:128], RT, ALU.add, reads=[ps.f, w.F4], writes=[w.RhT])
            yield
            P.mm(ps.b[0:64, 0:64], Abar, w.M3[:, 1, 0:64], True, True, reads=[w.AQ, w.M3], writes=[ps.b])
            P.mm(ps.b[0:64, 64:128], w.M3[:, 1, 0:64], Qm, True, False, reads=[w.AQ, w.M3], writes=[ps.b])
            P.mm(ps.b[0:64, 64:128], w.M3[:, 2, 0:64], vt, False, True, reads=[w.M3, w.rkv], writes=[ps.b])
            P.V("dve", "tensor_tensor", w.GT[:], ps.b[0:64, 0:64], ident[0:64, 0:64], ALU.add, reads=[ps.b, ident], writes=[w.GT])
            P.V("dve", "tensor_scalar_mul", w.F0[:], ps.b[0:64, 64:128], w.gC[:, 0:1], reads=[ps.b, w.gC], writes=[w.F0])
            yield
            P.mm(ps.a[:, 192:256], MrbT, Qm, True, False, reads=[w.GM, w.AQ], writes=[ps.a])
            P.mm(ps.a[:, 192:256], MrkT, vt, False, False, reads=[w.GM, w.rkv], writes=[ps.a])
            P.mm(ps.a[:, 192:256], w.RhT[:], H0[:], False, True, reads=[w.RhT, H0], writes=[ps.a])
            P.V("dve", "tensor_tensor", Ysum[:, tile, :], ps.a[:, 192:256], Ysum[:, tile, :], ALU.add, reads=[ps.a, Ysum], writes=[Ysum])
            P.mm(ps.c[0:64, 0:64], w.GT[:], H0[0:64, :], True, True, reads=[w.GT, H0], writes=[ps.c])
            P.V("dve", "scalar_tensor_tensor", H1[0:64, :], ps.c[0:64, 0:64], w.gC[:, 0:1], w.F0[:], ALU.mult, ALU.add,
                reads=[ps.c, w.gC, w.F0], writes=[H1])
            yield

        order = [list(range(NTILE)), [1, 0] + list(range(NTILE - 1, 1, -1))]
        import os
        for vi in range(int(os.environ.get('RWKV_NV', NTILE))):
            gens = [visit(d, vi, order[d][vi]) for d in range(2)]
            alive = [True, True]
            nsteps = [0, 0]
            while any(alive):
                for d in range(2):
                    if alive[d]:
                        try:
                            next(gens[d])
                            nsteps[d] += 1
                            if nsteps[d] >= RWKV_STOP:
                                alive[d] = False
                        except StopIteration:
                            alive[d] = False
        for _ in range(int(os.environ.get("RWKV_JUNKPE", "0"))):
            P.tr(PSd[0].d[:, 0:128], WSd[0][0].AQin[:, :], ident[:], reads=[WSd[0][0].AQin, ident], writes=[PSd[0].d])
        junk = A.sb("rw_junk", [128, 64])
        for _ in range(int(os.environ.get("RWKV_JUNK", "0"))):
            P.V("dve", "memset", junk[:], 0.0, writes=[junk])
        fo = [A.sb("rw_fo%d" % i, [128, 64]) for i in range(2)]
        fof = [A.sb("rw_fof%d" % i, [64, 128]) for i in range(2)]
        fs = [A.sb("rw_fs%d" % i, [128, 4]) for i in range(2)]
        ft = [A.sb("rw_ft%d" % i, [128, 64]) for i in range(2)]
        epsg = A.sb("rw_epsg", [128, 1])
        P.V("dve", "memset", epsg[:], GN_EPS_F, writes=[epsg])
        for tile in range(NTILE):
            o = fo[tile % 2]; s_ = fs[tile % 2]; t_ = ft[tile % 2]
            P.V("dve", "reduce_sum", s_[:, 0:1], Ysum[:, tile, :], AX.X, reads=[Ysum], writes=[s_])
            P.V("dve", "tensor_scalar_mul", s_[:, 0:1], s_[:, 0:1], 1.0 / 64.0, reads=[s_], writes=[s_])
            P.V("dve", "tensor_scalar", o[:], Ysum[:, tile, :], s_[:, 0:1], None, ALU.subtract, reads=[Ysum, s_], writes=[o])
            P.act(t_[:], o[:], AF.Square, reads=[o], writes=[t_, s_], accum_out=s_[:, 1:2])
            P.act(s_[:, 1:2], s_[:, 1:2], AF.Sqrt, reads=[s_, epsg], writes=[s_], scale=1.0 / 64.0, bias=epsg[:, 0:1])
            P.V("dve", "reciprocal", s_[:, 1:2], s_[:, 1:2], reads=[s_], writes=[s_])
            P.V("dve", "scalar_tensor_tensor", o[:], o[:], s_[:, 1:2], gng[:], ALU.mult, ALU.mult, reads=[o, s_, gng], writes=[o])
            P.V("dve", "tensor_tensor", o[:], o[:], gnb[:], ALU.add, reads=[o, gnb], writes=[o])
            P.V("dve", "tensor_tensor", o[:], o[:], Bsum[:, tile, :], ALU.add, reads=[o, Bsum], writes=[o])
            P.V("dve", "tensor_tensor", o[:], o[:], Gall[:, tile, :], ALU.mult, reads=[o, Gall], writes=[o])
            pT = PSd[tile % 2].d
            P.mm(pT[0:64, 0:128], o[:], ident[:], True, True, reads=[o, ident], writes=[pT])
            of = fof[tile % 2]
            P.act(of[:], pT[0:64, 0:128], AF.Copy, reads=[pT], writes=[of])
            for (dst, off, ww) in io["ypieces"]("yc", tile * 128, 128):
                P.dma("sp", dst, of[:, off:off + ww], reads=[of], is_output=True)


def _fm(v, n):
    return np.ascontiguousarray(np.asarray(v, np.float32).reshape(n, 128).T)


def rope_tables():
    t = np.arange(8192, dtype=np.int32)
    rows = (t // 64).astype(np.float32)
    cols = (t % 64).astype(np.float32)
    n_freq = 8
    inv_freq = (np.float32(10000.0) ** (-np.arange(n_freq, dtype=np.float32) / np.float32(n_freq))).astype(np.float32)
    ang = np.stack([rows[:, None] * inv_freq, cols[:, None] * inv_freq], axis=1)
    ang = np.concatenate([ang, ang], axis=-1).reshape(8192, 32).astype(np.float32)
    C = np.ones((96, 8192), np.float32)
    S_ = np.zeros((96, 8192), np.float32)
    C[64:96] = np.cos(ang).T
    S_[64:96] = np.sin(ang).T
    R = np.zeros((96, 96), np.float32)
    for a in range(2):
        for f in range(8):
            i0 = 64 + a * 16 + f
            i1 = 64 + a * 16 + 8 + f
            R[i1, i0] = -1.0
            R[i0, i1] = 1.0
    return np.ascontiguousarray(C), np.ascontiguousarray(S_), R


def rwkv_masks():
    i = np.arange(128)[:, None]
    t = np.arange(128)[None, :]
    m = np.stack([(i < t), (i > t), (i <= t), (i >= t)], axis=1).astype(np.float32)
    return np.ascontiguousarray(m)


_CONSTS = {}


def consts():
    if not _CONSTS:
        C, S_, R = rope_tables()
        _CONSTS.update(rope_C=C, rope_S=S_, rope_R=R, rw_masks=rwkv_masks())
    return _CONSTS


def mixer_inputs(inp, l, j, uT, which=("conv", "lru", "rwkv", "mla")):
    G = 256
    m = {}
    rep = lambda v: np.ascontiguousarray(np.broadcast_to(np.asarray(v, np.float32)[None, :], (128, len(v))))
    if "conv" in which:
        lat = uT[0:512, NCTX:]
        ctx = uT[0:512, :NCTX]
        pl = np.zeros((512, 8192 + 30), np.float32); pl[:, 15:15 + 8192] = lat
        pc = np.zeros((512, 256 + 30), np.float32); pc[:, 15:15 + 256] = ctx
        m["ua_lat"] = np.ascontiguousarray(pl[:, 2048 * j:2048 * j + 2048 + 30])
        m["ua_ctx"] = np.ascontiguousarray(pc[:, 64 * j:64 * j + 64 + 30])
        w = np.asarray(inp["cv_dw_w"][l], np.float32)
        m["cv_w"] = np.ascontiguousarray(w.T.reshape(2, 128, CONVK).transpose(1, 0, 2))
        m["cv_b"] = _fm(inp["cv_dw_b"][l], 2); m["cv_lng"] = _fm(inp["cv_ln_g"][l], 2); m["cv_lnb"] = _fm(inp["cv_ln_b"][l], 2)
    hs = slice(64 * j, 64 * j + 64)
    if "lru" in which:
        m["ub"] = np.ascontiguousarray(np.concatenate([uT[512 + 64 * j:512 + 64 * j + 64], uT[768 + 64 * j:768 + 64 * j + 64]], 0))
        cw = np.asarray(inp["lru_conv_w"][l], np.float32)[:, hs].T
        m["lru_cw"] = np.ascontiguousarray(np.concatenate([cw, cw], 0))
        cb = np.asarray(inp["lru_conv_b"][l], np.float32)[hs]
        m["lru_cb"] = np.ascontiguousarray(np.concatenate([cb, cb])[:, None])
        m["lru_wa"] = np.ascontiguousarray(np.concatenate([inp["lru_wa"][l, 0, j], inp["lru_wa"][l, 1, j]], 1).astype(np.float32))
        m["lru_wx"] = np.ascontiguousarray(np.concatenate([inp["lru_wx"][l, 0, j], inp["lru_wx"][l, 1, j]], 1).astype(np.float32))
        for nm, src in (("lru_ba", "lru_ba"), ("lru_bx", "lru_bx"), ("lru_lam", "lru_lambda")):
            v = np.asarray(inp[src][l], np.float32)
            m[nm] = np.ascontiguousarray(np.concatenate([v[0, hs], v[1, hs]])[:, None])
    if "rwkv" in which:
        base = 1024
        rows = np.concatenate([np.arange(base + 64 * j, base + 64 * j + 64), np.arange(base + 256 + 64 * j, base + 256 + 64 * j + 64),
                               np.arange(base + 768, base + 832), np.arange(base + 512 + 64 * j, base + 512 + 64 * j + 64),
                               np.arange(base + 832, base + 896)])
        ucm = np.zeros((512, TT), np.float32)
        ucm[0:320] = uT[rows]
        ucm[384:512] = uT[base + 896:base + 1024]
        m["uc"] = ucm
        cidx = np.concatenate([rows - base, -np.ones(64, np.int64), np.arange(896, 1024)])
        for nm, src in (("rw_mup", "rwkv_mu_prev"), ("rw_mun", "rwkv_mu_next")):
            v = np.asarray(inp[src][l], np.float32)
            full = np.where(cidx >= 0, v[np.maximum(cidx, 0)], 0.0).astype(np.float32)
            m[nm] = _fm(full, 4)
        wup = np.zeros((128, 2, 64), np.float32); aup = np.zeros((128, 2, 64), np.float32)
        for d in range(2):
            wup[0:64, d] = inp["rwkv_w_up"][l, d][:, hs]
            aup[0:64, d] = inp["rwkv_a_up"][l, d][:, hs]
        m["rw_wup"] = wup; m["rw_aup"] = aup
        m["rw_gup"] = np.ascontiguousarray(np.asarray(inp["rwkv_g_up"][l], np.float32)[:, hs])
        m["rw_w0b"] = np.ascontiguousarray(np.stack([rep(inp["rwkv_w0"][l, d, hs]) for d in range(2)], 1))
        m["rw_a0b"] = np.ascontiguousarray(np.stack([rep(inp["rwkv_a0"][l, d, hs]) for d in range(2)], 1))
        m["rw_kkb"] = rep(inp["rwkv_k_k"][l, hs]); m["rw_kab"] = rep(inp["rwkv_k_a"][l, hs]); m["rw_rkb"] = rep(inp["rwkv_r_k"][l, j])
        m["rw_gng"] = rep(inp["rwkv_gn_g"][l, hs]); m["rw_gnb"] = rep(inp["rwkv_gn_b"][l, hs])
        m["rw_masks"] = consts()["rw_masks"]
    if "mla" in which:
        m["ud"] = np.ascontiguousarray(uT[2048:2464])
        m["mla_wuq"] = np.ascontiguousarray(np.asarray(inp["mla_w_uq"][l], np.float32)[:, 96 * j:96 * j + 96])
        wkv = np.asarray(inp["mla_w_ukv"][l], np.float32)
        m["mla_wkn"] = np.ascontiguousarray(wkv[:, 128 * j:128 * j + 64]); m["mla_wv"] = np.ascontiguousarray(wkv[:, 128 * j + 64:128 * j + 128])
        m["mla_qn"] = _fm(inp["mla_q_norm"][l], 2); m["mla_kvn"] = _fm(inp["mla_kv_norm"][l], 1)
        for k in ("rope_R", "rope_C", "rope_S"):
            m[k] = consts()[k]
    return m


U32 = mybir.dt.uint32
CHU = 122
NCHU = 16
CHY = 96
NCHY = 8


def ugrow(R, q):
    return (R // CHU) * (4 * CHU) + q * CHU + (R % CHU)


def ygrow(R, q):
    return (R // CHY) * (4 * CHY) + q * CHY + (R % CHY)
NIDX = 26
GROUPS = [[0, 1, 2, 3], [4, 5, 6, 7]]

MIXER_CONSTS = [
    ("cv_w", [128, 2, CONVK]), ("cv_b", [128, 2]), ("cv_lng", [128, 2]), ("cv_lnb", [128, 2]),
    ("lru_cw", [128, 4]), ("lru_cb", [128, 1]), ("lru_wa", [64, 128]), ("lru_wx", [64, 128]),
    ("lru_ba", [128, 1]), ("lru_bx", [128, 1]), ("lru_lam", [128, 1]),
    ("mla_wuq", [256, 96]), ("mla_wkn", [128, 64]), ("mla_wv", [128, 64]), ("mla_qn", [128, 2]), ("mla_kvn", [128, 1]),
    ("rw_mup", [128, 4]), ("rw_mun", [128, 4]), ("rw_wup", [128, 2, 64]), ("rw_aup", [128, 2, 64]), ("rw_gup", [128, 64]),
    ("rw_w0b", [128, 2, 64]), ("rw_a0b", [128, 2, 64]), ("rw_kkb", [128, 64]), ("rw_kab", [128, 64]), ("rw_rkb", [128, 64]),
    ("rw_gng", [128, 64]), ("rw_gnb", [128, 64]),
]
SHARED_CONSTS = [("rope_R", [96, 96]), ("rope_C", [96, 8192]), ("rope_S", [96, 8192]), ("rw_masks", [128, 4, 128])]


def gather_indices(j):
    p = np.arange(128)
    lo = p < 64
    r = np.where(lo, p, p - 64)
    g = np.zeros((128, NIDX), np.uint32)
    for q in range(4):
        g[:, q] = np.where(lo, ugrow(64 * j + r, q), ugrow(256 + 64 * j + r, q))
        g[:, 4 + q] = np.where(lo, ugrow(512 + 64 * j + r, q), ugrow(768 + 64 * j + r, q))
        g[:, 8 + q] = np.where(lo, ugrow(1280 + r, q), ugrow(1024 + 64 * j + r, q))
    for k in range(4):
        g[:, 12 + k] = ((j - 1) if j > 0 else 4) * 512 + 128 * k + p
        g[:, 16 + k] = ((j + 1) if j < 3 else 4) * 512 + 128 * k + p
    for m in range(3):
        for h in range(2):
            g[:, 20 + 2 * m + h] = np.where(lo, ygrow(j * 192 + m * 64 + r, 2 * h), ygrow(j * 192 + m * 64 + r, 2 * h + 1))
    return g


def emit_exchange_u(P, nc, G, l, u_l, X):
    it, zero, scr = X["it"], X["zero"], X["scr"]
    hs, hgx, ug = X["hs"], X["hgx"], X["ug"]
    ua_lat, ua_ctx, ub, uc, ud = X["ua_lat"], X["ua_ctx"], X["ub"], X["uc"], X["ud"]
    b_hs = Buf("b_hs"); b_hgx = Buf("b_hgx"); b_ug = Buf("b_ug"); b_dst = Buf("b_dst")
    for (d0, s0) in ((0, 0), (15, 2033), (30, 2048), (45, 2097)):
        P.dma("sp", hs[:, d0:d0 + 15], u_l[0:512, s0:s0 + 15], writes=[b_hs])
    for k in range(4):
        P.dma("sp", hgx[2048 + 128 * k:2048 + 128 * (k + 1), :], zero[:, 0:60], reads=[zero], writes=[b_hgx])
    P.cc("AllGather", [hs], [hgx[0:2048, :]], GROUPS, scr, reads=[b_hs], writes=[b_hgx])
    P.dma("sp", ua_lat[:, 15:2063], u_l[0:512, 0:2048], writes=[b_dst])
    P.dma("sp", ua_ctx[:, 15:79], u_l[0:512, 2048:2112], writes=[b_dst])
    with Scope(P) as A0:
        hl = [A0.sb("xh%d" % i, [128, 60]) for i in range(2)]
        n = 0
        for k in range(4):
            for side in range(2):
                t = hl[n % 2]
                n += 1
                col = 12 + 4 * side + k
                P.op("pool", lambda e, t=t, col=col: e.indirect_dma_start(out=t[:], out_offset=None, in_=hgx,
                     in_offset=bass.IndirectOffsetOnAxis(ap=it[:, col:col + 1], axis=0)), reads=[b_hgx, it], writes=[t], dma=True)
                rs = slice(128 * k, 128 * (k + 1))
                if side == 0:
                    P.dma("sp", ua_lat[rs, 0:15], t[:, 15:30], reads=[t], writes=[b_dst])
                    P.dma("sp", ua_ctx[rs, 0:15], t[:, 45:60], reads=[t], writes=[b_dst])
                else:
                    P.dma("sp", ua_lat[rs, 2063:2078], t[:, 0:15], reads=[t], writes=[b_dst])
                    P.dma("sp", ua_ctx[rs, 79:94], t[:, 30:45], reads=[t], writes=[b_dst])
    P.barrier()
    if X.get("conv_cb") is not None:
        X["conv_cb"]()
    for c in range(NCHU):
        P.cc("AllGather", [u_l[512 + CHU * c:512 + CHU * (c + 1), :]], [ug[4 * CHU * c:4 * CHU * (c + 1), :]], GROUPS, scr, writes=[b_ug])
    with Scope(P) as A:
        gt = [A.sb("xg%d" % i, [128, 2112]) for i in range(2)]
        n = 0
        for q in range(4):
            lat = slice(NCTX + 2048 * q, NCTX + 2048 * (q + 1))
            ctx = slice(64 * q, 64 * (q + 1))
            for (dst, r0, col) in ((ub, 0, q), (uc, 0, 4 + q), (uc, 128, 8 + q)):
                t = gt[n % 2]
                n += 1
                P.op("pool", lambda e, t=t, col=col: e.indirect_dma_start(out=t[:], out_offset=None, in_=ug,
                     in_offset=bass.IndirectOffsetOnAxis(ap=it[:, col:col + 1], axis=0)), reads=[b_ug, it], writes=[t], dma=True)
                P.dma("sp", dst[r0:r0 + 128, lat], t[:, 0:2048], reads=[t], writes=[b_dst])
                P.dma("sp", dst[r0:r0 + 128, ctx], t[:, 2048:2112], reads=[t], writes=[b_dst])
            for (dst, d0, R0, n_) in ((uc, 256, 1344, 64), (uc, 384, 1408, 128), (ud, 0, 1536, 416)):
                done = 0
                while done < n_:
                    R = R0 + done
                    cnt = min(n_ - done, CHU - R % CHU)
                    srow = ugrow(R, q)
                    P.dma("sp", dst[d0 + done:d0 + done + cnt, lat], ug[srow:srow + cnt, 0:2048], reads=[b_ug], writes=[b_dst])
                    P.dma("sp", dst[d0 + done:d0 + done + cnt, ctx], ug[srow:srow + cnt, 2048:2112], reads=[b_ug], writes=[b_dst])
                    done += cnt
            P.dma("sp", uc[320:384, 2112 * q:2112 * (q + 1)], zero[0:64, :], reads=[zero], writes=[b_dst])


def emit_exchange_y(P, nc, G, l, X):
    it, scr = X["it"], X["scr"]
    ys, yg, ya, yT = X["ys"], X["yg"], X["ya"], X["yT"]
    b_yg = Buf("b_yg"); b_dst = Buf("b_ydst")
    ysf = ys.rearrange("d r t -> (d r) t")
    for c in range(NCHY):
        P.cc("AllGather", [ysf[CHY * c:CHY * (c + 1), :]], [yg[4 * CHY * c:4 * CHY * (c + 1), :]], GROUPS, scr, writes=[b_yg])
    P.dma("sp", yT[0:256, :], ya, writes=[b_dst])
    with Scope(P) as A:
        gt = [A.sb("yg%d" % i, [128, 2112]) for i in range(2)]
        n = 0
        for m in range(3):
            for h in range(2):
                t = gt[n % 2]
                n += 1
                col = 20 + 2 * m + h
                P.op("pool", lambda e, t=t, col=col: e.indirect_dma_start(out=t[:], out_offset=None, in_=yg,
                     in_offset=bass.IndirectOffsetOnAxis(ap=it[:, col:col + 1], axis=0)), reads=[b_yg, it], writes=[t], dma=True)
                r0 = 256 * (m + 1) + 128 * h
                P.dma("sp", yT[r0:r0 + 128, :], t[:], reads=[t], writes=[b_dst])


def build_fused():
    nc = bass.Bass("TRN2", target_bir_lowering=False)
    P = new_prog(nc)

    def din(name, shape, dt=F32):
        return nc.dram_tensor(name, list(shape), dt, kind="ExternalInput").ap()

    def dint(name, shape):
        return nc.dram_tensor(name, list(shape), F32).ap()

    xT_in = din("xT", [D, 2112]); cT = din("cT", [128, KC, 2]); gidx = din("gidx", [128, NIDX], U32); fin_g = din("fin_g", [128, KC])
    xfin = nc.dram_tensor("xfin", [D, 2048], F32, kind="ExternalOutput").ap()
    W = []
    for l in range(2):
        w = {"ada_w": din("ada_w_%d" % l, [D, NMOD * D]), "ada_b": din("ada_b_%d" % l, [128, NMOD * KC]),
             "f1w13": din("f1w13_%d" % l, [D, 2 * DFF]), "f1w2": din("f1w2_%d" % l, [DFF, D]), "w_in": din("w_in_%d" % l, [D, INC]),
             "w_out": din("w_out_%d" % l, [D, D]), "f2w13": din("f2w13_%d" % l, [D, 2 * DFF]), "f2w2": din("f2w2_%d" % l, [DFF, D])}
        for (nm, shp) in MIXER_CONSTS:
            w[nm] = din("%s_%d" % (nm, l), shp)
        W.append(w)
    SH = {nm: din(nm, shp) for (nm, shp) in SHARED_CONSTS}
    X = {"hs": dint("hs", [512, 60]), "hgx": dint("hgx", [5 * 512, 60]), "ug": dint("ug", [NCHU * 4 * CHU, 2112]),
         "ua_lat": dint("ua_lat", [512, 2078]), "ua_ctx": dint("ua_ctx", [512, 94]), "ub": dint("ub", [128, TT]),
         "uc": dint("uc", [512, TT]), "ud": dint("ud", [416, TT]),
         "ys": dint("ys", [4, 192, 2112]), "yg": dint("yg", [NCHY * 4 * CHY, 2112]), "ya": dint("ya", [256, 2112]), "yT": dint("yT", [D, 2112])}
    ys = X["ys"]

    def ypieces(name, c0, w):
        m = {"yb": 0, "yc": 1, "yd": 2}[name]
        rows = slice(64 * m, 64 * (m + 1))
        if c0 >= NCTX:
            q, col = divmod(c0 - NCTX, 2048)
            assert col + w <= 2048
            return [(ys[q, rows, col:col + w], 0, w)]
        out = []
        for q in range(4):
            lo = max(c0, 64 * q); hi = min(c0 + w, 64 * (q + 1))
            if hi > lo:
                out.append((ys[q, rows, 2048 + lo - 64 * q:2048 + hi - 64 * q], lo - c0, hi - lo))
        return out

    with Scope(P) as G:
        it = G.sb("gidx_sb", [128, NIDX], U32)
        P.dma("sp", it[:], gidx, writes=[it])
        zero = G.sb("zero_sb", [128, 2112])
        P.V("pool", "memset", zero[:], 0.0, writes=[zero])
        scr = G.sb("cc_scr", [128, 1])
        X.update(it=it, zero=zero, scr=scr)
        xsrc = xT_in
        for l in range(2):
            last = l == 1
            w = W[l]
            x1 = dint("x1_%d" % l, [D, 2112]); u_l = dint("u_%d" % l, [INC, 2112]); mod_l = dint("mod_%d" % l, [128, NMOD * KC, 2])
            T = {"xT": xsrc, "w13": w["f1w13"], "w2": w["f1w2"], "xnew": x1, "cT": cT, "ada_w": w["ada_w"], "ada_b": w["ada_b"],
                 "w_in": w["w_in"], "modo": mod_l, "uT": u_l}
            emit_stage(P, nc, "P", 2048, 64, False, T)
            P.barrier()
            io = {nm: w[nm] for (nm, _) in MIXER_CONSTS}
            io.update(SH)
            io.update(ua_lat=X["ua_lat"], ua_ctx=X["ua_ctx"], ub=X["ub"], uc=X["uc"], ud=X["ud"], ya=X["ya"], ypieces=ypieces)
            X["conv_cb"] = lambda io=io, last=last: emit_mixers(P, nc, io, not last, which=("conv",))
            emit_exchange_u(P, nc, G, l, u_l, X)
            P.barrier()
            emit_mixers(P, nc, io, not last, which=("lru", "rwkv", "mla"))
            P.barrier()
            emit_exchange_y(P, nc, G, l, X)
            P.barrier()
            nctx = 0 if last else 64
            NTq = 2048 + nctx
            x2 = dint("x2_%d" % l, [D, NTq]); xmid = dint("xmid_%d" % l, [D, NTq])
            T = {"xT": x1[:, 0:NTq], "w13": w["f2w13"], "w2": w["f2w2"], "xnew": x2, "modi": mod_l, "yT": X["yT"][:, 0:NTq],
                 "w_out": w["w_out"], "xmid": xmid}
            if last:
                T.update(fin_g=fin_g, xfin=xfin)
            emit_stage(P, nc, "Q", 2048, nctx, last, T)
            P.barrier()
            xsrc = x2
        P.emit()
    return nc


def fused_inputs(inp, i):
    b, j = divmod(i, 4)
    x = np.asarray(inp["x"], np.float32); ctx = np.asarray(inp["ctx"], np.float32)
    m = {"xT": np.ascontiguousarray(np.concatenate([x[b, 2048 * j:2048 * (j + 1)], ctx[b, 64 * j:64 * (j + 1)]], 0).T),
         "cT": np.ascontiguousarray(np.stack([_fm(inp["c"][b], 8), _fm(inp["c_ctx"], 8)], -1)),
         "gidx": gather_indices(j), "fin_g": _fm(inp["final_norm"], KC)}
    dummy = np.zeros((INC, TT), np.float32)
    for l in range(2):
        m["ada_w_%d" % l] = np.ascontiguousarray(inp["ada_w"][l], np.float32)
        m["ada_b_%d" % l] = _fm(inp["ada_b"][l], NMOD * KC)
        m["f1w13_%d" % l] = np.ascontiguousarray(inp["ffn1_w13"][l], np.float32)
        m["f1w2_%d" % l] = np.ascontiguousarray(inp["ffn1_w2"][l], np.float32)
        m["w_in_%d" % l] = np.ascontiguousarray(inp["w_in"][l], np.float32)
        m["w_out_%d" % l] = np.ascontiguousarray(inp["w_out"][l], np.float32)
        m["f2w13_%d" % l] = np.ascontiguousarray(inp["ffn2_w13"][l], np.float32)
        m["f2w2_%d" % l] = np.ascontiguousarray(inp["ffn2_w2"][l], np.float32)
        mi = mixer_inputs(inp, l, j, dummy)
        for (nm, _) in MIXER_CONSTS:
            m["%s_%d" % (nm, l)] = mi[nm]
    for (nm, _) in SHARED_CONSTS:
        m[nm] = consts()[nm]
    return m


def kernel_fused(**inputs):
    inp = {k: np.asarray(v) for k, v in inputs.items()}
    nc = _prog("F", build_fused)
    maps = [fused_inputs(inp, i) for i in range(8)]
    res = _run(nc, maps)
    out = np.zeros((2, 8192, 1024), np.float32)
    for i in range(8):
        b, q = divmod(i, 4)
        out[b, 2048 * q:2048 * (q + 1)] = res[i]["xfin"].T
    return out


_PROGS = {}


def _prog(key, fn):
    if key not in _PROGS:
        _PROGS[key] = fn()
    return _PROGS[key]


def _run(nc, in_maps):
    res = run_bass_kernel_spmd(nc, in_maps, core_ids=list(range(len(in_maps))))
    return res.results


def kernel_unfused(**inputs):
    inp = {k: np.asarray(v) for k, v in inputs.items()}
    NCORE = 8
    x = inp["x"].astype(np.float32)
    ctx = inp["ctx"].astype(np.float32)
    L = 2
    xT = []
    for i in range(NCORE):
        b, q = divmod(i, 4)
        xT.append(np.ascontiguousarray(np.concatenate([x[b, 2048 * q:2048 * (q + 1)], ctx[b, 64 * q:64 * (q + 1)]], 0).T))
    cT = [np.ascontiguousarray(np.stack([_fm(inp["c"][i // 4], 8), _fm(inp["c_ctx"], 8)], -1)) for i in range(NCORE)]
    out = None
    for l in range(L):
        last = l == L - 1
        ncP = _prog("P", lambda: build_stage("P", 2048, 64))
        maps = [{"xT": xT[i], "cT": cT[i], "ada_w": np.ascontiguousarray(inp["ada_w"][l], np.float32),
                 "ada_b": _fm(inp["ada_b"][l], NMOD * KC), "w13": np.ascontiguousarray(inp["ffn1_w13"][l], np.float32),
                 "w2": np.ascontiguousarray(inp["ffn1_w2"][l], np.float32), "w_in": np.ascontiguousarray(inp["w_in"][l], np.float32)}
                for i in range(NCORE)]
        rP = _run(ncP, maps)
        x1T = [r["xnew"] for r in rP]
        mods = [r["modo"] for r in rP]
        uT_full = []
        for b in range(2):
            cs = [rP[4 * b + q]["uT"] for q in range(4)]
            uT_full.append(np.ascontiguousarray(np.concatenate([c_[:, 2048:] for c_ in cs] + [c_[:, :2048] for c_ in cs], 1)))
        ncM = _prog(("M", not last), lambda: build_mixer(not last))
        maps = [mixer_inputs(inp, l, i % 4, uT_full[i // 4]) for i in range(NCORE)]
        rM = _run(ncM, maps)
        yT = []
        for i in range(NCORE):
            b, q = divmod(i, 4)
            y = np.zeros((1024, 2112), np.float32)
            y[0:256] = rM[i]["ya"]
            cols = np.concatenate([np.arange(NCTX + 2048 * q, NCTX + 2048 * (q + 1)), np.arange(64 * q, 64 * (q + 1))])
            for j in range(4):
                r = rM[4 * b + j]
                y[256 + 64 * j:256 + 64 * (j + 1)] = r["yb"][:, cols]
                y[512 + 64 * j:512 + 64 * (j + 1)] = r["yc"][:, cols]
                y[768 + 64 * j:768 + 64 * (j + 1)] = r["yd"][:, cols]
            yT.append(y)
        nctx = 0 if last else 64
        ncQ = _prog(("Q", last), lambda: build_stage("Q", 2048, nctx, final=last))
        NTq = 2048 + nctx
        maps = []
        for i in range(NCORE):
            m = {"xT": np.ascontiguousarray(x1T[i][:, :NTq]), "yT": np.ascontiguousarray(yT[i][:, :NTq]), "modi": mods[i],
                 "w_out": np.ascontiguousarray(inp["w_out"][l], np.float32),
                 "w13": np.ascontiguousarray(inp["ffn2_w13"][l], np.float32), "w2": np.ascontiguousarray(inp["ffn2_w2"][l], np.float32)}
            if last:
                m["fin_g"] = _fm(inp["final_norm"], KC)
            maps.append(m)
        rQ = _run(ncQ, maps)
        if last:
            out = np.zeros((2, 8192, 1024), np.float32)
            for i in range(NCORE):
                b, q = divmod(i, 4)
                out[b, 2048 * q:2048 * (q + 1)] = rQ[i]["xfin"].T
        else:
            xT = [r["xnew"] for r in rQ]
    return out


def kernel(**inputs):
    return kernel_fused(**inputs)
```

```python
import numpy as np
import concourse.bass as bass
import concourse.mybir as mybir
from concourse.bass_utils import run_bass_kernel_spmd

F32 = mybir.dt.float32
BF16 = mybir.dt.bfloat16
AF = mybir.ActivationFunctionType
ALU = mybir.AluOpType
AX = mybir.AxisListType

ENGS = ("pe", "act", "dve", "pool", "sp")
DMA_K = 6


class Buf:
    __slots__ = ("name", "t", "last_w", "readers", "parent")

    def __init__(self, name, t=None, parent=None):
        self.name = name
        self.t = t
        self.parent = parent
        self.last_w = None
        self.readers = []

    def __getitem__(self, idx):
        return self.t[idx]

    def root(self):
        b = self
        while b.parent is not None:
            b = b.parent
        return b

    def view(self, name, t):
        return Buf(name, t, parent=self)


class Op:
    __slots__ = ("eng", "idx", "fn", "waits", "is_dma", "needed", "semval", "dma_n", "is_cc")

    def __init__(self, eng, idx, fn, is_dma):
        self.eng = eng
        self.idx = idx
        self.fn = fn
        self.waits = []
        self.is_dma = is_dma
        self.needed = False
        self.semval = None
        self.dma_n = None
        self.is_cc = False


class Prog:
    def __init__(self, nc, same_engine_sync=True):
        self.nc = nc
        self.ops = {e: [] for e in ENGS}
        self.seen = {e: {} for e in ENGS}
        self.dma_cnt = {e: 0 for e in ENGS}
        self.same_engine_sync = same_engine_sync
        self.out_dma_ops = []
        self._dma_list = {}
        self.cc_cnt = 0
        self.pending = {e: [] for e in ENGS}

    def _key(self, op):
        if op.is_cc:
            return ("cc",)
        if op.is_dma:
            return ("dma", op.eng, op.dma_n % DMA_K)
        return ("eng", op.eng)

    def _ord(self, op):
        if op.is_cc:
            return op.dma_n
        return op.dma_n // DMA_K if op.is_dma else op.idx

    def _dep(self, op, dep):
        if dep is None or dep is op:
            return
        if (not dep.is_dma) and (not dep.is_cc) and dep.eng == op.eng and not op.is_dma and not op.is_cc:
            if op.eng == "pe" or not self.same_engine_sync:
                return
        k = self._key(dep)
        o = self._ord(dep)
        if self.seen[op.eng].get(k, -1) >= o:
            return
        self.seen[op.eng][k] = o
        dep.needed = True
        op.waits.append(dep)

    def op(self, eng, fn, reads=(), writes=(), dma=False, cc=False):
        op = Op(eng, len(self.ops[eng]), fn, dma)
        if cc:
            op.is_cc = True
            op.dma_n = self.cc_cnt
            self.cc_cnt += 1
            op.needed = True
        reads = [b.root() for b in reads]
        writes = [b.root() for b in writes]
        if self.pending[eng]:
            for d_ in self.pending[eng]:
                self._dep(op, d_)
            self.pending[eng] = []
        if dma:
            op.dma_n = self.dma_cnt[eng]
            self.dma_cnt[eng] += 1
            op.needed = True
            if op.dma_n >= DMA_K:
                prev = self._dma_ops(eng)[op.dma_n - DMA_K]
                self._dep(op, prev)
        for b in reads:
            self._dep(op, b.last_w)
        for b in writes:
            self._dep(op, b.last_w)
            for r in b.readers:
                self._dep(op, r)
        for b in reads:
            b.readers.append(op)
        for b in writes:
            b.last_w = op
            b.readers = []
        self.ops[eng].append(op)
        if dma:
            self._dma_list.setdefault(eng, []).append(op)
        return op

    def _dma_ops(self, eng):
        return self._dma_list.setdefault(eng, [])

    def barrier(self):
        deps = []
        for e in ENGS:
            comp = [o for o in self.ops[e] if not o.is_dma and not o.is_cc]
            if comp:
                deps.append(comp[-1])
            lst = self._dma_ops(e)
            deps.extend(lst[-DMA_K:])
            ccs = [o for o in self.ops[e] if o.is_cc]
            if ccs:
                deps.append(ccs[-1])
        for e in ENGS:
            self.pending[e] = list(deps)

    def cc(self, kind, ins, outs, groups, scratch, reads=(), writes=()):
        ccb = Buf("ccdone")
        self.op("pool", lambda e: e.collective_compute(kind, ALU.bypass, replica_groups=groups, ins=ins, outs=outs),
                reads, [ccb], cc=True)
        return self.op("pool", lambda e: e.memset(scratch[:], 0.0), [ccb], list(writes) + [scratch])

    def dma(self, eng, out, in_, reads=(), writes=(), is_output=False, **kw):
        op = self.op(eng, lambda e: e.dma_start(out=out, in_=in_, **kw), reads, writes, dma=True)
        if is_output:
            self.out_dma_ops.append(op)
        return op

    def mm(self, out, lhsT, rhs, start, stop, reads=(), writes=(), **kw):
        nc = self.nc
        return self.op("pe", lambda e: e.matmul(out, lhsT, rhs, start=start, stop=stop, **kw), reads, writes)

    def tr(self, out, in_, ident, reads=(), writes=()):
        nc = self.nc
        return self.op("pe", lambda e: e.transpose(out, in_, ident), reads, writes)

    def act(self, out, in_, func, reads=(), writes=(), **kw):
        nc = self.nc
        return self.op("act", lambda e: e.activation(out=out, in_=in_, func=func, **kw), reads, writes)

    def V(self, eng, name, *args, reads=(), writes=(), **kw):
        return self.op(eng, lambda e: getattr(e, name)(*args, **kw), reads, writes)

    def emit(self):
        nc = self.nc
        for e in ENGS:
            c = 0
            for op in self.ops[e]:
                if op.is_cc:
                    op.semval = op.dma_n + 1
                elif op.is_dma:
                    op.semval = 16 * (op.dma_n // DMA_K + 1)
                elif op.needed:
                    c += 1
                    op.semval = c
        self.sem_counts = {e: sum(1 for op in self.ops[e] if (op.needed and not op.is_dma)) for e in ENGS}
        self.sem_counts.update({"dma_" + e: self.dma_cnt[e] for e in ENGS})
        import os
        if os.environ.get("PROG_DEBUG"):
            print("sem counts", self.sem_counts)
            print("instr counts", {e: (len(self.ops[e]), sum(len(o.waits) for o in self.ops[e])) for e in ENGS})
        import contextlib
        with contextlib.ExitStack() as st:
            esem = {e: st.enter_context(nc.semaphore("s_" + e)) for e in ENGS}
            dsem = {}
            for e in ENGS:
                if self.dma_cnt[e]:
                    for k in range(min(DMA_K, self.dma_cnt[e])):
                        dsem[(e, k)] = st.enter_context(nc.semaphore("d_%s%d" % (e, k)))
            ccsem = st.enter_context(nc.semaphore("s_cc")) if self.cc_cnt else None
            block = st.enter_context(nc.Block())

            def semof(op):
                if op.is_cc:
                    return ccsem
                if op.is_dma:
                    return dsem[(op.eng, op.dma_n % DMA_K)]
                return esem[op.eng]

            def run(e, engine):
                for op in self.ops[e]:
                    for d in op.waits:
                        engine.wait_ge(semof(d), d.semval)
                    ins = op.fn(engine)
                    if op.is_cc:
                        ins.then_inc(semof(op), 1)
                    elif op.is_dma:
                        ins.then_inc(semof(op), 16)
                    elif op.needed:
                        ins.then_inc(semof(op), 1)
                if e == "sp":
                    for d in self.out_dma_ops:
                        engine.wait_ge(semof(d), d.semval)
                    for q in ENGS:
                        lst = self._dma_ops(q)
                        for d in lst[-DMA_K:]:
                            engine.wait_ge(semof(d), d.semval)

            block.sync(lambda eng: run("sp", eng))
            block.tensor(lambda eng: run("pe", eng))
            block.scalar(lambda eng: run("act", eng))
            block.vector(lambda eng: run("dve", eng))
            block.gpsimd(lambda eng: run("pool", eng))


import contextlib


class Scope:
    def __init__(self, P, parent=None):
        self.P = P
        self.nc = P.nc
        self.st = contextlib.ExitStack()
        self.bufs = []

    def __enter__(self):
        self.st.__enter__()
        return self

    def __exit__(self, *a):
        for b in self.bufs:
            if b.last_w is not None:
                self.P.freed.append(b.last_w)
            self.P.freed.extend(b.readers)
        best = {}
        for op in self.P.freed:
            k = self.P._key(op)
            if k not in best or self.P._ord(op) > self.P._ord(best[k]):
                best[k] = op
        self.P.freed = list(best.values())
        return self.st.__exit__(*a)

    _ctr = [0]

    def _uniq(self, name):
        Scope._ctr[0] += 1
        return "%s_%d" % (name, Scope._ctr[0])

    def _new(self, name, t):
        b = Buf(name, t)
        b.readers = list(self.P.freed)
        self.bufs.append(b)
        return b

    def sb(self, name, shape, dt=F32):
        return self._new(name, self.st.enter_context(self.nc.sbuf_tensor(self._uniq(name), list(shape), dt)))

    def ps(self, name, shape, dt=F32):
        return self._new(name, self.st.enter_context(self.nc.psum_tensor(self._uniq(name), list(shape), dt)))


SAME_ENGINE_SYNC = True


def new_prog(nc, **kw):
    kw.setdefault('same_engine_sync', SAME_ENGINE_SYNC)
    P = Prog(nc, **kw)
    P.freed = []
    return P


def make_ident(P, S, name="ident", n=128, dt=F32):
    ident = S.sb(name, [n, n], dt)
    P.V("pool", "memset", ident[:], 1.0, writes=[ident])
    P.op("pool", lambda e: e.affine_select(out=ident[:], in_=ident[:], pattern=[[-1, n]],
                                           compare_op=ALU.is_equal, fill=0.0, base=0,
                                           channel_multiplier=1), reads=[ident], writes=[ident])
    return ident


D = 1024
DFF = 2816
NMOD = 9
INC = 2464
EPS = 1e-6
KC = D // 128
FC = DFF // 128


def token_blocks(n_lat, n_ctx):
    blks = []
    c = 0
    while c < n_lat:
        w = min(512, n_lat - c)
        blks.append((c, w, 0))
        c += w
    if n_ctx:
        blks.append((n_lat, n_ctx, 1))
    return blks


class StageCtx:
    pass


def emit_modulate(P, S, C, src_view, src_bufs, mi_shift, mi_scale, hT, blks, psb):
    with Scope(P) as A:
        xblk = [A.sb("xblk%d" % i, [128, KC, 512]) for i in range(2)]
        sq = A.sb("sq", [128, KC, 512], BF16)
        rs = A.sb("rs", [128, 512])
        tmp = [A.sb("mtmp%d" % i, [128, 512]) for i in range(2)]
        for bi, (c0, w, mj) in enumerate(blks):
            xb = xblk[bi % 2]
            P.dma("sp", xb[:, :, 0:w], src_view[:, :, c0:c0 + w], reads=src_bufs(bi), writes=[xb])
            P.act(sq[:, :, 0:w], xb[:, :, 0:w], AF.Square, reads=[xb], writes=[sq])
            ss = psb[bi % 2]
            for c in range(KC):
                P.mm(ss[:, 0:w], C.ones_bf[:], sq[:, c, 0:w], c == 0, c == KC - 1, reads=[C.ones_bf, sq], writes=[ss])
            P.act(rs[:, 0:w], ss[:, 0:w], AF.Sqrt, reads=[ss, C.eps_col], writes=[rs], scale=1.0 / D, bias=C.eps_col[:, 0:1])
            P.V("dve", "reciprocal", rs[:, 0:w], rs[:, 0:w], reads=[rs], writes=[rs])
            for c in range(KC):
                t = tmp[c % 2]
                P.V("dve", "tensor_tensor", t[:, 0:w], xb[:, c, 0:w], rs[:, 0:w], ALU.mult, reads=[xb, rs], writes=[t])
                P.act(hT[:, c, c0:c0 + w], t[:, 0:w], AF.Identity, reads=[t, C.modp], writes=[hT],
                      scale=C.modp[:, mi_scale * KC + c, mj:mj + 1], bias=C.modp[:, mi_shift * KC + c, mj:mj + 1])


def emit_ffn(P, S, C, w13_d, w2_d, hT, blks, psb, mi_gate, x_view, x_bufs, out_view, out_bufs, NT):
    w13v = w13_d.rearrange("(c p) (two f) -> p c two f", p=128, two=2)
    w2v = w2_d.rearrange("(k p) f -> p k f", p=128)
    with Scope(P) as B:
        act = B.sb("act", [128, FC, NT], BF16)
        with Scope(P) as B2:
            wgu = [B2.sb("wgu%d" % i, [128, KC, 2, 128], BF16) for i in range(3)]
            sg = [B2.sb("sg%d" % i, [128, 512]) for i in range(2)]
            n = 0
            for i in range(FC):
                wt = wgu[i % 3]
                P.dma("pool", wt[:, :, 0, :], w13v[:, :, 0, i * 128:(i + 1) * 128], writes=[wt])
                P.dma("pool", wt[:, :, 1, :], w13v[:, :, 1, i * 128:(i + 1) * 128], writes=[wt])
                for bi, (c0, w, mj) in enumerate(blks):
                    pg = psb[2 + (n % 2) * 2]
                    pu = psb[3 + (n % 2) * 2]
                    for c in range(KC):
                        P.mm(pg[:, 0:w], wt[:, c, 0, :], hT[:, c, c0:c0 + w], c == 0, c == KC - 1, reads=[wt, hT], writes=[pg])
                    for c in range(KC):
                        P.mm(pu[:, 0:w], wt[:, c, 1, :], hT[:, c, c0:c0 + w], c == 0, c == KC - 1, reads=[wt, hT], writes=[pu])
                    s = sg[n % 2]
                    P.act(s[:, 0:w], pg[:, 0:w], AF.Silu, reads=[pg], writes=[s])
                    P.V("dve", "tensor_tensor", act[:, i, c0:c0 + w], s[:, 0:w], pu[:, 0:w], ALU.mult, reads=[s, pu], writes=[act])
                    n += 1
        with Scope(P) as B3:
            wd = [B3.sb("wd%d" % i, [128, FC, 128], BF16) for i in range(2)]
            xo = [B3.sb("xo%d" % i, [128, 512]) for i in range(3)]
            xn = [B3.sb("xn%d" % i, [128, 512]) for i in range(3)]
            n = 0
            for o in range(KC):
                wt = wd[o % 2]
                P.dma("pool", wt[:], w2v[:, :, o * 128:(o + 1) * 128], writes=[wt])
                for bi, (c0, w, mj) in enumerate(blks):
                    py = psb[n % 2]
                    for k in range(FC):
                        P.mm(py[:, 0:w], wt[:, k, :], act[:, k, c0:c0 + w], k == 0, k == FC - 1, reads=[wt, act], writes=[py])
                    xi = xo[n % 3]
                    xw = xn[n % 3]
                    P.dma("sp", xi[:, 0:w], x_view[:, o, c0:c0 + w], reads=x_bufs(bi), writes=[xi])
                    P.V("dve", "scalar_tensor_tensor", xw[:, 0:w], py[:, 0:w], C.modh[:, mi_gate * KC + o, mj:mj + 1], xi[:, 0:w],
                        ALU.mult, ALU.add, reads=[py, xi, C.modh], writes=[xw])
                    P.dma("sp", out_view[:, o, c0:c0 + w], xw[:, 0:w], reads=[xw], writes=[out_bufs[(o, bi)]],
                          is_output=True)
                    n += 1


def emit_proj(P, S, C, w_d, ncols, hT, blks, psb, out_fn):
    wv = w_d.rearrange("(c p) f -> p c f", p=128)
    nf = (ncols + 127) // 128
    with Scope(P) as E:
        wt_ = [E.sb("wpr%d" % i, [128, KC, 128], BF16) for i in range(3)]
        n = 0
        for f in range(nf):
            fw = min(128, ncols - f * 128)
            wt = wt_[f % 3]
            P.dma("pool", wt[:, :, 0:fw], wv[:, :, f * 128:f * 128 + fw], writes=[wt])
            for bi, blk in enumerate(blks):
                c0, w, mj = blk
                pp = psb[4 + n % 4]
                for c in range(KC):
                    P.mm(pp[0:fw, 0:w], wt[:, c, 0:fw], hT[:, c, c0:c0 + w], c == 0, c == KC - 1, reads=[wt, hT], writes=[pp])
                out_fn(E, n, f, fw, bi, blk, pp)
                n += 1


def build_stage(kind, NT_lat, NT_ctx, final=False):
    NT = NT_lat + NT_ctx
    nc = bass.Bass("TRN2", target_bir_lowering=False)
    P = new_prog(nc)

    def din(name, shape):
        return nc.dram_tensor(name, list(shape), F32, kind="ExternalInput").ap()

    def dout(name, shape):
        return nc.dram_tensor(name, list(shape), F32, kind="ExternalOutput").ap()

    T = {"xT": din("xT", [D, NT]), "w13": din("w13", [D, 2 * DFF]), "w2": din("w2", [DFF, D]), "xnew": dout("xnew", [D, NT])}
    if kind == "P":
        T.update(cT=din("cT", [128, KC, 2]), ada_w=din("ada_w", [D, NMOD * D]), ada_b=din("ada_b", [128, NMOD * KC]),
                 w_in=din("w_in", [D, INC]), modo=dout("modo", [128, NMOD * KC, 2]), uT=dout("uT", [INC, NT]))
    else:
        T.update(modi=din("modi", [128, NMOD * KC, 2]), yT=din("yT", [D, NT]), w_out=din("w_out", [D, D]), xmid=dout("xmid", [D, NT]))
        if final:
            T.update(fin_g=din("fin_g", [128, KC]), xfin=dout("xfin", [D, NT]))
    emit_stage(P, nc, kind, NT_lat, NT_ctx, final, T)
    P.emit()
    return nc


def emit_stage(P, nc, kind, NT_lat, NT_ctx, final, T):
    NT = NT_lat + NT_ctx
    blks = token_blocks(NT_lat, NT_ctx)
    nb = len(blks)
    xT = T["xT"]; w13 = T["w13"]; w2 = T["w2"]; xnew = T["xnew"]
    xv = xT.rearrange("(c p) t -> p c t", p=128)
    xnv = xnew.rearrange("(c p) t -> p c t", p=128)
    if kind == "P":
        cT = T["cT"]; ada_w = T["ada_w"]; ada_b = T["ada_b"]; w_in = T["w_in"]; modo = T["modo"]; uT = T["uT"]
    else:
        modi = T["modi"]; yT = T["yT"]; w_out = T["w_out"]; xmid = T["xmid"]
        yv = yT.rearrange("(c p) t -> p c t", p=128)
        xmv = xmid.rearrange("(c p) t -> p c t", p=128)
        if final:
            fin_g = T["fin_g"]; xfin = T["xfin"]
            xfv = xfin.rearrange("(c p) t -> p c t", p=128)

    with Scope(P) as S:
        C = StageCtx()
        C.ones_bf = S.sb("ones_bf", [128, 128], BF16)
        P.V("dve", "memset", C.ones_bf[:], 1.0, writes=[C.ones_bf])
        C.eps_col = S.sb("eps_col", [128, 1])
        P.V("dve", "memset", C.eps_col[:], EPS, writes=[C.eps_col])
        C.mod = S.sb("mod", [128, NMOD * KC, 2])
        C.modp = S.sb("modp", [128, NMOD * KC, 2])
        C.modh = S.sb("modh", [128, NMOD * KC, 2])
        psb = [S.ps("psb%d" % i, [128, 512]) for i in range(8)]
        hT = S.sb("hT", [128, KC, NT], BF16)
        none_bufs = lambda bi: []

        if kind == "P":
            with Scope(P) as M:
                cs = M.sb("cs", [128, KC, 2])
                cb = M.sb("cb", [128, KC, 2], BF16)
                ab = M.sb("ab", [128, NMOD * KC])
                P.dma("sp", cs[:], cT, writes=[cs])
                P.dma("sp", ab[:], ada_b, writes=[ab])
                P.act(cb[:], cs[:], AF.Silu, reads=[cs], writes=[cb])
                GRP = 4
                aw = [M.sb("aw%d" % i, [128, KC, GRP * 128], BF16) for i in range(2)]
                av = ada_w.rearrange("(c p) f -> p c f", p=128)
                pm = psb[0]
                ng = NMOD * KC // GRP
                for g in range(ng):
                    a = aw[g % 2]
                    P.dma("pool", a[:], av[:, :, g * GRP * 128:(g + 1) * GRP * 128], writes=[a])
                    for mm_ in range(GRP):
                        m = g * GRP + mm_
                        for c in range(KC):
                            P.mm(pm[:, 2 * m:2 * m + 2], a[:, c, mm_ * 128:(mm_ + 1) * 128], cb[:, c, :], c == 0, c == KC - 1,
                                 reads=[a, cb], writes=[pm])
                for j in range(2):
                    P.V("dve", "tensor_tensor", C.mod[:, :, j], pm[:, j:2 * NMOD * KC:2], ab[:], ALU.add,
                        reads=[pm, ab], writes=[C.mod])
            P.dma("sp", modo, C.mod[:], reads=[C.mod], is_output=True)
        else:
            P.dma("sp", C.mod[:], modi, writes=[C.mod])
        P.V("dve", "tensor_scalar_add", C.modp[:], C.mod[:], 1.0, reads=[C.mod], writes=[C.modp])
        for i in (0, 3, 6):
            P.V("dve", "tensor_copy", C.modp[:, i * KC:(i + 1) * KC, :], C.mod[:, i * KC:(i + 1) * KC, :], reads=[C.mod, C.modp], writes=[C.modp])
        P.V("dve", "tensor_scalar_mul", C.modh[:], C.mod[:], 0.5, reads=[C.mod], writes=[C.modh])
        P.V("dve", "tensor_copy", C.modh[:, 5 * KC:6 * KC, :], C.mod[:, 5 * KC:6 * KC, :], reads=[C.mod, C.modh], writes=[C.modh])

        if kind == "P":
            out1 = {(o, bi): Buf("x1_%d_%d" % (o, bi)) for o in range(KC) for bi in range(nb)}
            emit_modulate(P, S, C, xv, none_bufs, 0, 1, hT, blks, psb)
            emit_ffn(P, S, C, w13, w2, hT, blks, psb, 2, xv, none_bufs, xnv, out1, NT)
            rd1 = lambda bi: [out1[(o, bi)] for o in range(KC)]
            emit_modulate(P, S, C, xnv, rd1, 3, 4, hT, blks, psb)
            with Scope(P) as U:
                us = [U.sb("us%d" % i, [128, 512]) for i in range(4)]

                def out_u(E, n, f, fw, bi, blk, pp):
                    c0, w, mj = blk
                    t = us[n % 4]
                    if n % 2 == 0:
                        P.V("dve", "tensor_copy", t[0:fw, 0:w], pp[0:fw, 0:w], reads=[pp], writes=[t])
                    else:
                        P.act(t[0:fw, 0:w], pp[0:fw, 0:w], AF.Copy, reads=[pp], writes=[t])
                    P.dma("sp", uT[f * 128:f * 128 + fw, c0:c0 + w], t[0:fw, 0:w], reads=[t], is_output=True)

                emit_proj(P, S, C, w_in, INC, hT, blks, psb, out_u)
        else:
            mid = {(o, bi): Buf("xm_%d_%d" % (o, bi)) for o in range(KC) for bi in range(nb)}
            with Scope(P) as W:
                ybf = W.sb("ybf", [128, KC, NT], BF16)
                for bi, (c0, w, mj) in enumerate(blks):
                    P.dma("pool", ybf[:, :, c0:c0 + w], yv[:, :, c0:c0 + w], writes=[ybf])
                xo = [W.sb("wxo%d" % i, [128, 512]) for i in range(3)]
                xn = [W.sb("wxn%d" % i, [128, 512]) for i in range(3)]

                def out_w(E, n, f, fw, bi, blk, pp):
                    c0, w, mj = blk
                    xi = xo[n % 3]
                    xw = xn[n % 3]
                    P.dma("sp", xi[:, 0:w], xv[:, f, c0:c0 + w], writes=[xi])
                    P.V("dve", "scalar_tensor_tensor", xw[:, 0:w], pp[:, 0:w], C.modh[:, 5 * KC + f, mj:mj + 1], xi[:, 0:w],
                        ALU.mult, ALU.add, reads=[pp, xi, C.modh], writes=[xw])
                    P.dma("sp", xmv[:, f, c0:c0 + w], xw[:, 0:w], reads=[xw], writes=[mid[(f, bi)]], is_output=True)

                emit_proj(P, S, C, w_out, D, ybf, blks, psb, out_w)
            rdm = lambda bi: [mid[(o, bi)] for o in range(KC)]
            emit_modulate(P, S, C, xmv, rdm, 6, 7, hT, blks, psb)
            out2 = {(o, bi): Buf("x2_%d_%d" % (o, bi)) for o in range(KC) for bi in range(nb)}
            emit_ffn(P, S, C, w13, w2, hT, blks, psb, 8, xmv, rdm, xnv, out2, NT)
            if final:
                rd2 = lambda bi: [out2[(o, bi)] for o in range(KC)]
                with Scope(P) as Fz:
                    fg = Fz.sb("fg", [128, KC])
                    P.dma("sp", fg[:], fin_g, writes=[fg])
                    xblk = [Fz.sb("fxb%d" % i, [128, KC, 512]) for i in range(2)]
                    sq = Fz.sb("fsq", [128, KC, 512], BF16)
                    rs = Fz.sb("frs", [128, 512])
                    ob = [Fz.sb("fob%d" % i, [128, KC, 512]) for i in range(2)]
                    for bi, (c0, w, mj) in enumerate(blks):
                        xb = xblk[bi % 2]
                        oo = ob[bi % 2]
                        P.dma("sp", xb[:, :, 0:w], xnv[:, :, c0:c0 + w], reads=rd2(bi), writes=[xb])
                        P.act(sq[:, :, 0:w], xb[:, :, 0:w], AF.Square, reads=[xb], writes=[sq])
                        ss = psb[bi % 2]
                        for c in range(KC):
                            P.mm(ss[:, 0:w], C.ones_bf[:], sq[:, c, 0:w], c == 0, c == KC - 1, reads=[C.ones_bf, sq], writes=[ss])
                        P.act(rs[:, 0:w], ss[:, 0:w], AF.Sqrt, reads=[ss, C.eps_col], writes=[rs], scale=1.0 / D, bias=C.eps_col[:, 0:1])
                        P.V("dve", "reciprocal", rs[:, 0:w], rs[:, 0:w], reads=[rs], writes=[rs])
                        for c in range(KC):
                            P.V("dve", "scalar_tensor_tensor", oo[:, c, 0:w], xb[:, c, 0:w], fg[:, c:c + 1], rs[:, 0:w],
                                ALU.mult, ALU.mult, reads=[xb, rs, fg], writes=[oo])
                        P.dma("sp", xfv[:, :, c0:c0 + w], oo[:, :, 0:w], reads=[oo], is_output=True)


TT = 8448
NCTX = 256
NTILE = TT // 128
SM_SCALE_F = float((64 + 32) ** -0.5)
CONVK = 31
LN_EPS_F = 1e-5
GN_EPS_F = 64e-5


def seq_blocks():
    b = [(0, NCTX, True)]
    c = NCTX
    while c < TT:
        b.append((c, 512, False))
        c += 512
    return b


def emit_conv(P, S, nc, io, psb):
    ua_lat, ua_ctx, cvw, cvb, lng, lnb, ya = io["ua_lat"], io["ua_ctx"], io["cv_w"], io["cv_b"], io["cv_lng"], io["cv_lnb"], io["ya"]
    with Scope(P) as A:
        w = A.sb("cvw", [128, 2, CONVK]); bb = A.sb("cvb", [128, 2]); g = A.sb("cvg", [128, 2]); be = A.sb("cvbe", [128, 2])
        P.dma("sp", w[:], cvw, writes=[w]); P.dma("sp", bb[:], cvb, writes=[bb])
        P.dma("sp", g[:], lng, writes=[g]); P.dma("sp", be[:], lnb, writes=[be])
        onesf = A.sb("cv_ones", [128, 128])
        P.V("dve", "memset", onesf[:], 1.0 / 256.0, writes=[onesf])
        epsc = A.sb("cv_eps", [128, 1])
        P.V("dve", "memset", epsc[:], LN_EPS_F, writes=[epsc])
        for (src, N, ocol) in ((ua_lat, 2048, 0), (ua_ctx, 64, 2048)):
            NP = N + 30
            sv = src.rearrange("(two c p) t -> p two c t", p=128, two=2)
            val = A.sb("cv_val", [128, 2, NP]); gate = A.sb("cv_gate", [128, 2, NP])
            P.dma("sp", val[:], sv[:, 0], writes=[val]); P.dma("sp", gate[:], sv[:, 1], writes=[gate])
            P.act(gate[:], gate[:], AF.Sigmoid, reads=[gate], writes=[gate])
            P.V("dve", "tensor_tensor", val[:], val[:], gate[:], ALU.mult, reads=[val, gate], writes=[val])
            acc = A.sb("cv_acc", [128, 2, N]); sq = A.sb("cv_sq", [128, 2, N])
            for c in range(2):
                P.V("dve", "tensor_scalar", acc[:, c, :], val[:, c, 0:N], w[:, c, 0:1], bb[:, c:c + 1], ALU.mult, ALU.add,
                    reads=[val, w, bb], writes=[acc])
                for j in range(1, CONVK):
                    P.V("dve", "scalar_tensor_tensor", acc[:, c, :], val[:, c, j:j + N], w[:, c, j:j + 1], acc[:, c, :],
                        ALU.mult, ALU.add, reads=[val, w, acc], writes=[acc])
            P.act(sq[:], acc[:], AF.Square, reads=[acc], writes=[sq])
            c0 = 0
            ot = [A.sb("cv_o%d" % i, [128, 2, 512]) for i in range(2)]
            mu = A.sb("cv_mu", [128, 512]); var = A.sb("cv_var", [128, 512]); tmp = A.sb("cv_tmp", [128, 512])
            bi = 0
            while c0 < N:
                wd_ = min(512, N - c0)
                p1 = psb[0]; p2 = psb[1]
                for c in range(2):
                    P.mm(p1[:, 0:wd_], onesf[:], acc[:, c, c0:c0 + wd_], c == 0, c == 1, reads=[onesf, acc], writes=[p1])
                for c in range(2):
                    P.mm(p2[:, 0:wd_], onesf[:], sq[:, c, c0:c0 + wd_], c == 0, c == 1, reads=[onesf, sq], writes=[p2])
                P.act(mu[:, 0:wd_], p1[:, 0:wd_], AF.Copy, reads=[p1], writes=[mu])
                P.V("dve", "tensor_tensor", tmp[:, 0:wd_], mu[:, 0:wd_], mu[:, 0:wd_], ALU.mult, reads=[mu], writes=[tmp])
                P.V("dve", "tensor_tensor", var[:, 0:wd_], p2[:, 0:wd_], tmp[:, 0:wd_], ALU.subtract, reads=[p2, tmp], writes=[var])
                P.act(var[:, 0:wd_], var[:, 0:wd_], AF.Sqrt, reads=[var, epsc], writes=[var], bias=epsc[:, 0:1])
                P.V("dve", "reciprocal", var[:, 0:wd_], var[:, 0:wd_], reads=[var], writes=[var])
                o = ot[bi % 2]
                for c in range(2):
                    P.V("dve", "tensor_tensor", tmp[:, 0:wd_], acc[:, c, c0:c0 + wd_], mu[:, 0:wd_], ALU.subtract, reads=[acc, mu], writes=[tmp])
                    P.V("dve", "tensor_tensor", tmp[:, 0:wd_], tmp[:, 0:wd_], var[:, 0:wd_], ALU.mult, reads=[tmp, var], writes=[tmp])
                    P.V("dve", "tensor_scalar", tmp[:, 0:wd_], tmp[:, 0:wd_], g[:, c:c + 1], be[:, c:c + 1], ALU.mult, ALU.add,
                        reads=[tmp, g, be], writes=[tmp])
                    P.act(o[:, c, 0:wd_], tmp[:, 0:wd_], AF.Silu, reads=[tmp], writes=[o])
                P.dma("sp", ya.rearrange("(c p) t -> p c t", p=128)[:, :, ocol + c0:ocol + c0 + wd_], o[:, :, 0:wd_], reads=[o], is_output=True)
                c0 += wd_
                bi += 1


LRU_STOP = 0


def emit_lru(P, S, nc, io, psb):
    ub = io["ub"]
    blocks = seq_blocks()
    with Scope(P) as A:
        cw = A.sb("lr_cw", [128, 4]); cb = A.sb("lr_cb", [128, 1])
        Wa = A.sb("lr_wa", [64, 128]); Wx = A.sb("lr_wx", [64, 128])
        ba = A.sb("lr_ba", [128, 1]); bx = A.sb("lr_bx", [128, 1]); lam = A.sb("lr_lam", [128, 1])
        for t_, n_ in ((cw, "lru_cw"), (cb, "lru_cb"), (Wa, "lru_wa"), (Wx, "lru_wx"), (ba, "lru_ba"), (bx, "lru_bx"), (lam, "lru_lam")):
            P.dma("sp", t_[:], io[n_], writes=[t_])
        cc = A.sb("lr_c", [128, 1]); c2 = A.sb("lr_c2", [128, 1])
        P.act(cc[:], lam[:], AF.Exp, reads=[lam], writes=[cc], scale=-1.0)
        P.act(cc[:], cc[:], AF.Ln, reads=[cc], writes=[cc], bias=1.0)
        P.V("dve", "tensor_scalar_mul", c2[:], cc[:], -16.0, reads=[cc], writes=[c2])
        P.V("dve", "tensor_scalar_mul", cc[:], cc[:], -8.0, reads=[cc], writes=[cc])
        ident = make_ident(P, A, "lr_id")
        stk = A.sb("lr_stk", [128, 64])
        P.V("dve", "tensor_copy", stk[0:64, :], ident[0:64, 0:64], reads=[ident], writes=[stk])
        P.V("dve", "tensor_copy", stk[64:128, :], ident[64:128, 64:128], reads=[ident, stk], writes=[stk])
        X = A.sb("lr_x", [128, TT]); XV = A.sb("lr_xv", [128, TT])
        Aa = A.sb("lr_a", [128, TT]); Bb = A.sb("lr_b", [128, TT])
        P.dma("sp", X[0:64, :], ub[0:64, :], writes=[X])
        P.dma("sp", X[64:128, :], ub[0:64, :], writes=[X])
        P.V("dve", "tensor_scalar", XV[:], X[:], cw[:, 2:3], cb[:, 0:1], ALU.mult, ALU.add, reads=[X, cw, cb], writes=[XV])
        for (lo, hi) in ((0, NCTX), (NCTX, TT)):
            for i, s in ((0, -2), (1, -1), (3, 1)):
                a = max(lo, lo - s); b = min(hi, hi - s)
                P.V("dve", "scalar_tensor_tensor", XV[:, a:b], X[:, a + s:b + s], cw[:, i:i + 1], XV[:, a:b], ALU.mult, ALU.add,
                    reads=[X, cw, XV], writes=[XV])
        for bi, (c0, w, isc) in enumerate(blocks):
            pa = psb[(bi % 2) * 2]; px = psb[(bi % 2) * 2 + 1]
            P.mm(pa[:, 0:w], Wa[:], XV[0:64, c0:c0 + w], True, True, reads=[Wa, XV], writes=[pa])
            P.mm(px[:, 0:w], Wx[:], XV[0:64, c0:c0 + w], True, True, reads=[Wx, XV], writes=[px])
            P.act(Aa[:, c0:c0 + w], pa[:, 0:w], AF.Sigmoid, reads=[pa, ba], writes=[Aa], bias=ba[:, 0:1])
            P.act(Bb[:, c0:c0 + w], px[:, 0:w], AF.Sigmoid, reads=[px, bx], writes=[Bb], bias=bx[:, 0:1])
        t1 = [A.sb("lr_t%d" % i, [128, 512]) for i in range(2)]
        for bi, (c0, w, isc) in enumerate(blocks):
            t = t1[bi % 2]
            P.act(t[:, 0:w], Aa[:, c0:c0 + w], AF.Exp, reads=[Aa, c2], writes=[t], scale=c2[:, 0:1])
            P.act(Aa[:, c0:c0 + w], Aa[:, c0:c0 + w], AF.Exp, reads=[Aa, cc], writes=[Aa], scale=cc[:, 0:1])
            P.V("dve", "tensor_scalar", t[:, 0:w], t[:, 0:w], -1.0, 1.0, ALU.mult, ALU.add, reads=[t], writes=[t])
            P.V("dve", "tensor_scalar_max", t[:, 0:w], t[:, 0:w], 1e-30, reads=[t], writes=[t])
            P.act(t[:, 0:w], t[:, 0:w], AF.Ln, reads=[t], writes=[t])
            P.act(t[:, 0:w], t[:, 0:w], AF.Exp, reads=[t], writes=[t], scale=0.5)
            P.V("dve", "tensor_tensor", Bb[:, c0:c0 + w], Bb[:, c0:c0 + w], XV[:, c0:c0 + w], ALU.mult, reads=[Bb, XV], writes=[Bb])
            P.V("dve", "tensor_tensor", Bb[:, c0:c0 + w], Bb[:, c0:c0 + w], t[:, 0:w], ALU.mult, reads=[Bb, t], writes=[Bb])
        H = X
        P.V("dve", "tensor_tensor_scan", H[0:64, 0:NCTX], Aa[0:64, 0:NCTX], Bb[0:64, 0:NCTX], 0.0, ALU.mult, ALU.add,
            reads=[Aa, Bb, XV], writes=[H])
        P.V("dve", "tensor_tensor_scan", H[0:64, NCTX:TT], Aa[0:64, NCTX:TT], Bb[0:64, NCTX:TT], H[0:64, NCTX - 1:NCTX], ALU.mult, ALU.add,
            reads=[Aa, Bb, H], writes=[H])
        P.V("dve", "tensor_tensor_scan", H[64:128, NCTX - 1::-1], Aa[64:128, NCTX - 1::-1], Bb[64:128, NCTX - 1::-1], 0.0, ALU.mult, ALU.add,
            reads=[Aa, Bb, H], writes=[H])
        P.V("dve", "tensor_tensor_scan", H[64:128, TT - 1:NCTX - 1:-1], Aa[64:128, TT - 1:NCTX - 1:-1], Bb[64:128, TT - 1:NCTX - 1:-1], H[64:128, 0:1], ALU.mult, ALU.add,
            reads=[Aa, Bb, H], writes=[H])
        G = XV
        P.dma("sp", G[0:64, :], ub[64:128, :], reads=[Bb], writes=[G])
        o2 = [A.sb("lr_o%d" % i, [64, 512]) for i in range(2)]
        for bi, (c0, w, isc) in enumerate(blocks):
            t = t1[bi % 2]
            ph = psb[4 + bi % 2]
            P.mm(ph[0:64, 0:w], stk[:], H[:, c0:c0 + w], True, True, reads=[stk, H], writes=[ph])
            g_ = G[0:64, c0:c0 + w]
            tt = t[0:64, 0:w]
            P.V("dve", "tensor_tensor", tt, g_, g_, ALU.mult, reads=[G], writes=[t])
            P.V("dve", "tensor_scalar", tt, tt, 0.044715, 1.0, ALU.mult, ALU.add, reads=[t], writes=[t])
            P.V("dve", "tensor_tensor", tt, tt, g_, ALU.mult, reads=[t, G], writes=[t])
            P.act(tt, tt, AF.Sigmoid, reads=[t], writes=[t], scale=1.5957691216057308)
            P.V("dve", "tensor_tensor", tt, tt, g_, ALU.mult, reads=[t, G], writes=[t])
            o = o2[bi % 2]
            P.V("dve", "tensor_tensor", o[:, 0:w], ph[0:64, 0:w], tt, ALU.mult, reads=[ph, t], writes=[o])
            for (dst, off, ww) in io["ypieces"]("yb", c0, w):
                P.dma("sp", dst, o[:, off:off + ww], reads=[o], is_output=True)


def emit_mla(P, S, nc, io, psb, need_ctx_q):
    ud = io["ud"]
    blocks = seq_blocks()
    with Scope(P) as A:
        wuq = A.sb("ml_wuq", [128, 2, 96], BF16); wkn = A.sb("ml_wkn", [128, 64], BF16); wv = A.sb("ml_wv", [128, 64], BF16)
        qn = A.sb("ml_qn", [128, 2]); kvn = A.sb("ml_kvn", [128, 1]); Rp = A.sb("ml_rp", [96, 96])
        P.dma("pool", wuq[:], io["mla_wuq"].rearrange("(c p) f -> p c f", p=128), writes=[wuq])
        P.dma("pool", wkn[:], io["mla_wkn"], writes=[wkn]); P.dma("pool", wv[:], io["mla_wv"], writes=[wv])
        P.dma("sp", qn[:], io["mla_qn"], writes=[qn]); P.dma("sp", kvn[:], io["mla_kvn"], writes=[kvn])
        P.dma("sp", Rp[:], io["rope_R"], writes=[Rp])
        ones_bf = A.sb("ml_ones", [128, 128], BF16)
        P.V("dve", "memset", ones_bf[:], 1.0, writes=[ones_bf])
        onesf = A.sb("ml_onesf", [128, 64])
        P.V("dve", "memset", onesf[:], 1.0, writes=[onesf])
        epsc = A.sb("ml_eps", [128, 1])
        P.V("dve", "memset", epsc[:], EPS, writes=[epsc])
        KT = A.sb("ml_KT", [96, TT], BF16); QT = A.sb("ml_QT", [96, TT], BF16)
        Va = A.sb("ml_Va", [128, NTILE, 65], BF16)
        P.V("pool", "memset", Va[:], 1.0, writes=[Va])
        qm = A.sb("ml_qm", [128, 1]); km = A.sb("ml_km", [128, 1]); negM = A.sb("ml_negM", [128, 1])
        P.V("dve", "memset", qm[:], 0.0, writes=[qm]); P.V("dve", "memset", km[:], 0.0, writes=[km])
        udv = ud[0:384, :].rearrange("(c p) t -> p c t", p=128)
        with Scope(P) as B:
            xin = [B.sb("ml_x%d" % i, [128, 3, 512]) for i in range(2)]
            kr = [B.sb("ml_kr%d" % i, [96, 512]) for i in range(2)]
            for k_ in kr:
                P.V("pool", "memset", k_[:], 0.0, writes=[k_])
            sq_2 = [B.sb("ml_sq%d" % _i, [128, 3, 512], BF16) for _i in range(2)]
            rq_2 = [B.sb("ml_rq%d" % _i, [128, 512]) for _i in range(2)]; rk_2 = [B.sb("ml_rk%d" % _i, [128, 512]) for _i in range(2)]
            cqn_2 = [B.sb("ml_cqn%d" % _i, [128, 2, 512], BF16) for _i in range(2)]; ckvn_2 = [B.sb("ml_ckvn%d" % _i, [128, 512], BF16) for _i in range(2)]
            qsb_2 = [B.sb("ml_qsb%d" % _i, [96, 512]) for _i in range(2)]; t1_2 = [B.sb("ml_t1%d" % _i, [96, 512]) for _i in range(2)]; t2_2 = [B.sb("ml_t2%d" % _i, [96, 512]) for _i in range(2)]
            ct_2 = [B.sb("ml_ct%d" % _i, [96, 512]) for _i in range(2)]; st__2 = [B.sb("ml_st%d" % _i, [96, 512]) for _i in range(2)]
            sqq = B.sb("ml_sqq", [96, 512], BF16); mx = B.sb("ml_mx", [128, 1])
            for bi, (c0, w, isc) in enumerate(blocks):
                x = xin[bi % 2]; krt = kr[bi % 2]
                sq = sq_2[bi % 2]; rq = rq_2[bi % 2]; rk = rk_2[bi % 2]; cqn = cqn_2[bi % 2]; ckvn = ckvn_2[bi % 2]
                qsb = qsb_2[bi % 2]; t1 = t1_2[bi % 2]; t2 = t2_2[bi % 2]; ct = ct_2[bi % 2]; st_ = st__2[bi % 2]
                P.dma("sp", x[:, :, 0:w], udv[:, :, c0:c0 + w], writes=[x])
                P.dma("sp", krt[64:96, 0:w], ud[384:416, c0:c0 + w], writes=[krt])
                if not isc:
                    P.dma("sp", ct[:, 0:w], io["rope_C"][:, c0 - NCTX:c0 - NCTX + w], writes=[ct])
                    P.dma("sp", st_[:, 0:w], io["rope_S"][:, c0 - NCTX:c0 - NCTX + w], writes=[st_])
                P.act(sq[:, :, 0:w], x[:, :, 0:w], AF.Square, reads=[x], writes=[sq])
                pq = psb[0]; pk = psb[1]
                for c in range(2):
                    P.mm(pq[:, 0:w], ones_bf[:], sq[:, c, 0:w], c == 0, c == 1, reads=[ones_bf, sq], writes=[pq])
                P.mm(pk[:, 0:w], ones_bf[:], sq[:, 2, 0:w], True, True, reads=[ones_bf, sq], writes=[pk])
                P.act(rq[:, 0:w], pq[:, 0:w], AF.Sqrt, reads=[pq, epsc], writes=[rq], scale=1.0 / 256.0, bias=epsc[:, 0:1])
                P.V("dve", "reciprocal", rq[:, 0:w], rq[:, 0:w], reads=[rq], writes=[rq])
                P.act(rk[:, 0:w], pk[:, 0:w], AF.Sqrt, reads=[pk, epsc], writes=[rk], scale=1.0 / 128.0, bias=epsc[:, 0:1])
                P.V("dve", "reciprocal", rk[:, 0:w], rk[:, 0:w], reads=[rk], writes=[rk])
                for c in range(2):
                    P.V("dve", "scalar_tensor_tensor", cqn[:, c, 0:w], x[:, c, 0:w], qn[:, c:c + 1], rq[:, 0:w], ALU.mult, ALU.mult,
                        reads=[x, qn, rq], writes=[cqn])
                P.V("dve", "scalar_tensor_tensor", ckvn[:, 0:w], x[:, 2, 0:w], kvn[:, 0:1], rk[:, 0:w], ALU.mult, ALU.mult,
                    reads=[x, kvn, rk], writes=[ckvn])
                pqq = psb[2]
                for c in range(2):
                    P.mm(pqq[0:96, 0:w], wuq[:, c, :], cqn[:, c, 0:w], c == 0, c == 1, reads=[wuq, cqn], writes=[pqq])
                if isc:
                    P.act(QT[:, c0:c0 + w], pqq[0:96, 0:w], AF.Copy, reads=[pqq], writes=[QT])
                else:
                    P.act(qsb[:, 0:w], pqq[0:96, 0:w], AF.Copy, reads=[pqq], writes=[qsb])
                    pr = psb[3]
                    P.mm(pr[0:96, 0:w], Rp[:], qsb[:, 0:w], True, True, reads=[Rp, qsb], writes=[pr])
                    P.V("dve", "tensor_tensor", t1[:, 0:w], qsb[:, 0:w], ct[:, 0:w], ALU.mult, reads=[qsb, ct], writes=[t1])
                    P.V("dve", "tensor_tensor", t2[:, 0:w], pr[0:96, 0:w], st_[:, 0:w], ALU.mult, reads=[pr, st_], writes=[t2])
                    P.V("dve", "tensor_tensor", QT[:, c0:c0 + w], t1[:, 0:w], t2[:, 0:w], ALU.add, reads=[t1, t2], writes=[QT])
                pkn = psb[4]
                P.mm(pkn[0:64, 0:w], wkn[:], ckvn[:, 0:w], True, True, reads=[wkn, ckvn], writes=[pkn])
                P.act(KT[0:64, c0:c0 + w], pkn[0:64, 0:w], AF.Copy, reads=[pkn], writes=[KT])
                if isc:
                    P.V("dve", "tensor_copy", KT[64:96, c0:c0 + w], krt[64:96, 0:w], reads=[krt], writes=[KT])
                else:
                    prk = psb[5]
                    P.mm(prk[0:96, 0:w], Rp[:], krt[:, 0:w], True, True, reads=[Rp, krt], writes=[prk])
                    P.V("dve", "tensor_tensor", t1[64:96, 0:w], krt[64:96, 0:w], ct[64:96, 0:w], ALU.mult, reads=[krt, ct], writes=[t1])
                    P.V("dve", "tensor_tensor", t2[64:96, 0:w], prk[64:96, 0:w], st_[64:96, 0:w], ALU.mult, reads=[prk, st_], writes=[t2])
                    P.V("dve", "tensor_tensor", KT[64:96, c0:c0 + w], t1[64:96, 0:w], t2[64:96, 0:w], ALU.add, reads=[t1, t2], writes=[KT])
                for ti in range(w // 128):
                    kt = c0 // 128 + ti
                    pv = psb[6 + ti % 2]
                    P.mm(pv[:, 0:64], ckvn[:, ti * 128:(ti + 1) * 128], wv[:], True, True, reads=[ckvn, wv], writes=[pv])
                    if ti % 2 == 0:
                        P.V("dve", "tensor_copy", Va[:, kt, 0:64], pv[:, 0:64], reads=[pv], writes=[Va])
                    else:
                        P.act(Va[:, kt, 0:64], pv[:, 0:64], AF.Copy, reads=[pv], writes=[Va])
                for (SRC, mmx) in ((QT, qm), (KT, km)):
                    P.act(sqq[:, 0:w], SRC[:, c0:c0 + w], AF.Square, reads=[SRC], writes=[sqq])
                    pn = psb[0]
                    P.mm(pn[:, 0:w], ones_bf[0:96, :], sqq[:, 0:w], True, True, reads=[ones_bf, sqq], writes=[pn])
                    P.V("dve", "reduce_max", mx[:], pn[:, 0:w], AX.X, reads=[pn], writes=[mx])
                    P.V("dve", "tensor_max", mmx[:], mmx[:], mx[:], reads=[mmx, mx], writes=[mmx])
        P.V("dve", "tensor_tensor", negM[:], qm[:], km[:], ALU.mult, reads=[qm, km], writes=[negM])
        P.act(negM[:], negM[:], AF.Sqrt, reads=[negM], writes=[negM])
        P.V("dve", "tensor_scalar_mul", negM[:], negM[:], -SM_SCALE_F, reads=[negM], writes=[negM])
        with Scope(P) as C_:
            PT = [C_.sb("ml_pt%d" % i, [128, 512], BF16) for i in range(3)]
            osb = [C_.sb("ml_osb%d" % i, [65, 512]) for i in range(2)]
            rden = [C_.sb("ml_rden%d" % i, [64, 512]) for i in range(2)]
            yo = [C_.sb("ml_yo%d" % i, [64, 512]) for i in range(2)]
            qblocks = []
            if need_ctx_q:
                qblocks.append((0, NCTX, list(range(NCTX // 128))))
            for (c0, w, isc) in blocks[1:]:
                qblocks.append((c0, w, list(range(NTILE))))
            items = []
            for qi, (c0, w, kts) in enumerate(qblocks):
                for ki, kt in enumerate(kts):
                    items.append((qi, c0, w, ki, kt, len(kts)))
            AHEAD = 2

            def issue_S(n):
                qi, c0, w, ki, kt, nk = items[n]
                ps_ = psb[n % 3]
                P.mm(ps_[:, 0:w], KT[:, kt * 128:(kt + 1) * 128], QT[:, c0:c0 + w], True, True, reads=[KT, QT], writes=[ps_])

            for n in range(min(AHEAD, len(items))):
                issue_S(n)
            for n, (qi, c0, w, ki, kt, nk) in enumerate(items):
                if n + AHEAD < len(items):
                    issue_S(n + AHEAD)
                ps_ = psb[n % 3]
                pt = PT[n % 3]
                po = psb[6 + qi % 2]
                P.act(pt[:, 0:w], ps_[:, 0:w], AF.Exp, reads=[ps_, negM], writes=[pt], scale=SM_SCALE_F, bias=negM[:, 0:1])
                P.mm(po[0:65, 0:w], Va[:, kt, :], pt[:, 0:w], ki == 0, ki == nk - 1, reads=[Va, pt], writes=[po])
                if ki == nk - 1:
                    ob = osb[qi % 2]
                    P.V("dve", "tensor_copy", ob[:, 0:w], po[0:65, 0:w], reads=[po], writes=[ob])
                    pd = psb[3 + qi % 2]
                    P.mm(pd[0:64, 0:w], onesf[64:65, 0:64], ob[64:65, 0:w], True, True, reads=[onesf, ob], writes=[pd])
                    rd = rden[qi % 2]
                    P.V("dve", "reciprocal", rd[:, 0:w], pd[0:64, 0:w], reads=[pd], writes=[rd])
                    y_ = yo[qi % 2]
                    P.V("dve", "tensor_tensor", y_[:, 0:w], ob[0:64, 0:w], rd[:, 0:w], ALU.mult, reads=[ob, rd], writes=[y_])
                    for (dst, off, ww) in io["ypieces"]("yd", c0, w):
                        P.dma("sp", dst, y_[:, off:off + ww], reads=[y_], is_output=True)


def build_mixer(layer_has_ctx_out, which=("conv", "lru", "rwkv", "mla")):
    nc = bass.Bass("TRN2", target_bir_lowering=False)
    P = new_prog(nc)
    io = {}

    def din(name, shape):
        io[name] = nc.dram_tensor(name, list(shape), F32, kind="ExternalInput").ap()

    def dout(name, shape):
        io[name] = nc.dram_tensor(name, list(shape), F32, kind="ExternalOutput").ap()

    if "conv" in which:
        din("ua_lat", [512, 2048 + 30]); din("ua_ctx", [512, 64 + 30])
        din("cv_w", [128, 2, CONVK]); din("cv_b", [128, 2]); din("cv_lng", [128, 2]); din("cv_lnb", [128, 2])
        dout("ya", [256, 2112])
    if "lru" in which:
        din("ub", [128, TT])
        din("lru_cw", [128, 4]); din("lru_cb", [128, 1]); din("lru_wa", [64, 128]); din("lru_wx", [64, 128])
        din("lru_ba", [128, 1]); din("lru_bx", [128, 1]); din("lru_lam", [128, 1])
        dout("yb", [64, TT])
    if "mla" in which:
        din("ud", [416, TT])
        din("mla_wuq", [256, 96]); din("mla_wkn", [128, 64]); din("mla_wv", [128, 64])
        din("mla_qn", [128, 2]); din("mla_kvn", [128, 1])
        din("rope_R", [96, 96]); din("rope_C", [96, 8192]); din("rope_S", [96, 8192])
        dout("yd", [64, TT])
    if "rwkv" in which:
        rwkv_io(din, dout)
    io["ypieces"] = lambda name, c0, w: [(io[name][:, c0:c0 + w], 0, w)]
    emit_mixers(P, nc, io, layer_has_ctx_out, which)
    P.emit()
    return nc


def emit_mixers(P, nc, io, need_ctx_q, which=("conv", "lru", "rwkv", "mla")):
    with Scope(P) as S:
        for nm, fn in (("conv", emit_conv), ("lru", emit_lru), ("mla", emit_mla)):
            if nm in which:
                with Scope(P) as PS_:
                    psb = [PS_.ps("psb%d" % i, [128, 512]) for i in range(8)]
                    if nm == "mla":
                        fn(P, S, nc, io, psb, need_ctx_q)
                    else:
                        fn(P, S, nc, io, psb)
        if "rwkv" in which:
            emit_rwkv(P, S, nc, io)


def rwkv_io(din, dout):
    din("uc", [512, TT])
    din("rw_mup", [128, 4]); din("rw_mun", [128, 4])
    din("rw_wup", [128, 2, 64])
    din("rw_aup", [128, 2, 64])
    din("rw_gup", [128, 64])
    din("rw_w0b", [128, 2, 64]); din("rw_a0b", [128, 2, 64])
    din("rw_kkb", [128, 64]); din("rw_kab", [128, 64]); din("rw_rkb", [128, 64])
    din("rw_gng", [128, 64]); din("rw_gnb", [128, 64])
    din("rw_masks", [128, 4, 128])
    dout("yc", [64, TT])


RWKV_STOP = 999


def emit_rwkv(P, S, nc, io):
    uc = io["uc"]
    ucv = uc.rearrange("(c p) t -> p c t", p=128)
    with Scope(P) as A:
        def load(name, shape, src=None):
            t = A.sb(name, shape)
            P.dma("sp", t[:], io[name] if src is None else src, writes=[t])
            return t
        mup = load("rw_mup", [128, 4]); mun = load("rw_mun", [128, 4])
        wup = load("rw_wup", [128, 2, 64]); aup = load("rw_aup", [128, 2, 64]); gup = load("rw_gup", [128, 64])
        w0b = load("rw_w0b", [128, 2, 64]); a0b = load("rw_a0b", [128, 2, 64])
        kkb = load("rw_kkb", [128, 64]); kab = load("rw_kab", [128, 64]); rkb = load("rw_rkb", [128, 64])
        gng = load("rw_gng", [128, 64]); gnb = load("rw_gnb", [128, 64])
        masks = load("rw_masks", [128, 4, 128])
        UPs, LOs, UPi, LOi = 0, 1, 2, 3
        ident = make_ident(P, A, "rw_id")
        omm = A.sb("rw_omm", [128, 4])
        P.V("dve", "tensor_tensor", omm[:], mup[:], mun[:], ALU.add, reads=[mup, mun], writes=[omm])
        P.V("dve", "tensor_scalar", omm[:], omm[:], -1.0, 1.0, ALU.mult, ALU.add, reads=[omm], writes=[omm])
        omka = A.sb("rw_omka", [128, 64])
        P.V("dve", "tensor_scalar", omka[:], kab[:], -1.0, 1.0, ALU.mult, ALU.add, reads=[kab], writes=[omka])
        ones1 = A.sb("rw_ones1", [128, 1])
        P.V("dve", "memset", ones1[:], 1.0, writes=[ones1])
        mask4 = []
        for d in range(2):
            m4 = A.sb("rw_mask4_%d" % d, [128, 4, 128])
            sbi = UPs if d == 0 else LOs
            ibi = UPi if d == 0 else LOi
            for q, mi in enumerate((sbi, ibi, sbi, ibi)):
                P.V("dve", "tensor_copy", m4[:, q, :], masks[:, mi, :], reads=[masks], writes=[m4])
            mask4.append(m4)
        Ysum = A.sb("rw_Ysum", [128, NTILE, 64]); Bsum = A.sb("rw_Bsum", [128, NTILE, 64]); Gall = A.sb("rw_Gall", [128, NTILE, 64])
        P.V("pool", "memset", Ysum[:], 0.0, writes=[Ysum]); P.V("pool", "memset", Bsum[:], 0.0, writes=[Bsum])
        Hs = [[A.sb("rw_H%d_%d" % (d, i), [128, 64]) for i in range(2)] for d in range(2)]
        for d in range(2):
            for i in range(2):
                P.V("dve", "memset", Hs[d][i][:], 0.0, writes=[Hs[d][i]])

        class WS:
            pass
        PSd = []
        for d in range(2):
            ps = WS()
            bk = [A.ps("rwp_bank%d_%d" % (d, i), [128, 512]) for i in range(4)]
            ps.a = bk[0].view("rwp_a", bk[0].t[:, 0:256]); ps.b = bk[0].view("rwp_b", bk[0].t[:, 256:384]); ps.c = bk[0].view("rwp_c", bk[0].t[:, 384:512])
            ps.d = bk[1]; ps.e = bk[2]
            ps.g = bk[3].view("rwp_g", bk[3].t[:, 0:256]); ps.h = bk[3].view("rwp_h", bk[3].t[:, 256:384]); ps.f = bk[3].view("rwp_f", bk[3].t[:, 384:512])
            PSd.append(ps)
        WSd = [[None, None], [None, None]]
        for d in range(2):
            for s_ in range(2):
                w = WS()
                n = "%d%d" % (d, s_)
                w.uin = A.sb("rw_uin" + n, [128, 4, 130]); w.usc = [A.sb("rw_us%d_" % c_ + n, [128, 128]) for c_ in range(4)]
                w.rkv = A.sb("rw_rkv" + n, [128, 3, 64])
                w.kk = A.sb("rw_kk" + n, [128, 64]); w.rrk = A.sb("rw_rrk" + n, [128, 64]); w.sm = A.sb("rw_sm" + n, [128, 4])
                w.t = [A.sb("rw_t%d_" % i + n, [128, 64]) for i in range(13)]
                w.sm2 = A.sb("rw_sm2" + n, [128, 1])
                w.M3 = A.sb("rw_M3" + n, [128, 3, 128])
                w.AQin = A.sb("rw_AQin" + n, [128, 128])
                w.F4 = A.sb("rw_F4" + n, [64, 4, 128])
                w.GM = A.sb("rw_GM" + n, [128, 4, 128])
                w.L = A.sb("rw_L" + n, [128, 128])
                w.XX = [A.sb("rw_XX%d_" % i + n, [128, 2, 128]) for i in range(2)]
                w.W = [A.sb("rw_W%d_" % i + n, [128, 128]) for i in range(2)]
                w.AQ = A.sb("rw_AQ" + n, [128, 128])
                w.RhT = A.sb("rw_RhT" + n, [128, 128]); w.GT = A.sb("rw_GT" + n, [64, 64]); w.F0 = A.sb("rw_F0" + n, [64, 64])
                w.gC = A.sb("rw_gC" + n, [64, 1])
                P.V("pool", "memset", w.RhT[:], 0.0, writes=[w.RhT])
                P.V("pool", "memset", w.M3[:], 0.0, writes=[w.M3])
                P.V("pool", "memset", w.AQin[:], 0.0, writes=[w.AQin])
                WSd[d][s_] = w

        def visit(d, vi, tile):
            w = WSd[d][vi % 2]
            ps = PSd[d]
            c0 = tile * 128
            H0 = Hs[d][vi % 2]; H1 = Hs[d][(vi + 1) % 2]
            vprev = c0 not in (0, NCTX)
            vnext = (c0 + 128) not in (NCTX, TT)
            lo = c0 - 1 if vprev else c0
            hi = c0 + 129 if vnext else c0 + 128
            if not vprev:
                P.V("pool", "memset", w.uin[:, :, 0:1], 0.0, writes=[w.uin])
            if not vnext:
                P.V("pool", "memset", w.uin[:, :, 129:130], 0.0, writes=[w.uin])
            P.dma("sp", w.uin[:, :, lo - (c0 - 1):hi - (c0 - 1)], ucv[:, :, lo:hi], writes=[w.uin])
            for c in range(4):
                P.act(w.usc[c][:], w.uin[:, c, 1:129], AF.Identity, reads=[w.uin, omm], writes=[w.usc[c]], scale=omm[:, c:c + 1])
            yield
            for c in range(4):
                P.V("dve", "scalar_tensor_tensor", w.usc[c][:], w.uin[:, c, 0:128], mup[:, c:c + 1], w.usc[c][:], ALU.mult, ALU.add,
                    reads=[w.uin, mup, w.usc[c]], writes=[w.usc[c]])
            for c in range(4):
                P.V("dve", "scalar_tensor_tensor", w.usc[c][:], w.uin[:, c, 2:130], mun[:, c:c + 1], w.usc[c][:], ALU.mult, ALU.add,
                    reads=[w.uin, mun, w.usc[c]], writes=[w.usc[c]])
            yield
            P.tr(ps.a[:, 0:128], w.usc[0][:], ident[:], reads=[w.usc[0], ident], writes=[ps.a])
            P.tr(ps.a[:, 128:256], w.usc[1][:], ident[:], reads=[w.usc[1], ident], writes=[ps.a])
            P.act(w.rkv[:, 0:2, :].rearrange("p a b -> p (a b)"), ps.a[:, 0:128], AF.Copy, reads=[ps.a], writes=[w.rkv])
            P.act(w.rkv[:, 2, :], ps.a[:, 192:256], AF.Copy, reads=[ps.a], writes=[w.rkv])
            rt = w.rkv[:, 0, :]; kt_ = w.rkv[:, 1, :]; vt = w.rkv[:, 2, :]
            P.act(w.usc[1][0:64, :], w.usc[1][0:64, :], AF.Sigmoid, reads=[w.usc[1]], writes=[w.usc[1]], scale=2.0)
            P.V("dve", "tensor_scalar", w.usc[1][0:64, :], w.usc[1][0:64, :], 2.0, -1.0, ALU.mult, ALU.add, reads=[w.usc[1]], writes=[w.usc[1]])
            if d == 0:
                P.act(w.usc[3][:], w.usc[3][:], AF.Sigmoid, reads=[w.usc[3]], writes=[w.usc[3]])
            P.mm(ps.b[:, 0:64], w.usc[1][0:64, :], wup[0:64, d, :], True, True, reads=[w.usc[1], wup], writes=[ps.b])
            P.mm(ps.b[:, 64:128], w.usc[2][0:64, :], aup[0:64, d, :], True, True, reads=[w.usc[2], aup], writes=[ps.b])
            yield
            zt, sg, lw, asig, t5, kd, bb_, ec, enc, ecx, t6, t7, t8 = w.t
            P.V("dve", "tensor_tensor", zt[:], ps.b[:, 0:64], w0b[:, d, :], ALU.add, reads=[ps.b, w0b], writes=[zt])
            P.V("dve", "tensor_tensor", asig[:], ps.b[:, 64:128], a0b[:, d, :], ALU.add, reads=[ps.b, a0b], writes=[asig])
            P.act(sg[:], zt[:], AF.Sigmoid, reads=[zt], writes=[sg])
            P.act(asig[:], asig[:], AF.Sigmoid, reads=[asig], writes=[asig])
            yield
            P.V("dve", "tensor_scalar_mul", lw[:], sg[:], -0.6065306597126334, reads=[sg], writes=[lw])
            IB = masks[:, UPi if d == 0 else LOi, :]
            P.mm(ps.c[:, 0:64], IB, lw[:], True, True, reads=[masks, lw], writes=[ps.c])
            P.mm(ps.c[0:64, 64:65], lw[:], ones1[:], True, True, reads=[lw, ones1], writes=[ps.c])
            if d == 0:
                P.mm(ps.f[:, 0:64], w.usc[3][:], gup[:], True, True, reads=[w.usc[3], gup], writes=[ps.f])
                P.V("dve", "tensor_copy", Gall[:, tile, :], ps.f[:, 0:64], reads=[ps.f], writes=[Gall])
            yield
            P.V("dve", "tensor_tensor", w.kk[:], kt_, kkb[:], ALU.mult, reads=[w.rkv, kkb], writes=[w.kk])
            P.act(t6[:], w.kk[:], AF.Square, reads=[w.kk], writes=[t6, w.sm], accum_out=w.sm[:, 0:1])
            yield
            P.V("dve", "tensor_scalar_max", w.sm[:, 0:1], w.sm[:, 0:1], 1e-24, reads=[w.sm], writes=[w.sm])
            P.act(w.sm[:, 0:1], w.sm[:, 0:1], AF.Sqrt, reads=[w.sm], writes=[w.sm])
            P.V("dve", "reciprocal", w.sm[:, 0:1], w.sm[:, 0:1], reads=[w.sm], writes=[w.sm])
            P.V("dve", "tensor_scalar_mul", w.kk[:], w.kk[:], w.sm[:, 0:1], reads=[w.kk, w.sm], writes=[w.kk])
            yield
            P.V("dve", "tensor_tensor", w.rrk[:], rt, rkb[:], ALU.mult, reads=[w.rkv, rkb], writes=[w.rrk])
            P.V("dve", "tensor_tensor", t5[:], asig[:], kab[:], ALU.mult, reads=[asig, kab, t5], writes=[t5])
            P.V("dve", "tensor_tensor", t5[:], t5[:], omka[:], ALU.add, reads=[t5, omka], writes=[t5])
            P.V("dve", "tensor_tensor", kd[:], t5[:], kt_, ALU.mult, reads=[t5, w.rkv], writes=[kd])
            yield
            P.V("dve", "tensor_tensor", bb_[:], w.kk[:], asig[:], ALU.mult, reads=[w.kk, asig], writes=[bb_])
            P.V("dve", "tensor_tensor", t7[:], w.rrk[:], kd[:], ALU.mult, reads=[w.rrk, kd], writes=[t7])
            P.V("dve", "reduce_sum", w.sm2[:, 0:1], t7[:], AX.X, reads=[t7], writes=[w.sm2])
            P.V("dve", "scalar_tensor_tensor", Bsum[:, tile, :], vt, w.sm2[:, 0:1], Bsum[:, tile, :], ALU.mult, ALU.add,
                reads=[w.rkv, w.sm2, Bsum], writes=[Bsum])
            yield
            P.act(ec[:], ps.c[:, 0:64], AF.Exp, reads=[ps.c], writes=[ec])
            P.act(enc[:], ps.c[:, 0:64], AF.Exp, reads=[ps.c], writes=[enc], scale=-1.0)
            P.V("dve", "tensor_tensor", ecx[:], ps.c[:, 0:64], lw[:], ALU.subtract, reads=[ps.c, lw], writes=[ecx])
            P.act(ecx[:], ecx[:], AF.Exp, reads=[ecx], writes=[ecx])
            P.act(w.gC[:], ps.c[0:64, 64:65], AF.Exp, reads=[ps.c], writes=[w.gC])
            yield
            P.V("dve", "tensor_tensor", w.M3[:, 0, 0:64], rt, ec[:], ALU.mult, reads=[w.rkv, ec], writes=[w.M3])
            P.V("dve", "tensor_tensor", w.M3[:, 1, 0:64], bb_[:], enc[:], ALU.mult, reads=[bb_, enc], writes=[w.M3])
            P.V("dve", "tensor_tensor", w.M3[:, 2, 0:64], kd[:], enc[:], ALU.mult, reads=[kd, enc], writes=[w.M3])
            P.V("dve", "scalar_tensor_tensor", w.AQin[:, 0:64], w.kk[:], -1.0, ecx[:], ALU.mult, ALU.mult, reads=[w.kk, ecx], writes=[w.AQin])
            yield
            P.mm(ps.d[0:64, 0:128], w.AQin[:, 0:64], ident[:], True, True, reads=[w.AQin, ident], writes=[ps.d])
            P.mm(ps.d[0:64, 128:256], w.M3[:, 0, 0:64], ident[:], True, True, reads=[w.M3, ident], writes=[ps.d])
            P.mm(ps.d[0:64, 256:384], w.M3[:, 1, 0:64], ident[:], True, True, reads=[w.M3, ident], writes=[ps.d])
            P.mm(ps.d[0:64, 384:512], w.M3[:, 2, 0:64], ident[:], True, True, reads=[w.M3, ident], writes=[ps.d])
            P.act(w.F4[:].rearrange("p a b -> p (a b)"), ps.d[0:64, :], AF.Copy, reads=[ps.d], writes=[w.F4])
            yield
            AT = w.F4[:, 0, :]; RT = w.F4[:, 1, :]; BT = w.F4[:, 2, :]; KT_ = w.F4[:, 3, :]
            AR = w.F4[:, 0:2, :].rearrange("p a b -> p (a b)")
            P.mm(ps.e[:, 0:256], BT, AR, True, True, reads=[w.F4], writes=[ps.e])
            P.mm(ps.e[:, 256:512], KT_, AR, True, True, reads=[w.F4], writes=[ps.e])
            P.mm(ps.f[:, 0:128], AT, BT, True, True, reads=[w.F4], writes=[ps.f])
            yield
            P.V("dve", "tensor_tensor", w.GM[:].rearrange("p a b -> p (a b)"), ps.e[:, :], mask4[d][:].rearrange("p a b -> p (a b)"), ALU.mult,
                reads=[ps.e, mask4[d]], writes=[w.GM])
            SBT = masks[:, LOs if d == 0 else UPs, :]
            P.V("dve", "tensor_tensor", w.L[:], ps.f[:, 0:128], SBT, ALU.mult, reads=[ps.f, masks], writes=[w.L])
            LT = w.GM[:, 0, :]; MrbT = w.GM[:, 1, :]; LakT = w.GM[:, 2, :]; MrkT = w.GM[:, 3, :]
            P.V("dve", "tensor_tensor", w.W[0][:], LT, ident[:], ALU.add, reads=[w.GM, ident], writes=[w.W[0]])
            yield
            P.mm(ps.a[:, 128:192], LakT, vt, True, True, reads=[w.GM, w.rkv], writes=[ps.a])
            P.act(w.AQin[:, 64:128], ps.a[:, 128:192], AF.Copy, reads=[ps.a], writes=[w.AQin])
            X = w.L[:]; XT = LT
            wi = 0
            NIT = 6
            for it in range(NIT):
                xx = w.XX[it % 2]
                P.mm(ps.g[:, 0:128], XT, X, True, True, reads=[w.GM, w.L, w.XX[(it + 1) % 2]], writes=[ps.g])
                if it < NIT - 1:
                    P.mm(ps.g[:, 128:256], X, XT, True, True, reads=[w.GM, w.L, w.XX[(it + 1) % 2]], writes=[ps.g])
                    P.act(xx[:].rearrange("p a b -> p (a b)"), ps.g[:, :], AF.Copy, reads=[ps.g], writes=[xx])
                else:
                    P.act(xx[:, 0, :], ps.g[:, 0:128], AF.Copy, reads=[ps.g], writes=[xx])
                X = xx[:, 0, :]; XT = xx[:, 1, :]
                P.mm(ps.h[:, 0:128], X, w.W[wi][:], True, True, reads=[xx, w.W[wi]], writes=[ps.h])
                P.V("dve", "tensor_tensor", w.W[1 - wi][:], ps.h[:, 0:128], w.W[wi][:], ALU.add, reads=[ps.h, w.W[wi]], writes=[w.W[1 - wi]])
                wi = 1 - wi
                yield
            Wf = w.W[wi]
            P.mm(ps.a[:, 0:128], Wf[:], w.AQin[:], True, True, reads=[Wf, w.AQin], writes=[ps.a])
            P.act(w.AQ[:], ps.a[:, 0:128], AF.Copy, reads=[ps.a], writes=[w.AQ])
            Abar = w.AQ[:, 0:64]; Qm = w.AQ[:, 64:128]
            yield
            P.mm(ps.f[0:64, 0:128], Abar, MrbT, True, True, reads=[w.AQ, w.GM], writes=[ps.f])
            P.V("dve", "tensor_tensor", w.RhT[0:64, :], ps.f[0:64, 0:128], RT, ALU.add, reads=[ps.f, w.F4], writes=[w.RhT])
            yield
            P.mm(ps.b[0:64, 0:64], Abar, w.M3[:, 1, 0:64], True, True, reads=[w.AQ, w.M3], writes=[ps.b])
            P.mm(ps.b[0:64, 64:128], w.M3[:, 1, 0:64], Qm, True, False, reads=[w.AQ, w.M3], writes=[ps.b])
            P.mm(ps.b[0:64, 64:128], w.M3[:, 2, 0:64], vt, False, True, reads=[w.M3, w.rkv], writes=[ps.b])
            P.V("dve", "tensor_tensor", w.GT[:], ps.b[0:64, 0:64], ident[0:64, 0:64], ALU.add, reads=[ps.b, ident], writes=[w.GT])
            P.V("dve", "tensor_scalar_mul", w.F0[:], ps.b[0:64, 64:128], w.gC[:, 0:1], reads=[ps.b, w.gC], writes=[w.F0])
            yield
            P.mm(ps.a[:, 192:256], MrbT, Qm, True, False, reads=[w.GM, w.AQ], writes=[ps.a])
            P.mm(ps.a[:, 192:256], MrkT, vt, False, False, reads=[w.GM, w.rkv], writes=[ps.a])
            P.mm(ps.a[:, 192:256], w.RhT[:], H0[:], False, True, reads=[w.RhT, H0], writes=[ps.a])
            P.V("dve", "tensor_tensor", Ysum[:, tile, :], ps.a[:, 192:256], Ysum[:, tile, :], ALU.add, reads=[ps.a, Ysum], writes=[Ysum])
            P.mm(ps.c[0:64, 0:64], w.GT[:], H0[0:64, :], True, True, reads=[w.GT, H0], writes=[ps.c])
            P.V("dve", "scalar_tensor_tensor", H1[0:64, :], ps.c[0:64, 0:64], w.gC[:, 0:1], w.F0[:], ALU.mult, ALU.add,
                reads=[ps.c, w.gC, w.F0], writes=[H1])
            yield

        order = [list(range(NTILE)), [1, 0] + list(range(NTILE - 1, 1, -1))]
        import os
        for vi in range(int(os.environ.get('RWKV_NV', NTILE))):
            gens = [visit(d, vi, order[d][vi]) for d in range(2)]
            alive = [True, True]
            nsteps = [0, 0]
            while any(alive):
                for d in range(2):
                    if alive[d]:
                        try:
                            next(gens[d])
                            nsteps[d] += 1
                            if nsteps[d] >= RWKV_STOP:
                                alive[d] = False
                        except StopIteration:
                            alive[d] = False
        for _ in range(int(os.environ.get("RWKV_JUNKPE", "0"))):
            P.tr(PSd[0].d[:, 0:128], WSd[0][0].AQin[:, :], ident[:], reads=[WSd[0][0].AQin, ident], writes=[PSd[0].d])
        junk = A.sb("rw_junk", [128, 64])
        for _ in range(int(os.environ.get("RWKV_JUNK", "0"))):
            P.V("dve", "memset", junk[:], 0.0, writes=[junk])
        fo = [A.sb("rw_fo%d" % i, [128, 64]) for i in range(2)]
        fof = [A.sb("rw_fof%d" % i, [64, 128]) for i in range(2)]
        fs = [A.sb("rw_fs%d" % i, [128, 4]) for i in range(2)]
        ft = [A.sb("rw_ft%d" % i, [128, 64]) for i in range(2)]
        epsg = A.sb("rw_epsg", [128, 1])
        P.V("dve", "memset", epsg[:], GN_EPS_F, writes=[epsg])
        for tile in range(NTILE):
            o = fo[tile % 2]; s_ = fs[tile % 2]; t_ = ft[tile % 2]
            P.V("dve", "reduce_sum", s_[:, 0:1], Ysum[:, tile, :], AX.X, reads=[Ysum], writes=[s_])
            P.V("dve", "tensor_scalar_mul", s_[:, 0:1], s_[:, 0:1], 1.0 / 64.0, reads=[s_], writes=[s_])
            P.V("dve", "tensor_scalar", o[:], Ysum[:, tile, :], s_[:, 0:1], None, ALU.subtract, reads=[Ysum, s_], writes=[o])
            P.act(t_[:], o[:], AF.Square, reads=[o], writes=[t_, s_], accum_out=s_[:, 1:2])
            P.act(s_[:, 1:2], s_[:, 1:2], AF.Sqrt, reads=[s_, epsg], writes=[s_], scale=1.0 / 64.0, bias=epsg[:, 0:1])
            P.V("dve", "reciprocal", s_[:, 1:2], s_[:, 1:2], reads=[s_], writes=[s_])
            P.V("dve", "scalar_tensor_tensor", o[:], o[:], s_[:, 1:2], gng[:], ALU.mult, ALU.mult, reads=[o, s_, gng], writes=[o])
            P.V("dve", "tensor_tensor", o[:], o[:], gnb[:], ALU.add, reads=[o, gnb], writes=[o])
            P.V("dve", "tensor_tensor", o[:], o[:], Bsum[:, tile, :], ALU.add, reads=[o, Bsum], writes=[o])
            P.V("dve", "tensor_tensor", o[:], o[:], Gall[:, tile, :], ALU.mult, reads=[o, Gall], writes=[o])
            pT = PSd[tile % 2].d
            P.mm(pT[0:64, 0:128], o[:], ident[:], True, True, reads=[o, ident], writes=[pT])
            of = fof[tile % 2]
            P.act(of[:], pT[0:64, 0:128], AF.Copy, reads=[pT], writes=[of])
            for (dst, off, ww) in io["ypieces"]("yc", tile * 128, 128):
                P.dma("sp", dst, of[:, off:off + ww], reads=[of], is_output=True)


def _fm(v, n):
    return np.ascontiguousarray(np.asarray(v, np.float32).reshape(n, 128).T)


def rope_tables():
    t = np.arange(8192, dtype=np.int32)
    rows = (t // 64).astype(np.float32)
    cols = (t % 64).astype(np.float32)
    n_freq = 8
    inv_freq = (np.float32(10000.0) ** (-np.arange(n_freq, dtype=np.float32) / np.float32(n_freq))).astype(np.float32)
    ang = np.stack([rows[:, None] * inv_freq, cols[:, None] * inv_freq], axis=1)
    ang = np.concatenate([ang, ang], axis=-1).reshape(8192, 32).astype(np.float32)
    C = np.ones((96, 8192), np.float32)
    S_ = np.zeros((96, 8192), np.float32)
    C[64:96] = np.cos(ang).T
    S_[64:96] = np.sin(ang).T
    R = np.zeros((96, 96), np.float32)
    for a in range(2):
        for f in range(8):
            i0 = 64 + a * 16 + f
            i1 = 64 + a * 16 + 8 + f
            R[i1, i0] = -1.0
            R[i0, i1] = 1.0
    return np.ascontiguousarray(C), np.ascontiguousarray(S_), R


def rwkv_masks():
    i = np.arange(128)[:, None]
    t = np.arange(128)[None, :]
    m = np.stack([(i < t), (i > t), (i <= t), (i >= t)], axis=1).astype(np.float32)
    return np.ascontiguousarray(m)


_CONSTS = {}


def consts():
    if not _CONSTS:
        C, S_, R = rope_tables()
        _CONSTS.update(rope_C=C, rope_S=S_, rope_R=R, rw_masks=rwkv_masks())
    return _CONSTS


def mixer_inputs(inp, l, j, uT, which=("conv", "lru", "rwkv", "mla")):
    G = 256
    m = {}
    rep = lambda v: np.ascontiguousarray(np.broadcast_to(np.asarray(v, np.float32)[None, :], (128, len(v))))
    if "conv" in which:
        lat = uT[0:512, NCTX:]
        ctx = uT[0:512, :NCTX]
        pl = np.zeros((512, 8192 + 30), np.float32); pl[:, 15:15 + 8192] = lat
        pc = np.zeros((512, 256 + 30), np.float32); pc[:, 15:15 + 256] = ctx
        m["ua_lat"] = np.ascontiguousarray(pl[:, 2048 * j:2048 * j + 2048 + 30])
        m["ua_ctx"] = np.ascontiguousarray(pc[:, 64 * j:64 * j + 64 + 30])
        w = np.asarray(inp["cv_dw_w"][l], np.float32)
        m["cv_w"] = np.ascontiguousarray(w.T.reshape(2, 128, CONVK).transpose(1, 0, 2))
        m["cv_b"] = _fm(inp["cv_dw_b"][l], 2); m["cv_lng"] = _fm(inp["cv_ln_g"][l], 2); m["cv_lnb"] = _fm(inp["cv_ln_b"][l], 2)
    hs = slice(64 * j, 64 * j + 64)
    if "lru" in which:
        m["ub"] = np.ascontiguousarray(np.concatenate([uT[512 + 64 * j:512 + 64 * j + 64], uT[768 + 64 * j:768 + 64 * j + 64]], 0))
        cw = np.asarray(inp["lru_conv_w"][l], np.float32)[:, hs].T
        m["lru_cw"] = np.ascontiguousarray(np.concatenate([cw, cw], 0))
        cb = np.asarray(inp["lru_conv_b"][l], np.float32)[hs]
        m["lru_cb"] = np.ascontiguousarray(np.concatenate([cb, cb])[:, None])
        m["lru_wa"] = np.ascontiguousarray(np.concatenate([inp["lru_wa"][l, 0, j], inp["lru_wa"][l, 1, j]], 1).astype(np.float32))
        m["lru_wx"] = np.ascontiguousarray(np.concatenate([inp["lru_wx"][l, 0, j], inp["lru_wx"][l, 1, j]], 1).astype(np.float32))
        for nm, src in (("lru_ba", "lru_ba"), ("lru_bx", "lru_bx"), ("lru_lam", "lru_lambda")):
            v = np.asarray(inp[src][l], np.float32)
            m[nm] = np.ascontiguousarray(np.concatenate([v[0, hs], v[1, hs]])[:, None])
    if "rwkv" in which:
        base = 1024
        rows = np.concatenate([np.arange(base + 64 * j, base + 64 * j + 64), np.arange(base + 256 + 64 * j, base + 256 + 64 * j + 64),
                               np.arange(base + 768, base + 832), np.arange(base + 512 + 64 * j, base + 512 + 64 * j + 64),
                               np.arange(base + 832, base + 896)])
        ucm = np.zeros((512, TT), np.float32)
        ucm[0:320] = uT[rows]
        ucm[384:512] = uT[base + 896:base + 1024]
        m["uc"] = ucm
        cidx = np.concatenate([rows - base, -np.ones(64, np.int64), np.arange(896, 1024)])
        for nm, src in (("rw_mup", "rwkv_mu_prev"), ("rw_mun", "rwkv_mu_next")):
            v = np.asarray(inp[src][l], np.float32)
            full = np.where(cidx >= 0, v[np.maximum(cidx, 0)], 0.0).astype(np.float32)
            m[nm] = _fm(full, 4)
        wup = np.zeros((128, 2, 64), np.float32); aup = np.zeros((128, 2, 64), np.float32)
        for d in range(2):
            wup[0:64, d] = inp["rwkv_w_up"][l, d][:, hs]
            aup[0:64, d] = inp["rwkv_a_up"][l, d][:, hs]
        m["rw_wup"] = wup; m["rw_aup"] = aup
        m["rw_gup"] = np.ascontiguousarray(np.asarray(inp["rwkv_g_up"][l], np.float32)[:, hs])
        m["rw_w0b"] = np.ascontiguousarray(np.stack([rep(inp["rwkv_w0"][l, d, hs]) for d in range(2)], 1))
        m["rw_a0b"] = np.ascontiguousarray(np.stack([rep(inp["rwkv_a0"][l, d, hs]) for d in range(2)], 1))
        m["rw_kkb"] = rep(inp["rwkv_k_k"][l, hs]); m["rw_kab"] = rep(inp["rwkv_k_a"][l, hs]); m["rw_rkb"] = rep(inp["rwkv_r_k"][l, j])
        m["rw_gng"] = rep(inp["rwkv_gn_g"][l, hs]); m["rw_gnb"] = rep(inp["rwkv_gn_b"][l, hs])
        m["rw_masks"] = consts()["rw_masks"]
    if "mla" in which:
        m["ud"] = np.ascontiguousarray(uT[2048:2464])
        m["mla_wuq"] = np.ascontiguousarray(np.asarray(inp["mla_w_uq"][l], np.float32)[:, 96 * j:96 * j + 96])
        wkv = np.asarray(inp["mla_w_ukv"][l], np.float32)
        m["mla_wkn"] = np.ascontiguousarray(wkv[:, 128 * j:128 * j + 64]); m["mla_wv"] = np.ascontiguousarray(wkv[:, 128 * j + 64:128 * j + 128])
        m["mla_qn"] = _fm(inp["mla_q_norm"][l], 2); m["mla_kvn"] = _fm(inp["mla_kv_norm"][l], 1)
        for k in ("rope_R", "rope_C", "rope_S"):
            m[k] = consts()[k]
    return m


U32 = mybir.dt.uint32
CHU = 122
NCHU = 16
CHY = 96
NCHY = 8


def ugrow(R, q):
    return (R // CHU) * (4 * CHU) + q * CHU + (R % CHU)


def ygrow(R, q):
    return (R // CHY) * (4 * CHY) + q * CHY + (R % CHY)
NIDX = 26
GROUPS = [[0, 1, 2, 3], [4, 5, 6, 7]]

MIXER_CONSTS = [
    ("cv_w", [128, 2, CONVK]), ("cv_b", [128, 2]), ("cv_lng", [128, 2]), ("cv_lnb", [128, 2]),
    ("lru_cw", [128, 4]), ("lru_cb", [128, 1]), ("lru_wa", [64, 128]), ("lru_wx", [64, 128]),
    ("lru_ba", [128, 1]), ("lru_bx", [128, 1]), ("lru_lam", [128, 1]),
    ("mla_wuq", [256, 96]), ("mla_wkn", [128, 64]), ("mla_wv", [128, 64]), ("mla_qn", [128, 2]), ("mla_kvn", [128, 1]),
    ("rw_mup", [128, 4]), ("rw_mun", [128, 4]), ("rw_wup", [128, 2, 64]), ("rw_aup", [128, 2, 64]), ("rw_gup", [128, 64]),
    ("rw_w0b", [128, 2, 64]), ("rw_a0b", [128, 2, 64]), ("rw_kkb", [128, 64]), ("rw_kab", [128, 64]), ("rw_rkb", [128, 64]),
    ("rw_gng", [128, 64]), ("rw_gnb", [128, 64]),
]
SHARED_CONSTS = [("rope_R", [96, 96]), ("rope_C", [96, 8192]), ("rope_S", [96, 8192]), ("rw_masks", [128, 4, 128])]


def gather_indices(j):
    p = np.arange(128)
    lo = p < 64
    r = np.where(lo, p, p - 64)
    g = np.zeros((128, NIDX), np.uint32)
    for q in range(4):
        g[:, q] = np.where(lo, ugrow(64 * j + r, q), ugrow(256 + 64 * j + r, q))
        g[:, 4 + q] = np.where(lo, ugrow(512 + 64 * j + r, q), ugrow(768 + 64 * j + r, q))
        g[:, 8 + q] = np.where(lo, ugrow(1280 + r, q), ugrow(1024 + 64 * j + r, q))
    for k in range(4):
        g[:, 12 + k] = ((j - 1) if j > 0 else 4) * 512 + 128 * k + p
        g[:, 16 + k] = ((j + 1) if j < 3 else 4) * 512 + 128 * k + p
    for m in range(3):
        for h in range(2):
            g[:, 20 + 2 * m + h] = np.where(lo, ygrow(j * 192 + m * 64 + r, 2 * h), ygrow(j * 192 + m * 64 + r, 2 * h + 1))
    return g


def emit_exchange_u(P, nc, G, l, u_l, X):
    it, zero, scr = X["it"], X["zero"], X["scr"]
    hs, hgx, ug = X["hs"], X["hgx"], X["ug"]
    ua_lat, ua_ctx, ub, uc, ud = X["ua_lat"], X["ua_ctx"], X["ub"], X["uc"], X["ud"]
    b_hs = Buf("b_hs"); b_hgx = Buf("b_hgx"); b_ug = Buf("b_ug"); b_dst = Buf("b_dst")
    for (d0, s0) in ((0, 0), (15, 2033), (30, 2048), (45, 2097)):
        P.dma("sp", hs[:, d0:d0 + 15], u_l[0:512, s0:s0 + 15], writes=[b_hs])
    for k in range(4):
        P.dma("sp", hgx[2048 + 128 * k:2048 + 128 * (k + 1), :], zero[:, 0:60], reads=[zero], writes=[b_hgx])
    P.cc("AllGather", [hs], [hgx[0:2048, :]], GROUPS, scr, reads=[b_hs], writes=[b_hgx])
    P.dma("sp", ua_lat[:, 15:2063], u_l[0:512, 0:2048], writes=[b_dst])
    P.dma("sp", ua_ctx[:, 15:79], u_l[0:512, 2048:2112], writes=[b_dst])
    with Scope(P) as A0:
        hl = [A0.sb("xh%d" % i, [128, 60]) for i in range(2)]
        n = 0
        for k in range(4):
            for side in range(2):
                t = hl[n % 2]
                n += 1
                col = 12 + 4 * side + k
                P.op("pool", lambda e, t=t, col=col: e.indirect_dma_start(out=t[:], out_offset=None, in_=hgx,
                     in_offset=bass.IndirectOffsetOnAxis(ap=it[:, col:col + 1], axis=0)), reads=[b_hgx, it], writes=[t], dma=True)
                rs = slice(128 * k, 128 * (k + 1))
                if side == 0:
                    P.dma("sp", ua_lat[rs, 0:15], t[:, 15:30], reads=[t], writes=[b_dst])
                    P.dma("sp", ua_ctx[rs, 0:15], t[:, 45:60], reads=[t], writes=[b_dst])
                else:
                    P.dma("sp", ua_lat[rs, 2063:2078], t[:, 0:15], reads=[t], writes=[b_dst])
                    P.dma("sp", ua_ctx[rs, 79:94], t[:, 30:45], reads=[t], writes=[b_dst])
    P.barrier()
    if X.get("conv_cb") is not None:
        X["conv_cb"]()
    for c in range(NCHU):
        P.cc("AllGather", [u_l[512 + CHU * c:512 + CHU * (c + 1), :]], [ug[4 * CHU * c:4 * CHU * (c + 1), :]], GROUPS, scr, writes=[b_ug])
    with Scope(P) as A:
        gt = [A.sb("xg%d" % i, [128, 2112]) for i in range(2)]
        n = 0
        for q in range(4):
            lat = slice(NCTX + 2048 * q, NCTX + 2048 * (q + 1))
            ctx = slice(64 * q, 64 * (q + 1))
            for (dst, r0, col) in ((ub, 0, q), (uc, 0, 4 + q), (uc, 128, 8 + q)):
                t = gt[n % 2]
                n += 1
                P.op("pool", lambda e, t=t, col=col: e.indirect_dma_start(out=t[:], out_offset=None, in_=ug,
                     in_offset=bass.IndirectOffsetOnAxis(ap=it[:, col:col + 1], axis=0)), reads=[b_ug, it], writes=[t], dma=True)
                P.dma("sp", dst[r0:r0 + 128, lat], t[:, 0:2048], reads=[t], writes=[b_dst])
                P.dma("sp", dst[r0:r0 + 128, ctx], t[:, 2048:2112], reads=[t], writes=[b_dst])
            for (dst, d0, R0, n_) in ((uc, 256, 1344, 64), (uc, 384, 1408, 128), (ud, 0, 1536, 416)):
                done = 0
                while done < n_:
                    R = R0 + done
                    cnt = min(n_ - done, CHU - R % CHU)
                    srow = ugrow(R, q)
                    P.dma("sp", dst[d0 + done:d0 + done + cnt, lat], ug[srow:srow + cnt, 0:2048], reads=[b_ug], writes=[b_dst])
                    P.dma("sp", dst[d0 + done:d0 + done + cnt, ctx], ug[srow:srow + cnt, 2048:2112], reads=[b_ug], writes=[b_dst])
                    done += cnt
            P.dma("sp", uc[320:384, 2112 * q:2112 * (q + 1)], zero[0:64, :], reads=[zero], writes=[b_dst])


def emit_exchange_y(P, nc, G, l, X):
    it, scr = X["it"], X["scr"]
    ys, yg, ya, yT = X["ys"], X["yg"], X["ya"], X["yT"]
    b_yg = Buf("b_yg"); b_dst = Buf("b_ydst")
    ysf = ys.rearrange("d r t -> (d r) t")
    for c in range(NCHY):
        P.cc("AllGather", [ysf[CHY * c:CHY * (c + 1), :]], [yg[4 * CHY * c:4 * CHY * (c + 1), :]], GROUPS, scr, writes=[b_yg])
    P.dma("sp", yT[0:256, :], ya, writes=[b_dst])
    with Scope(P) as A:
        gt = [A.sb("yg%d" % i, [128, 2112]) for i in range(2)]
        n = 0
        for m in range(3):
            for h in range(2):
                t = gt[n % 2]
                n += 1
                col = 20 + 2 * m + h
                P.op("pool", lambda e, t=t, col=col: e.indirect_dma_start(out=t[:], out_offset=None, in_=yg,
                     in_offset=bass.IndirectOffsetOnAxis(ap=it[:, col:col + 1], axis=0)), reads=[b_yg, it], writes=[t], dma=True)
                r0 = 256 * (m + 1) + 128 * h
                P.dma("sp", yT[r0:r0 + 128, :], t[:], reads=[t], writes=[b_dst])


def build_fused():
    nc = bass.Bass("TRN2", target_bir_lowering=False)
    P = new_prog(nc)

    def din(name, shape, dt=F32):
        return nc.dram_tensor(name, list(shape), dt, kind="ExternalInput").ap()

    def dint(name, shape):
        return nc.dram_tensor(name, list(shape), F32).ap()

    xT_in = din("xT", [D, 2112]); cT = din("cT", [128, KC, 2]); gidx = din("gidx", [128, NIDX], U32); fin_g = din("fin_g", [128, KC])
    xfin = nc.dram_tensor("xfin", [D, 2048], F32, kind="ExternalOutput").ap()
    W = []
    for l in range(2):
        w = {"ada_w": din("ada_w_%d" % l, [D, NMOD * D]), "ada_b": din("ada_b_%d" % l, [128, NMOD * KC]),
             "f1w13": din("f1w13_%d" % l, [D, 2 * DFF]), "f1w2": din("f1w2_%d" % l, [DFF, D]), "w_in": din("w_in_%d" % l, [D, INC]),
             "w_out": din("w_out_%d" % l, [D, D]), "f2w13": din("f2w13_%d" % l, [D, 2 * DFF]), "f2w2": din("f2w2_%d" % l, [DFF, D])}
        for (nm, shp) in MIXER_CONSTS:
            w[nm] = din("%s_%d" % (nm, l), shp)
        W.append(w)
    SH = {nm: din(nm, shp) for (nm, shp) in SHARED_CONSTS}
    X = {"hs": dint("hs", [512, 60]), "hgx": dint("hgx", [5 * 512, 60]), "ug": dint("ug", [NCHU * 4 * CHU, 2112]),
         "ua_lat": dint("ua_lat", [512, 2078]), "ua_ctx": dint("ua_ctx", [512, 94]), "ub": dint("ub", [128, TT]),
         "uc": dint("uc", [512, TT]), "ud": dint("ud", [416, TT]),
         "ys": dint("ys", [4, 192, 2112]), "yg": dint("yg", [NCHY * 4 * CHY, 2112]), "ya": dint("ya", [256, 2112]), "yT": dint("yT", [D, 2112])}
    ys = X["ys"]

    def ypieces(name, c0, w):
        m = {"yb": 0, "yc": 1, "yd": 2}[name]
        rows = slice(64 * m, 64 * (m + 1))
        if c0 >= NCTX:
            q, col = divmod(c0 - NCTX, 2048)
            assert col + w <= 2048
            return [(ys[q, rows, col:col + w], 0, w)]
        out = []
        for q in range(4):
            lo = max(c0, 64 * q); hi = min(c0 + w, 64 * (q + 1))
            if hi > lo:
                out.append((ys[q, rows, 2048 + lo - 64 * q:2048 + hi - 64 * q], lo - c0, hi - lo))
        return out

    with Scope(P) as G:
        it = G.sb("gidx_sb", [128, NIDX], U32)
        P.dma("sp", it[:], gidx, writes=[it])
        zero = G.sb("zero_sb", [128, 2112])
        P.V("pool", "memset", zero[:], 0.0, writes=[zero])
        scr = G.sb("cc_scr", [128, 1])
        X.update(it=it, zero=zero, scr=scr)
        xsrc = xT_in
        for l in range(2):
            last = l == 1
            w = W[l]
            x1 = dint("x1_%d" % l, [D, 2112]); u_l = dint("u_%d" % l, [INC, 2112]); mod_l = dint("mod_%d" % l, [128, NMOD * KC, 2])
            T = {"xT": xsrc, "w13": w["f1w13"], "w2": w["f1w2"], "xnew": x1, "cT": cT, "ada_w": w["ada_w"], "ada_b": w["ada_b"],
                 "w_in": w["w_in"], "modo": mod_l, "uT": u_l}
            emit_stage(P, nc, "P", 2048, 64, False, T)
            P.barrier()
            io = {nm: w[nm] for (nm, _) in MIXER_CONSTS}
            io.update(SH)
            io.update(ua_lat=X["ua_lat"], ua_ctx=X["ua_ctx"], ub=X["ub"], uc=X["uc"], ud=X["ud"], ya=X["ya"], ypieces=ypieces)
            X["conv_cb"] = lambda io=io, last=last: emit_mixers(P, nc, io, not last, which=("conv",))
            emit_exchange_u(P, nc, G, l, u_l, X)
            P.barrier()
            emit_mixers(P, nc, io, not last, which=("lru", "rwkv", "mla"))
            P.barrier()
            emit_exchange_y(P, nc, G, l, X)
            P.barrier()
            nctx = 0 if last else 64
            NTq = 2048 + nctx
            x2 = dint("x2_%d" % l, [D, NTq]); xmid = dint("xmid_%d" % l, [D, NTq])
            T = {"xT": x1[:, 0:NTq], "w13": w["f2w13"], "w2": w["f2w2"], "xnew": x2, "modi": mod_l, "yT": X["yT"][:, 0:NTq],
                 "w_out": w["w_out"], "xmid": xmid}
            if last:
                T.update(fin_g=fin_g, xfin=xfin)
            emit_stage(P, nc, "Q", 2048, nctx, last, T)
            P.barrier()
            xsrc = x2
        P.emit()
    return nc


def fused_inputs(inp, i):
    b, j = divmod(i, 4)
    x = np.asarray(inp["x"], np.float32); ctx = np.asarray(inp["ctx"], np.float32)
    m = {"xT": np.ascontiguousarray(np.concatenate([x[b, 2048 * j:2048 * (j + 1)], ctx[b, 64 * j:64 * (j + 1)]], 0).T),
         "cT": np.ascontiguousarray(np.stack([_fm(inp["c"][b], 8), _fm(inp["c_ctx"], 8)], -1)),
         "gidx": gather_indices(j), "fin_g": _fm(inp["final_norm"], KC)}
    dummy = np.zeros((INC, TT), np.float32)
    for l in range(2):
        m["ada_w_%d" % l] = np.ascontiguousarray(inp["ada_w"][l], np.float32)
        m["ada_b_%d" % l] = _fm(inp["ada_b"][l], NMOD * KC)
        m["f1w13_%d" % l] = np.ascontiguousarray(inp["ffn1_w13"][l], np.float32)
        m["f1w2_%d" % l] = np.ascontiguousarray(inp["ffn1_w2"][l], np.float32)
        m["w_in_%d" % l] = np.ascontiguousarray(inp["w_in"][l], np.float32)
        m["w_out_%d" % l] = np.ascontiguousarray(inp["w_out"][l], np.float32)
        m["f2w13_%d" % l] = np.ascontiguousarray(inp["ffn2_w13"][l], np.float32)
        m["f2w2_%d" % l] = np.ascontiguousarray(inp["ffn2_w2"][l], np.float32)
        mi = mixer_inputs(inp, l, j, dummy)
        for (nm, _) in MIXER_CONSTS:
            m["%s_%d" % (nm, l)] = mi[nm]
    for (nm, _) in SHARED_CONSTS:
        m[nm] = consts()[nm]
    return m


def kernel_fused(**inputs):
    inp = {k: np.asarray(v) for k, v in inputs.items()}
    nc = _prog("F", build_fused)
    maps = [fused_inputs(inp, i) for i in range(8)]
    res = _run(nc, maps)
    out = np.zeros((2, 8192, 1024), np.float32)
    for i in range(8):
        b, q = divmod(i, 4)
        out[b, 2048 * q:2048 * (q + 1)] = res[i]["xfin"].T
    return out


_PROGS = {}


def _prog(key, fn):
    if key not in _PROGS:
        _PROGS[key] = fn()
    return _PROGS[key]


def _run(nc, in_maps):
    res = run_bass_kernel_spmd(nc, in_maps, core_ids=list(range(len(in_maps))))
    return res.results


def kernel_unfused(**inputs):
    inp = {k: np.asarray(v) for k, v in inputs.items()}
    NCORE = 8
    x = inp["x"].astype(np.float32)
    ctx = inp["ctx"].astype(np.float32)
    L = 2
    xT = []
    for i in range(NCORE):
        b, q = divmod(i, 4)
        xT.append(np.ascontiguousarray(np.concatenate([x[b, 2048 * q:2048 * (q + 1)], ctx[b, 64 * q:64 * (q + 1)]], 0).T))
    cT = [np.ascontiguousarray(np.stack([_fm(inp["c"][i // 4], 8), _fm(inp["c_ctx"], 8)], -1)) for i in range(NCORE)]
    out = None
    for l in range(L):
        last = l == L - 1
        ncP = _prog("P", lambda: build_stage("P", 2048, 64))
        maps = [{"xT": xT[i], "cT": cT[i], "ada_w": np.ascontiguousarray(inp["ada_w"][l], np.float32),
                 "ada_b": _fm(inp["ada_b"][l], NMOD * KC), "w13": np.ascontiguousarray(inp["ffn1_w13"][l], np.float32),
                 "w2": np.ascontiguousarray(inp["ffn1_w2"][l], np.float32), "w_in": np.ascontiguousarray(inp["w_in"][l], np.float32)}
                for i in range(NCORE)]
        rP = _run(ncP, maps)
        x1T = [r["xnew"] for r in rP]
        mods = [r["modo"] for r in rP]
        uT_full = []
        for b in range(2):
            cs = [rP[4 * b + q]["uT"] for q in range(4)]
            uT_full.append(np.ascontiguousarray(np.concatenate([c_[:, 2048:] for c_ in cs] + [c_[:, :2048] for c_ in cs], 1)))
        ncM = _prog(("M", not last), lambda: build_mixer(not last))
        maps = [mixer_inputs(inp, l, i % 4, uT_full[i // 4]) for i in range(NCORE)]
        rM = _run(ncM, maps)
        yT = []
        for i in range(NCORE):
            b, q = divmod(i, 4)
            y = np.zeros((1024, 2112), np.float32)
            y[0:256] = rM[i]["ya"]
            cols = np.concatenate([np.arange(NCTX + 2048 * q, NCTX + 2048 * (q + 1)), np.arange(64 * q, 64 * (q + 1))])
            for j in range(4):
                r = rM[4 * b + j]
                y[256 + 64 * j:256 + 64 * (j + 1)] = r["yb"][:, cols]
                y[512 + 64 * j:512 + 64 * (j + 1)] = r["yc"][:, cols]
                y[768 + 64 * j:768 + 64 * (j + 1)] = r["yd"][:, cols]
            yT.append(y)
        nctx = 0 if last else 64
        ncQ = _prog(("Q", last), lambda: build_stage("Q", 2048, nctx, final=last))
        NTq = 2048 + nctx
        maps = []
        for i in range(NCORE):
            m = {"xT": np.ascontiguousarray(x1T[i][:, :NTq]), "yT": np.ascontiguousarray(yT[i][:, :NTq]), "modi": mods[i],
                 "w_out": np.ascontiguousarray(inp["w_out"][l], np.float32),
                 "w13": np.ascontiguousarray(inp["ffn2_w13"][l], np.float32), "w2": np.ascontiguousarray(inp["ffn2_w2"][l], np.float32)}
            if last:
                m["fin_g"] = _fm(inp["final_norm"], KC)
            maps.append(m)
        rQ = _run(ncQ, maps)
        if last:
            out = np.zeros((2, 8192, 1024), np.float32)
            for i in range(NCORE):
                b, q = divmod(i, 4)
                out[b, 2048 * q:2048 * (q + 1)] = rQ[i]["xfin"].T
        else:
            xT = [r["xnew"] for r in rQ]
    return out


def kernel(**inputs):
    return kernel_fused(**inputs)
```
